# Optimizing a Trainium2 kernel written in Bass

```python
import math
import jax, jax.numpy as jnp
from jax import lax
import numpy as np

D_MODEL = 2048
BATCH = 32
SEQ = 256
DEPTH = 4
DEC_BATCH = 8
DEC_SEQ = 2048
PAST_LEN = 512

GRID_W = 64
HEAD_DIM = 64
RWKV_W = 3 * D_MODEL // 8
RWKV_HEADS = RWKV_W // HEAD_DIM
HYENA_W = D_MODEL // 4
ATTN_W = 3 * D_MODEL // 8
ATTN_HEADS = ATTN_W // HEAD_DIM
ATTN_KV_HEADS = ATTN_HEADS // 3
GQA_GROUP = ATTN_HEADS // ATTN_KV_HEADS
KV_W = ATTN_KV_HEADS * HEAD_DIM
DECAY_LORA = 64
A_LORA = 64
GN_EPS = 64e-5
HYENA_ORDER = 2
HYENA_SHORT = 3
FILTER_EMB = 33
FILTER_HID = 64
HY_MIN_DECAY = math.log(1e-2) / 1.5
HY_MAX_DECAY = math.log(1e-2) / 0.3
WINDOW = 128
BLOCK = 128
ROPE_BASE = 10000.0
D_FF = 5504
FFN_CONV = 3
LN_EPS = 1e-5
DN_ALPHA = (2 * DEPTH) ** 0.25
DN_BETA = (8 * DEPTH) ** -0.25
IN_WIDTHS = (RWKV_W, RWKV_W, RWKV_W, RWKV_W, (HYENA_ORDER + 1) * HYENA_W, ATTN_W, KV_W, KV_W)
IN_W = sum(IN_WIDTHS)
IN_SPLITS = tuple(int(s) for s in np.cumsum(IN_WIDTHS)[:-1])

kernel_name = 'hybrid_rwkv_hyena_swa_diffusion_step'

F32 = jnp.float32


def layer_norm(x, g, b):
    xf = x.astype(F32)
    mu = jnp.mean(xf, -1, keepdims=True)
    var = jnp.mean(jnp.square(xf - mu), -1, keepdims=True)
    return ((xf - mu) * lax.rsqrt(var + LN_EPS)).astype(x.dtype) * g + b


def dwconv_centred(u, w):
    K = w.shape[0]
    pad = K // 2
    T = u.shape[1]
    up = jnp.pad(u, ((0, 0), (pad, pad), (0, 0)))
    return sum(up[:, j:j + T] * w[j] for j in range(K))


def rwkv_scan(r, decay, k, v, kk, a, init, reverse):
    def step(S, inp):
        r_t, w_t, k_t, v_t, kk_t, a_t = inp
        sa = jnp.einsum('bhvk,bhk->bhv', S, -kk_t)
        S = (S * w_t[:, :, None, :] + sa[..., None] * (kk_t * a_t)[:, :, None, :]
             + v_t[..., None] * k_t[:, :, None, :])
        return S, jnp.einsum('bhvk,bhk->bhv', S, r_t)
    xs = tuple(jnp.swapaxes(t, 0, 1) for t in (r, decay, k, v, kk, a))
    S, o = lax.scan(step, init.astype(F32), xs, reverse=reverse)
    return jnp.swapaxes(o, 0, 1), S


def rwkv_mixer(h, r, k, v, g, init, w0, w1, w2, a0, a1, a2, k_k, k_a, r_k, lnx_g, lnx_b):
    B, T, _ = h.shape
    heads = lambda t: t.reshape(B, T, RWKV_HEADS, HEAD_DIM).astype(F32)
    kk = heads(k * k_k)
    kk = kk / jnp.maximum(jnp.sqrt(jnp.sum(kk * kk, -1, keepdims=True)), 1e-12)
    rh, vh, kh = heads(r), heads(v), heads(k)
    k_a_h = k_a.reshape(RWKV_HEADS, HEAD_DIM).astype(F32)
    out = 0.0
    bonus = 0.0
    finals = []
    for d in range(2):
        w = -jax.nn.softplus(-(w0[d] + jnp.tanh(h @ w1[d]) @ w2[d])) - 0.5
        decay = jnp.exp(-jnp.exp(heads(w)))
        a = heads(jax.nn.sigmoid(a0[d] + (h @ a1[d]) @ a2[d]))
        kd = kh * (1.0 + (a - 1.0) * k_a_h)
        o, S = rwkv_scan(rh, decay, kd, vh, kk, a, init[d], reverse=(d == 1))
        out = out + o
        bonus = bonus + jnp.sum(rh * kd * r_k.astype(F32), -1, keepdims=True) * vh
        finals.append(S)
    mu = jnp.mean(out, -1, keepdims=True)
    var = jnp.mean(jnp.square(out - mu), -1, keepdims=True)
    on = ((out - mu) * lax.rsqrt(var + GN_EPS)).reshape(B, T, RWKV_W) * lnx_g + lnx_b
    y = (on + bonus.reshape(B, T, RWKV_W)) * jax.nn.sigmoid(g.astype(F32))
    return y.astype(h.dtype), jnp.stack(finals, 0)


def hyena_filters(L, f_w1, f_b1, f_freq1, f_w2, f_b2, f_freq2, f_w3):
    t = jnp.linspace(0.0, 1.0, L, dtype=F32)[:, None]
    bands = (FILTER_EMB - 1) // 2
    t_res = jnp.arange(L, dtype=F32)[:, None]
    f = jnp.linspace(1e-4, bands - 1, bands, dtype=F32)[None, :]
    ang = 2.0 * math.pi * t_res * f / L
    z = jnp.concatenate([t, jnp.cos(ang), jnp.sin(ang)], -1)
    hm = jnp.sin(f_freq1 * (z @ f_w1 + f_b1))
    hm = jnp.sin(f_freq2 * (hm @ f_w2 + f_b2))
    hf = (hm @ f_w3).astype(F32).reshape(L, HYENA_ORDER, 2, HYENA_W)
    deltas = jnp.abs(jnp.linspace(HY_MIN_DECAY, HY_MAX_DECAY, HYENA_W, dtype=F32))
    hf = hf * jnp.exp(-t * deltas)[:, None, None, :]
    causal = hf[:, :, 0]
    anti = hf[1:, :, 1][::-1]
    zero = jnp.zeros((1, HYENA_ORDER, HYENA_W), F32)
    return jnp.concatenate([causal, zero, anti], 0)


def fft_conv(u, kern, bias):
    L = u.shape[1]
    uf = jnp.fft.rfft(u.astype(F32), n=2 * L, axis=1)
    kf = jnp.fft.rfft(kern, n=2 * L, axis=0)
    y = jnp.fft.irfft(uf * kf[None], n=2 * L, axis=1)[:, :L]
    return (y + u.astype(F32) * bias).astype(u.dtype)


def hyena_mixer(u, short_w, f_w1, f_b1, f_freq1, f_w2, f_b2, f_freq2, f_w3, hy_bias):
    L = u.shape[1]
    u = dwconv_centred(u, short_w)
    streams = jnp.split(u, HYENA_ORDER + 1, axis=-1)
    kern = hyena_filters(L, f_w1, f_b1, f_freq1, f_w2, f_b2, f_freq2, f_w3)
    z = streams[0]
    for n in range(HYENA_ORDER):
        z = streams[n + 1] * fft_conv(z, kern[:, n], hy_bias[n])
    return z


def axial_rope(T):
    rows = T // GRID_W
    row = jnp.repeat(jnp.arange(rows), GRID_W).astype(F32)
    col = jnp.tile(jnp.arange(GRID_W), rows).astype(F32)
    nf = HEAD_DIM // 4
    inv = ROPE_BASE ** (-jnp.arange(nf, dtype=F32) / nf)
    ang = jnp.concatenate([row[:, None] * inv, col[:, None] * inv], -1)
    return jnp.cos(ang), jnp.sin(ang)


def apply_rope(x, cos, sin):
    half = HEAD_DIM // 2
    x1 = x[..., :half].astype(F32)
    x2 = x[..., half:].astype(F32)
    c = cos[None, :, None, :]
    s = sin[None, :, None, :]
    return jnp.concatenate([x1 * c - x2 * s, x1 * s + x2 * c], -1).astype(x.dtype)


def sink_softmax_av(s, v, sink):
    snk = sink.astype(F32).reshape(1, ATTN_KV_HEADS, GQA_GROUP, 1, 1)
    m = jnp.maximum(jnp.max(s, -1, keepdims=True), snk)
    p = jnp.exp(s - m)
    p = p / (jnp.sum(p, -1, keepdims=True) + jnp.exp(snk - m))
    return jnp.einsum('bhgqk,bkhd->bqhgd', p.astype(v.dtype), v)


def context_attention(q, k, v, sink):
    B, S = q.shape[:2]
    nb = S // BLOCK
    qb = jnp.swapaxes(q.reshape(B, nb, BLOCK, ATTN_KV_HEADS, GQA_GROUP, HEAD_DIM), 0, 1)
    scale = HEAD_DIM ** -0.5
    def block(qblk):
        s = jnp.einsum('bqhgd,bkhd->bhgqk', qblk, k).astype(F32) * scale
        return sink_softmax_av(s, v, sink)
    o = lax.map(block, qb)
    return jnp.swapaxes(o, 0, 1).reshape(B, S, ATTN_W)


def latent_attention(q, k, v, ctx_k, ctx_v, sink):
    B, T = q.shape[:2]
    nb = T // BLOCK
    qg = q.reshape(B, T, ATTN_KV_HEADS, GQA_GROUP, HEAD_DIM)
    kp = jnp.pad(k, ((0, 0), (BLOCK, BLOCK), (0, 0), (0, 0)))
    vp = jnp.pad(v, ((0, 0), (BLOCK, BLOCK), (0, 0), (0, 0)))
    ctx_k = ctx_k.astype(q.dtype)
    ctx_v = ctx_v.astype(v.dtype)
    scale = HEAD_DIM ** -0.5
    def block(i):
        start = i * BLOCK
        qb = lax.dynamic_slice_in_dim(qg, start, BLOCK, axis=1)
        kb = lax.dynamic_slice_in_dim(kp, start, 3 * BLOCK, axis=1)
        vb = lax.dynamic_slice_in_dim(vp, start, 3 * BLOCK, axis=1)
        qpos = start + jnp.arange(BLOCK)
        kpos = start - BLOCK + jnp.arange(3 * BLOCK)
        ok = ((jnp.abs(qpos[:, None] - kpos[None, :]) <= WINDOW)
              & (kpos >= 0)[None, :] & (kpos < T)[None, :])
        s_loc = jnp.where(ok, jnp.einsum('bqhgd,bkhd->bhgqk', qb, kb).astype(F32) * scale, -jnp.inf)
        s_ctx = jnp.einsum('bqhgd,bkhd->bhgqk', qb, ctx_k).astype(F32) * scale
        return sink_softmax_av(jnp.concatenate([s_loc, s_ctx], -1),
                               jnp.concatenate([vb, ctx_v], 1), sink)
    o = lax.map(block, jnp.arange(nb))
    return jnp.swapaxes(o, 0, 1).reshape(B, T, ATTN_W)


def conv_ffn(h, w_up, conv_w, w_down):
    u = dwconv_centred(h @ w_up, conv_w)
    a, b = jnp.split(u, 2, axis=-1)
    return (jax.nn.silu(a) * b) @ w_down


def trunk_layer(x, cond, p, rwkv_init, ctx_k=None, ctx_v=None):
    B, T, _ = x.shape
    mod = (jax.nn.silu(cond) @ p['w_mod'] + p['b_mod'])[:, None, :]
    shift1, scale1, gate1, shift2, scale2, gate2 = jnp.split(mod, 6, axis=-1)
    h = x * (1.0 + scale1) + shift1
    proj = h @ p['w_in']
    r, k, v, g, hy_u, q, ak, av = jnp.split(proj, IN_SPLITS, axis=-1)
    o_a, S = rwkv_mixer(h, r, k, v, g, rwkv_init, p['rwkv_w0'], p['rwkv_w1'], p['rwkv_w2'],
                        p['rwkv_a0'], p['rwkv_a1'], p['rwkv_a2'], p['rwkv_k_k'], p['rwkv_k_a'],
                        p['rwkv_r_k'], p['rwkv_lnx_g'], p['rwkv_lnx_b'])
    o_b = hyena_mixer(hy_u, p['hy_short_w'], p['hy_f_w1'], p['hy_f_b1'], p['hy_f_freq1'],
                      p['hy_f_w2'], p['hy_f_b2'], p['hy_f_freq2'], p['hy_f_w3'], p['hy_bias'])
    q = q.reshape(B, T, ATTN_HEADS, HEAD_DIM)
    ak = ak.reshape(B, T, ATTN_KV_HEADS, HEAD_DIM)
    av = av.reshape(B, T, ATTN_KV_HEADS, HEAD_DIM)
    if ctx_k is None:
        o_c = context_attention(q, ak, av, p['attn_sink'])
        ctx_out = (S, ak, av)
    else:
        cos, sin = axial_rope(T)
        o_c = latent_attention(apply_rope(q, cos, sin), apply_rope(ak, cos, sin), av,
                               ctx_k, ctx_v, p['attn_sink'])
        ctx_out = None
    mix = jnp.concatenate([o_a.astype(x.dtype), o_b.astype(x.dtype), o_c.astype(x.dtype)], -1) @ p['w_out']
    x = layer_norm(DN_ALPHA * x + gate1 * mix, p['ln1_g'], p['ln1_b'])
    h2 = x * (1.0 + scale2) + shift2
    f = conv_ffn(h2, p['ffn_w_up'], p['ffn_conv_w'], p['ffn_w_down'])
    x = layer_norm(DN_ALPHA * x + gate2 * f, p['ln2_g'], p['ln2_b'])
    return x, ctx_out


def setup_inputs(seed: int = 0) -> dict:
    key = jax.random.key(seed)
    ks = iter(jax.random.split(key, 64))
    def nrm(shape, s):
        return s * jax.random.normal(next(ks), shape, F32)
    L = DEPTH
    D = D_MODEL
    return {
        'x_prompt': nrm((BATCH, SEQ, D), 1.0),
        'x_sample': nrm((DEC_BATCH, DEC_SEQ, D), 1.0),
        'c': nrm((DEC_BATCH, D), 1.0),
        'state_rwkv': nrm((DEC_BATCH, DEPTH, 2, RWKV_HEADS, HEAD_DIM, HEAD_DIM), 0.5),
        'cache_k': nrm((DEC_BATCH, DEPTH, PAST_LEN, ATTN_KV_HEADS, HEAD_DIM), 1.0),
        'cache_v': nrm((DEC_BATCH, DEPTH, PAST_LEN, ATTN_KV_HEADS, HEAD_DIM), 1.0),
        'c_ctx': nrm((D,), 1.0),
        'w_mod': nrm((L, D, 6 * D), 0.5 * D ** -0.5),
        'b_mod': nrm((L, 6 * D), 0.02),
        'w_in': nrm((L, D, IN_W), D ** -0.5),
        'rwkv_w0': jax.random.uniform(next(ks), (L, 2, RWKV_W), F32, -6.0, -1.0),
        'rwkv_w1': nrm((L, 2, D, DECAY_LORA), D ** -0.5),
        'rwkv_w2': nrm((L, 2, DECAY_LORA, RWKV_W), 0.1 * DECAY_LORA ** -0.5),
        'rwkv_a0': nrm((L, 2, RWKV_W), 0.1),
        'rwkv_a1': nrm((L, 2, D, A_LORA), D ** -0.5),
        'rwkv_a2': nrm((L, 2, A_LORA, RWKV_W), 0.1 * A_LORA ** -0.5),
        'rwkv_k_k': 0.85 + nrm((L, RWKV_W), 0.02),
        'rwkv_k_a': 1.0 + nrm((L, RWKV_W), 0.02),
        'rwkv_r_k': nrm((L, RWKV_HEADS, HEAD_DIM), 0.1),
        'rwkv_lnx_g': 1.0 + nrm((L, RWKV_W), 0.02),
        'rwkv_lnx_b': nrm((L, RWKV_W), 0.02),
        'hy_short_w': nrm((L, HYENA_SHORT, (HYENA_ORDER + 1) * HYENA_W), HYENA_SHORT ** -0.5),
        'hy_f_w1': nrm((L, FILTER_EMB, FILTER_HID), FILTER_EMB ** -0.5),
        'hy_f_b1': nrm((L, FILTER_HID), 0.1),
        'hy_f_freq1': 1.0 + nrm((L, FILTER_HID), 0.02),
        'hy_f_w2': nrm((L, FILTER_HID, FILTER_HID), FILTER_HID ** -0.5),
        'hy_f_b2': nrm((L, FILTER_HID), 0.1),
        'hy_f_freq2': 1.0 + nrm((L, FILTER_HID), 0.02),
        'hy_f_w3': nrm((L, FILTER_HID, HYENA_ORDER * 2 * HYENA_W), 0.05 * FILTER_HID ** -0.5),
        'hy_bias': nrm((L, HYENA_ORDER, HYENA_W), 0.5),
        'attn_sink': nrm((L, ATTN_HEADS), 0.5),
        'w_out': nrm((L, D, D), DN_BETA * D ** -0.5),
        'ln1_g': 1.0 + nrm((L, D), 0.02),
        'ln1_b': nrm((L, D), 0.02),
        'ffn_w_up': nrm((L, D, 2 * D_FF), D ** -0.5),
        'ffn_conv_w': nrm((L, FFN_CONV, 2 * D_FF), FFN_CONV ** -0.5),
        'ffn_w_down': nrm((L, D_FF, D), DN_BETA * D_FF ** -0.5),
        'ln2_g': 1.0 + nrm((L, D), 0.02),
        'ln2_b': nrm((L, D), 0.02),
    }


def reference(x_prompt, x_sample, c, state_rwkv, cache_k, cache_v, c_ctx,
              w_mod, b_mod, w_in,
              rwkv_w0, rwkv_w1, rwkv_w2, rwkv_a0, rwkv_a1, rwkv_a2,
              rwkv_k_k, rwkv_k_a, rwkv_r_k, rwkv_lnx_g, rwkv_lnx_b,
              hy_short_w, hy_f_w1, hy_f_b1, hy_f_freq1, hy_f_w2, hy_f_b2, hy_f_freq2, hy_f_w3, hy_bias,
              attn_sink, w_out, ln1_g, ln1_b, ffn_w_up, ffn_conv_w, ffn_w_down, ln2_g, ln2_b):
    params = dict(w_mod=w_mod, b_mod=b_mod, w_in=w_in,
                  rwkv_w0=rwkv_w0, rwkv_w1=rwkv_w1, rwkv_w2=rwkv_w2,
                  rwkv_a0=rwkv_a0, rwkv_a1=rwkv_a1, rwkv_a2=rwkv_a2,
                  rwkv_k_k=rwkv_k_k, rwkv_k_a=rwkv_k_a, rwkv_r_k=rwkv_r_k,
                  rwkv_lnx_g=rwkv_lnx_g, rwkv_lnx_b=rwkv_lnx_b,
                  hy_short_w=hy_short_w, hy_f_w1=hy_f_w1, hy_f_b1=hy_f_b1, hy_f_freq1=hy_f_freq1,
                  hy_f_w2=hy_f_w2, hy_f_b2=hy_f_b2, hy_f_freq2=hy_f_freq2, hy_f_w3=hy_f_w3,
                  hy_bias=hy_bias, attn_sink=attn_sink, w_out=w_out, ln1_g=ln1_g, ln1_b=ln1_b,
                  ffn_w_up=ffn_w_up, ffn_conv_w=ffn_conv_w, ffn_w_down=ffn_w_down,
                  ln2_g=ln2_g, ln2_b=ln2_b)

    xc = x_prompt
    Bp = x_prompt.shape[0]
    zero_state = jnp.zeros((2, Bp, RWKV_HEADS, HEAD_DIM, HEAD_DIM), F32)
    states, keys, vals = [], [], []
    for l in range(DEPTH):
        p = {n: a[l] for n, a in params.items()}
        xc, (S, kc, vc) = trunk_layer(xc, c_ctx[None, :], p, zero_state)
        states.append(jnp.moveaxis(S, 0, 1))
        keys.append(kc)
        vals.append(vc)
    y_prompt = xc
    new_state_rwkv = jnp.stack(states, 1)
    new_cache_k = jnp.stack(keys, 1)
    new_cache_v = jnp.stack(vals, 1)

    xs = x_sample
    for l in range(DEPTH):
        p = {n: a[l] for n, a in params.items()}
        init = jnp.moveaxis(state_rwkv[:, l], 1, 0)
        xs, _ = trunk_layer(xs, c, p, init, cache_k[:, l], cache_v[:, l])
    y_sample = xs

    return (y_prompt, y_sample, new_state_rwkv, new_cache_k, new_cache_v)
```

```python
import math
import numpy as np
from contextlib import ExitStack
import concourse.bass as bass
import concourse.mybir as mybir
from concourse.bass_utils import run_bass_kernel_spmd

F32 = mybir.dt.float32
BF16 = mybir.dt.bfloat16
AF = mybir.ActivationFunctionType
ALU = mybir.AluOpType
AX = mybir.AxisListType

D = 2048
KT = 16
NL = 4
TP = 1024
TS = 2048
T = 3072
SEGS = [(0, 256), (256, 256), (512, 256), (768, 256), (1024, 2048)]
RW = 768
HW = 512
AW = 768
KVW = 256
INW = 5888
DFF = 5504
FT = 43
ALPHA = (2 * NL) ** 0.25
LN_EPS = 1e-5
GN_EPS = 64e-5
C_R = 0
C_K = 768
C_V = 1536
C_G = 2304
C_HY = 3072
C_Q = 4608
C_AK = 5376
C_AV = 5632
PI = math.pi


class Buf:
    __slots__ = ("name", "w", "r")

    def __init__(self, name=""):
        self.name = name
        self.w = None
        self.r = {}


class Prog:
    NDMA = 6

    def __init__(self, nc, es):
        self.nc = nc
        self.eng = {"pe": nc.tensor, "act": nc.scalar, "dve": nc.vector,
                    "pool": nc.gpsimd, "sp": nc.sync}
        self.sems = {}
        self.semval = {}
        for e in self.eng:
            self.sems[e] = es.enter_context(nc.semaphore("s_" + e))
            self.semval[e] = 0
        self.dq = {}
        for q in ("sp", "act", "pool"):
            ring = []
            for i in range(self.NDMA):
                k = "d_%s_%d" % (q, i)
                self.sems[k] = es.enter_context(nc.semaphore(k))
                self.semval[k] = 0
                ring.append(k)
            self.dq[q] = [ring, 0]
        self.waited = {e: {} for e in self.eng}
        self.nins = 0
        self.uid = 0

    def _wait(self, e, tok):
        if tok is None:
            return
        k, v = tok
        if self.waited[e].get(k, 0) >= v:
            return
        if e == "pe" and k == "pe":
            return
        self.eng[e].wait_ge(self.sems[k], v)
        self.waited[e][k] = v
        self.nins += 1

    def _deps(self, e, rd, wr):
        for b in rd:
            self._wait(e, b.w)
        for b in wr:
            self._wait(e, b.w)
            for k, v in b.r.items():
                self._wait(e, (k, v))

    def _mark(self, tok, rd, wr):
        for b in wr:
            b.w = tok
            b.r = {}
        for b in rd:
            if b.r.get(tok[0], 0) < tok[1]:
                b.r[tok[0]] = tok[1]

    def ins(self, e, fn, rd=(), wr=()):
        self._deps(e, rd, wr)
        i = fn(self.eng[e])
        self.semval[e] += 1
        i.then_inc(self.sems[e], 1)
        tok = (e, self.semval[e])
        self._mark(tok, rd, wr)
        self.nins += 1
        return tok

    def group(self, e, fns, rd=(), wr=()):
        self._deps(e, rd, wr)
        for fn in fns[:-1]:
            fn(self.eng[e])
        i = fns[-1](self.eng[e])
        self.semval[e] += 1
        i.then_inc(self.sems[e], 1)
        tok = (e, self.semval[e])
        self._mark(tok, rd, wr)
        self.nins += len(fns)
        return tok

    def dma(self, q, out, in_, rd=(), wr=(), **kw):
        ring, idx = self.dq[q]
        k = ring[idx % self.NDMA]
        self.dq[q][1] = idx + 1
        if self.semval[k] > 0:
            self._wait(q, (k, self.semval[k]))
        self._deps(q, rd, wr)
        i = self.eng[q].dma_start(out=out, in_=in_, **kw)
        self.semval[k] += 16
        i.then_inc(self.sems[k], 16)
        tok = (k, self.semval[k])
        self._mark(tok, rd, wr)
        self.nins += 1
        return tok

    def barrier(self):
        for e in self.eng:
            for k, v in self.semval.items():
                if v > 0:
                    self._wait(e, (k, v))


class Ctx:
    def __init__(self, nc, p, es):
        self.nc = nc
        self.p = p
        self.es = es
        self.n = 0

    def sb(self, es, shape, dt, name="t"):
        self.n += 1
        t = es.enter_context(self.nc.sbuf_tensor("%s_%d" % (name, self.n), list(shape), dt))
        return t, Buf(name)

    def dram(self, name, shape, dt, kind="Internal"):
        return self.nc.dram_tensor(name, list(shape), dt, kind=kind).ap()


class Ring:
    def __init__(self, items):
        self.items = items
        self.i = 0

    def next(self):
        it = self.items[self.i % len(self.items)]
        self.i += 1
        return it


def mm(p, out_ap, pairs, rd, wr):
    n = len(pairs)
    fns = []
    for i, (a, b) in enumerate(pairs):
        fns.append(lambda e, a=a, b=b, i=i: e.matmul(out_ap, a, b, start=(i == 0), stop=(i == n - 1)))
    return p.group("pe", fns, rd=rd, wr=wr)


class Glob:
    pass


def setup_globals(cx, es, cst):
    nc, p = cx.nc, cx.p
    G = Glob()
    banks = []
    for i in range(8):
        t = es.enter_context(nc.psum_tensor("psb%d" % i, [128, 512], F32))
        banks.append((t, Buf("ps%d" % i)))
    G.banks = banks
    G.ps = Ring(banks[0:6])
    G.psx = Ring(banks[6:8])
    G.ident, G.identB = cx.sb(es, [128, 128], F32, "ident")
    G.identb, G.identbB = cx.sb(es, [128, 128], BF16, "identb")
    p.dma("sp", G.ident[:], cst["ident"], wr=[G.identB])
    p.dma("pool", G.identb[:], cst["ident"], wr=[G.identbB])
    G.ones, G.onesB = cx.sb(es, [128, 128], F32, "ones")
    p.ins("dve", lambda e: e.memset(G.ones[:], 1.0), wr=[G.onesB])
    return G


def vecT(cx, G, es, vec2d, n, dst_ap, dstB, eng="dve"):
    p = cx.p
    st, stB = G.vstage.next()
    p.dma("sp", st[0:n, :], vec2d, wr=[stB])
    ps, pb = G.ps.next()
    mm(p, ps[:, 0:n], [(st[0:n, :], G.ident[0:n, 0:n])], rd=[stB, G.identB], wr=[pb])
    if eng == "dve":
        p.ins("dve", lambda e: e.tensor_copy(dst_ap, ps[:, 0:n]), rd=[pb], wr=[dstB])
    else:
        p.ins("act", lambda e: e.activation(dst_ap, ps[:, 0:n], AF.Copy), rd=[pb], wr=[dstB])


def rows128(vec1d):
    return vec1d.rearrange("(t p) -> t p", p=128)


def phase_x0(cx, G, x_p, x_s, xT):
    p = cx.p
    with ExitStack() as es:
        xin = Ring([cx.sb(es, [128, D], F32, "xin") for _ in range(2)])
        stg = Ring([cx.sb(es, [128, 4, 128], F32, "x0s") for _ in range(3)])
        for tt in range(T // 128):
            xt, xb = xin.next()
            src = x_p[tt * 128:(tt + 1) * 128, :] if tt < 8 else x_s[(tt - 8) * 128:(tt - 7) * 128, :]
            p.dma("sp", xt[:], src, wr=[xb])
            for f4 in range(4):
                ps, pb = G.ps.next()
                for j in range(4):
                    ft = f4 * 4 + j
                    mm(p, ps[:, j * 128:(j + 1) * 128], [(xt[:, ft * 128:(ft + 1) * 128], G.ident[:])],
                       rd=[xb, G.identB], wr=[pb])
                st, sb_ = stg.next()
                if f4 % 2 == 0:
                    p.ins("dve", lambda e: e.tensor_copy(st[:].rearrange("p a b -> p (a b)"), ps[:]), rd=[pb], wr=[sb_])
                else:
                    p.ins("act", lambda e: e.activation(st[:].rearrange("p a b -> p (a b)"), ps[:], AF.Copy), rd=[pb], wr=[sb_])
                p.dma("sp", xT[f4 * 4:(f4 + 1) * 4, :, tt * 128:(tt + 1) * 128].rearrange("f p t -> p f t"),
                      st[:], rd=[sb_])
    p.barrier()


def phase_cond(cx, G, es, c_ctx, c_s):
    p = cx.p
    G.sT, G.sTB = cx.sb(es, [128, KT, 2], BF16, "sT")
    tmp, tmpB = cx.sb(es, [128, KT], F32, "ctmp")
    for c, src in enumerate((c_ctx, c_s)):
        vecT(cx, G, es, src.rearrange("o (t p) -> (o t) p", p=128), KT, tmp[:], tmpB)
        p.ins("act", lambda e: e.activation(G.sT[:, :, c], tmp[:], AF.Silu), rd=[tmpB], wr=[G.sTB])


def phase_mod(cx, G, L, w_mod, b_mod):
    p = cx.p
    with ExitStack() as es:
        wb = Ring([cx.sb(es, [128, KT, 512], BF16, "wmod") for _ in range(2)])
        bT, bTB = cx.sb(es, [128, 96], F32, "bmodT")
        vecT(cx, G, es, rows128(b_mod[L]), 96, bT[:], bTB)
        psM, psMB = G.ps.next()
        for cb in range(24):
            wt, wtb = wb.next()
            p.dma("pool", wt[:], w_mod[L][:, cb * 512:(cb + 1) * 512].rearrange("(kt p) n -> p kt n", p=128), wr=[wtb])
            for oc in range(4):
                j = cb * 4 + oc
                mm(p, psM[:, j * 2:j * 2 + 2],
                   [(wt[:, kt, oc * 128:(oc + 1) * 128], G.sT[:, kt, :]) for kt in range(KT)],
                   rd=[wtb, G.sTB], wr=[psMB])
        psv = psM[:, 0:192].rearrange("p (j c) -> p j c", c=2)
        for c in range(2):
            p.ins("dve", lambda e: e.tensor_tensor(G.modT[:, :, c], psv[:, :, c], bT[:], ALU.add),
                  rd=[psMB, bTB], wr=[G.modTB])
        for j0 in (16, 64):
            p.ins("dve", lambda e: e.tensor_scalar(G.modT[:, j0:j0 + 16, :], G.modT[:, j0:j0 + 16, :], 1.0, None, ALU.add),
                  rd=[G.modTB], wr=[G.modTB])
    p.barrier()


def modulate(cx, G, src, srcB, dst, dstB, ft, j_scale, j_shift, t0=0, tn=T, base=0):
    p = cx.p
    for c, (a, b) in enumerate(((0, TP), (TP, T))):
        lo, hi = max(a, t0), min(b, t0 + tn)
        if lo >= hi:
            continue
        s = src[:, lo - t0:hi - t0]
        d = dst[:, lo - t0:hi - t0]
        sc = G.modT[:, j_scale + ft, c:c + 1]
        sh = G.modT[:, j_shift + ft, c:c + 1]
        if c == 0:
            p.ins("act", lambda e, s=s, d=d, sc=sc, sh=sh: e.activation(d, s, AF.Identity, bias=sh, scale=sc),
                  rd=[srcB, G.modTB], wr=[dstB])
        else:
            p.ins("dve", lambda e, s=s, d=d, sc=sc, sh=sh: e.tensor_scalar(d, s, sc, sh, ALU.mult, ALU.add),
                  rd=[srcB, G.modTB], wr=[dstB])


def phase_h(cx, G, xT, hT, hTB):
    p = cx.p
    with ExitStack() as es:
        xin = Ring([cx.sb(es, [128, T], F32, "hx") for _ in range(2)])
        for ft in range(KT):
            xt, xb = xin.next()
            p.dma("sp", xt[:], xT[ft], wr=[xb])
            modulate(cx, G, xt[:], xb, hT[:, ft, :], hTB, ft, 16, 0)
    p.barrier()


def linear_fm(cx, G, inT, inB, ktn, groups, evac, tblocks, wbufs):
    p = cx.p
    for pieces, tags in groups:
        wt, wb = wbufs.next()
        for ap, c0 in pieces:
            n = ap.shape[1]
            p.dma("pool", wt[:, 0:ktn, c0:c0 + n], ap.rearrange("(kt p) n -> p kt n", p=128), wr=[wb])
        for oc, tag in enumerate(tags):
            for (t0, tn) in tblocks:
                ps, pb = G.ps.next()
                mm(p, ps[:, 0:tn],
                   [(wt[:, kt, oc * 128:(oc + 1) * 128], inT[:, kt, t0:t0 + tn]) for kt in range(ktn)],
                   rd=[wb, inB], wr=[pb])
                evac(tag, t0, tn, ps, pb)


TB512 = [(i * 512, 512) for i in range(6)]


def phase_proj(cx, G, L, hT, hTB, W, projT, projTok, loraT, loraTB, new_k, new_v):
    p = cx.p
    w_in = W["w_in"][L]
    with ExitStack() as es:
        wbufs = Ring([cx.sb(es, [128, KT, 512], BF16, "win") for _ in range(2)])
        stg = Ring([cx.sb(es, [128, 512], F32, "pst") for _ in range(4)])
        cnt = [0]

        def evac_fm(tag, t0, tn, ps, pb):
            cnt[0] += 1
            if isinstance(tag, str):
                li = int(tag[1])
                fn = AF.Tanh if li == 0 else AF.Copy
                p.ins("act", lambda e: e.activation(loraT[:, li, t0:t0 + tn], ps[:, 0:tn], fn), rd=[pb], wr=[loraTB])
                return
            st, sb_ = stg.next()
            if cnt[0] % 2 == 0:
                p.ins("dve", lambda e: e.tensor_copy(st[:, 0:tn], ps[:, 0:tn]), rd=[pb], wr=[sb_])
            else:
                p.ins("act", lambda e: e.activation(st[:, 0:tn], ps[:, 0:tn], AF.Copy), rd=[pb], wr=[sb_])
            p.dma("sp", projT[tag, :, t0:t0 + tn], st[:, 0:tn], rd=[sb_])

        groups = []
        for c0 in list(range(0, 1536, 512)) + list(range(3072, 5632, 512)):
            groups.append(([(w_in[:, c0:c0 + 512], 0)], [c0 // 128 + i for i in range(4)]))
        groups.append(([(W["rwkv_w1"][L, 0], 0), (W["rwkv_w1"][L, 1], 64),
                        (W["rwkv_a1"][L, 0], 128), (W["rwkv_a1"][L, 1], 192)], ["l0", "l1"]))
        linear_fm(cx, G, hT, hTB, KT, groups, evac_fm, TB512, wbufs)

        for gi, c0 in enumerate((C_V, C_V + 512, C_V + 1024, C_AK)):
            wt, wb = wbufs.next()
            p.dma("pool", wt[:], w_in[:, c0:c0 + 512].rearrange("(kt p) n -> p kt n", p=128), wr=[wb])
            for tt in range(T // 128):
                ps, pb = G.ps.next()
                mm(p, ps[:], [(hT[:, kt, tt * 128:(tt + 1) * 128], wt[:, kt, :]) for kt in range(KT)],
                   rd=[wb, hTB], wr=[pb])
                st, sb_ = stg.next()
                if tt % 2 == 0:
                    p.ins("dve", lambda e: e.tensor_copy(st[:], ps[:]), rd=[pb], wr=[sb_])
                else:
                    p.ins("act", lambda e: e.activation(st[:], ps[:], AF.Copy), rd=[pb], wr=[sb_])
                p.dma("sp", projTok[tt * 128:(tt + 1) * 128, gi * 512:(gi + 1) * 512], st[:], rd=[sb_])
                if gi == 3 and tt < 8:
                    s, r0 = tt // 2, (tt % 2) * 128
                    p.dma("sp", new_k[s, L, r0:r0 + 128, :], st[:, 0:256], rd=[sb_])
                    p.dma("sp", new_v[s, L, r0:r0 + 128, :], st[:, 256:512], rd=[sb_])
    p.barrier()


def resid_evac(cx, G, stg, xT, yT, j_gate):
    p = cx.p

    def evac(ot, t0, tn, ps, pb):
        c = 0 if t0 < TP else 1
        (xt, xb), (tm, tb) = stg.next()
        p.dma("sp", xt[:, 0:tn], xT[ot, :, t0:t0 + tn], wr=[xb])
        p.ins("act", lambda e: e.activation(tm[:, 0:tn], ps[:, 0:tn], AF.Copy, scale=G.modT[:, j_gate + ot, c:c + 1]),
              rd=[pb, G.modTB], wr=[tb])
        p.ins("dve", lambda e: e.scalar_tensor_tensor(xt[:, 0:tn], xt[:, 0:tn], ALPHA, tm[:, 0:tn], ALU.mult, ALU.add),
              rd=[xb, tb], wr=[xb])
        p.dma("sp", yT[ot, :, t0:t0 + tn], xt[:, 0:tn], rd=[xb])
    return evac


def phase_out(cx, G, L, W, mixT, xT, yT):
    p = cx.p
    with ExitStack() as es:
        mT, mB = cx.sb(es, [128, KT, T], BF16, "mixres")
        for ft in range(KT):
            p.dma("sp", mT[:, ft, :], mixT[ft], wr=[mB])
        wbufs = Ring([cx.sb(es, [128, KT, 512], BF16, "wout") for _ in range(2)])
        stg = Ring([(cx.sb(es, [128, 512], F32, "ox"), cx.sb(es, [128, 512], F32, "ot")) for _ in range(3)])
        groups = [([(W["w_out"][L][:, c0:c0 + 512], 0)], [c0 // 128 + i for i in range(4)]) for c0 in range(0, D, 512)]
        linear_fm(cx, G, mT, mB, KT, groups, resid_evac(cx, G, stg, xT, yT, 32), TB512, wbufs)
    p.barrier()


def ln_pass(cx, G, L, yT, g_vec, b_vec, xT, h2T, h2TB, final=None):
    p = cx.p
    with ExitStack() as es:
        gT, gB = cx.sb(es, [128, KT], F32, "lng")
        bT, bB = cx.sb(es, [128, KT], F32, "lnb")
        vecT(cx, G, es, rows128(g_vec), KT, gT[:], gB)
        vecT(cx, G, es, rows128(b_vec), KT, bT[:], bB)
        ybuf = Ring([cx.sb(es, [128, KT, 512], F32, "lny") for _ in range(2)])
        sqr = Ring([cx.sb(es, [128, 512], F32, "lnsq") for _ in range(3)])
        mean, meanB = cx.sb(es, [128, 512], F32, "mean")
        rstd, rstdB = cx.sb(es, [128, 512], F32, "rstd")
        msq, msqB = cx.sb(es, [128, 512], F32, "msq")
        zr = Ring([cx.sb(es, [128, 512], F32, "lnz") for _ in range(3)])
        if final is not None:
            osb = Ring([cx.sb(es, [128, D], F32, "lno") for _ in range(2)])
        for (t0, tn) in TB512:
            yt, yb = ybuf.next()
            p.dma("sp", yt[:], yT[:, :, t0:t0 + tn].rearrange("f p t -> p f t"), wr=[yb])
            ps1, pb1 = G.ps.next()
            mm(p, ps1[:], [(G.ones[:], yt[:, ft, :]) for ft in range(KT)], rd=[yb, G.onesB], wr=[pb1])
            ps2, pb2 = G.ps.next()
            toks = []
            sq_list = []
            n = KT
            for ft in range(KT):
                sq, sqb = sqr.next()
                p.ins("act", lambda e, sq=sq, ft=ft: e.activation(sq[:], yt[:, ft, :], AF.Square), rd=[yb], wr=[sqb])
                p.ins("pe", lambda e, sq=sq, ft=ft: e.matmul(ps2[:], G.ones[:], sq[:], start=(ft == 0), stop=(ft == n - 1)),
                      rd=[sqb, G.onesB], wr=[pb2])
            p.ins("act", lambda e: e.activation(mean[:], ps1[:], AF.Copy, scale=1.0 / D), rd=[pb1], wr=[meanB])
            p.ins("dve", lambda e: e.tensor_tensor(msq[:], mean[:], mean[:], ALU.mult), rd=[meanB], wr=[msqB])
            p.ins("dve", lambda e: e.scalar_tensor_tensor(msq[:], ps2[:], 1.0 / D, msq[:], ALU.mult, ALU.subtract),
                  rd=[pb2, msqB], wr=[msqB])
            p.ins("dve", lambda e: e.tensor_scalar(msq[:], msq[:], LN_EPS, None, ALU.add), rd=[msqB], wr=[msqB])
            p.ins("act", lambda e: e.activation(msq[:], msq[:], AF.Sqrt), rd=[msqB], wr=[msqB])
            p.ins("dve", lambda e: e.reciprocal(rstd[:], msq[:]), rd=[msqB], wr=[rstdB])
            for ft in range(KT):
                z, zb = zr.next()
                p.ins("dve", lambda e, z=z, ft=ft: e.tensor_tensor(z[:], yt[:, ft, :], mean[:], ALU.subtract),
                      rd=[yb, meanB], wr=[zb])
                p.ins("dve", lambda e, z=z: e.tensor_tensor(z[:], z[:], rstd[:], ALU.mult), rd=[zb, rstdB], wr=[zb])
                p.ins("act", lambda e, z=z, ft=ft: e.activation(z[:], z[:], AF.Identity, bias=bT[:, ft:ft + 1], scale=gT[:, ft:ft + 1]),
                      rd=[zb, gB, bB], wr=[zb])
                if final is None:
                    p.dma("sp", xT[ft, :, t0:t0 + tn], z[:], rd=[zb])
                    if h2T is not None:
                        modulate(cx, G, z[:], zb, h2T[:, ft, t0:t0 + tn], h2TB, ft, 64, 48, t0=t0, tn=tn)
                else:
                    ps, pb = G.ps.next()
                    for j in range(4):
                        mm(p, ps[:, j * 128:(j + 1) * 128], [(z[:, j * 128:(j + 1) * 128], G.ident[:])], rd=[zb, G.identB], wr=[pb])
                    for j in range(4):
                        pass
                    final_store(cx, G, ps, pb, ft, t0, final, osb)
    p.barrier()


def final_store(cx, G, ps, pb, ft, t0, final, osb):
    p = cx.p
    st = final["stage"]
    p.ins("dve" if ft % 2 == 0 else "act",
          (lambda e: e.tensor_copy(st[0][:, :, ft, :], ps[:].rearrange("p (j f) -> p j f", j=4))) if ft % 2 == 0 else
          (lambda e: e.activation(st[0][:, :, ft, :], ps[:].rearrange("p (j f) -> p j f", j=4), AF.Copy)),
          rd=[pb], wr=[st[1]])
    if ft == KT - 1:
        for j in range(4):
            tt = t0 // 128 + j
            dst = final["y_p"][tt * 128:(tt + 1) * 128, :] if tt < 8 else final["y_s"][(tt - 8) * 128:(tt - 7) * 128, :]
            p.dma("sp", dst, st[0][:, j, :, :].rearrange("p a b -> p (a b)"), rd=[st[1]])


def phase_ffn(cx, G, L, W, h2T, h2TB, gT_d, xT, yT):
    p = cx.p
    w_up = W["ffn_w_up"][L]
    GL = T + 6

    def gcol(t):
        for s, (a, n) in enumerate(SEGS):
            if a <= t < a + n:
                return t + s + 1
        raise ValueError

    with ExitStack() as es:
        wbufs = Ring([cx.sb(es, [128, KT, 512], BF16, "wup") for _ in range(2)])
        cw, cwB = cx.sb(es, [128, 3, 86], F32, "convw")
        for k in range(3):
            vecT(cx, G, es, rows128(W["ffn_conv_w"][L, k]), 86, cw[:, k, :], cwB)
        ubuf = {}
        for nm in ("a", "b"):
            ubuf[nm] = Ring([cx.sb(es, [128, GL], F32, "u" + nm) for _ in range(2 if nm == "a" else 1)])
            for (t_, b_) in ubuf[nm].items:
                p.ins("pool", lambda e, t_=t_: e.memset(t_[:], 0.0), wr=[b_])
        cbuf = {nm: cx.sb(es, [128, GL], F32, "c" + nm) for nm in ("a", "b")}
        gout = Ring([cx.sb(es, [128, T], BF16, "gout") for _ in range(1)])
        cur = {}

        def evac(tag, t0, tn, ps, pb):
            nm, j = tag
            if t0 == 0:
                cur[nm] = ubuf[nm].next()
            u, ub = cur[nm]
            for s, (a, n) in enumerate(SEGS):
                lo, hi = max(a, t0), min(a + n, t0 + tn)
                if lo >= hi:
                    continue
                eng = "act" if nm == "a" else "dve"
                src = ps[:, lo - t0:hi - t0]
                dst = u[:, lo + s + 1:hi + s + 1]
                if eng == "act":
                    p.ins("act", lambda e, src=src, dst=dst: e.activation(dst, src, AF.Copy), rd=[pb], wr=[ub])
                else:
                    p.ins("dve", lambda e, src=src, dst=dst: e.tensor_copy(dst, src), rd=[pb], wr=[ub])
            if t0 + tn == T:
                fi = j if nm == "a" else FT + j
                c, cb = cbuf[nm]
                eng = "dve"
                p.ins(eng, lambda e: e.tensor_scalar(c[:, 1:GL - 1], u[:, 1:GL - 1], cw[:, 1, fi:fi + 1], None, ALU.mult),
                      rd=[ub, cwB], wr=[cb])
                p.ins(eng, lambda e: e.scalar_tensor_tensor(c[:, 1:GL - 1], u[:, 0:GL - 2], cw[:, 0, fi:fi + 1], c[:, 1:GL - 1], ALU.mult, ALU.add),
                      rd=[ub, cwB, cb], wr=[cb])
                p.ins(eng, lambda e: e.scalar_tensor_tensor(c[:, 1:GL - 1], u[:, 2:GL], cw[:, 2, fi:fi + 1], c[:, 1:GL - 1], ALU.mult, ALU.add),
                      rd=[ub, cwB, cb], wr=[cb])
                if nm == "b":
                    ca, cab = cbuf["a"]
                    g, gb = gout.next()
                    p.ins("act", lambda e: e.activation(ca[:, 1:GL - 1], ca[:, 1:GL - 1], AF.Silu), rd=[cab], wr=[cab])
                    for s, (a, n) in enumerate(SEGS):
                        p.ins("dve", lambda e, s=s, a=a, n=n: e.tensor_tensor(g[:, a:a + n], ca[:, a + s + 1:a + n + s + 1],
                                                                             c[:, a + s + 1:a + n + s + 1], ALU.mult),
                              rd=[cab, cb], wr=[gb])
                    p.dma("sp", gT_d[j], g[:], rd=[gb])

        groups = []
        for j0 in range(0, FT, 2):
            nj = min(2, FT - j0)
            pieces = [(w_up[:, j0 * 128:(j0 + nj) * 128], 0), (w_up[:, DFF + j0 * 128:DFF + (j0 + nj) * 128], 256)]
            tags = []
            for i in range(nj):
                tags.append((("a", j0 + i), i * 128))
                tags.append((("b", j0 + i), 256 + i * 128))
            groups.append((pieces, tags))
        for pieces, tags in groups:
            wt, wb = wbufs.next()
            for ap, c0 in pieces:
                n = ap.shape[1]
                p.dma("pool", wt[:, :, c0:c0 + n], ap.rearrange("(kt p) n -> p kt n", p=128), wr=[wb])
            for tag, co in tags:
                for (t0, tn) in TB512:
                    ps, pb = G.ps.next()
                    mm(p, ps[:, 0:tn], [(wt[:, kt, co:co + 128], h2T[:, kt, t0:t0 + tn]) for kt in range(KT)],
                       rd=[wb, h2TB], wr=[pb])
                    evac(tag, t0, tn, ps, pb)
    p.barrier()


def phase_down(cx, G, L, W, gT_d, xT, yT):
    p = cx.p
    w_dn = W["ffn_w_down"][L]
    TC = 1536
    with ExitStack() as es:
        gres, gB = cx.sb(es, [128, FT, TC], BF16, "gres")
        wbufs = Ring([cx.sb(es, [128, FT, 256], BF16, "wdn") for _ in range(3)])
        stg = Ring([(cx.sb(es, [128, 512], F32, "dx"), cx.sb(es, [128, 512], F32, "dt")) for _ in range(2)])
        ev = resid_evac(cx, G, stg, xT, yT, 80)
        for tc in range(T // TC):
            for ft in range(FT):
                p.dma("sp", gres[:, ft, :], gT_d[ft, :, tc * TC:(tc + 1) * TC], wr=[gB])
            for c0 in range(0, D, 256):
                wt, wb = wbufs.next()
                p.dma("pool", wt[:], w_dn[:, c0:c0 + 256].rearrange("(kt p) n -> p kt n", p=128), wr=[wb])
                for oc in range(2):
                    ot = c0 // 128 + oc
                    for tb in range(TC // 512):
                        ps, pb = G.ps.next()
                        mm(p, ps[:], [(wt[:, kt, oc * 128:(oc + 1) * 128], gres[:, kt, tb * 512:(tb + 1) * 512]) for kt in range(FT)],
                           rd=[wb, gB], wr=[pb])
                        ev(ot, tc * TC + tb * 512, 512, ps, pb)
    p.barrier()


W_NAMES = ['w_mod', 'b_mod', 'w_in', 'rwkv_w0', 'rwkv_w1', 'rwkv_w2', 'rwkv_a0', 'rwkv_a1', 'rwkv_a2',
           'rwkv_k_k', 'rwkv_k_a', 'rwkv_r_k', 'rwkv_lnx_g', 'rwkv_lnx_b', 'hy_short_w', 'hy_f_w1', 'hy_f_b1',
           'hy_f_freq1', 'hy_f_w2', 'hy_f_b2', 'hy_f_freq2', 'hy_f_w3', 'hy_bias', 'attn_sink', 'w_out',
           'ln1_g', 'ln1_b', 'ffn_w_up', 'ffn_conv_w', 'ffn_w_down', 'ln2_g', 'ln2_b']
W_SHAPES = {
    'w_mod': (NL, D, 6 * D), 'b_mod': (NL, 6 * D), 'w_in': (NL, D, INW), 'rwkv_w0': (NL, 2, RW),
    'rwkv_w1': (NL, 2, D, 64), 'rwkv_w2': (NL, 2, 64, RW), 'rwkv_a0': (NL, 2, RW), 'rwkv_a1': (NL, 2, D, 64),
    'rwkv_a2': (NL, 2, 64, RW), 'rwkv_k_k': (NL, RW), 'rwkv_k_a': (NL, RW), 'rwkv_r_k': (NL, RW),
    'rwkv_lnx_g': (NL, RW), 'rwkv_lnx_b': (NL, RW), 'hy_short_w': (NL, 3, 1536), 'hy_f_w1': (NL, 33, 64),
    'hy_f_b1': (NL, 64), 'hy_f_freq1': (NL, 64), 'hy_f_w2': (NL, 64, 64), 'hy_f_b2': (NL, 64), 'hy_f_freq2': (NL, 64),
    'hy_f_w3': (NL, 64, 2048), 'hy_bias': (NL, 2 * HW), 'attn_sink': (NL, 12), 'w_out': (NL, D, D),
    'ln1_g': (NL, D), 'ln1_b': (NL, D), 'ffn_w_up': (NL, D, 2 * DFF), 'ffn_conv_w': (NL, 3, 2 * DFF),
    'ffn_w_down': (NL, DFF, D), 'ln2_g': (NL, D), 'ln2_b': (NL, D)}


def make_consts():
    c = {}
    c["ident"] = np.eye(128, dtype=np.float32)
    attn_consts(c)
    hyena_consts(c)
    rwkv_consts(c)
    return c


def build(nlayers=NL, debug=None):
    nc = bass.Bass("TRN2", target_bir_lowering=False)
    ext = lambda n, s: nc.dram_tensor(n, list(s), F32, kind="ExternalInput").ap()
    x_p = ext("x_p", (TP, D))
    x_s = ext("x_s", (TS, D))
    c_s = ext("c_s", (1, D))
    c_ctx = ext("c_ctx", (1, D))
    st_in = ext("st_in", (NL, 2, 12, 64, 64))
    ck_in = ext("ck_in", (NL, 512, KVW))
    cv_in = ext("cv_in", (NL, 512, KVW))
    W = {n: ext(n, W_SHAPES[n]) for n in W_NAMES}
    consts = make_consts()
    cst = {n: ext("cst_" + n, v.shape) for n, v in consts.items()}
    out = lambda n, s: nc.dram_tensor(n, list(s), F32, kind="ExternalOutput").ap()
    y_p = out("y_p", (TP, D))
    y_s = out("y_s", (TS, D))
    new_st = out("new_st", (4, NL, 2, 12, 64, 64))
    new_k = out("new_k", (4, NL, 256, KVW))
    new_v = out("new_v", (4, NL, 256, KVW))
    dk = "ExternalOutput" if debug else "Internal"
    xT = nc.dram_tensor("xT", [KT, 128, T], F32, kind=dk).ap()
    yT = nc.dram_tensor("yT", [KT, 128, T], F32, kind=dk).ap()
    projT = nc.dram_tensor("projT", [46, 128, T], F32, kind=dk).ap()
    projTok = nc.dram_tensor("projTok", [T, 2048], F32, kind=dk).ap()
    mixT = nc.dram_tensor("mixT", [KT, 128, T], BF16, kind="Internal").ap()
    gT_d = nc.dram_tensor("gT_d", [FT, 128, T], BF16, kind="Internal").ap()
    hyT_d = nc.dram_tensor("hyT", [12, 128, T], F32, kind="Internal").ap()
    Kd_d = {Lq: nc.dram_tensor("Kd%d" % Lq, [2, 2, Lq // 128 + 1, 128, HW], F32, kind="Internal").ap() for Lq in (256, 2048)}
    if debug and debug.get("mix_override"):
        mix_ov = ext("mix_ov", (KT, 128, T))
    if debug and debug.get("stop") == "mix":
        mix_dbg = out("mix_dbg", (KT, 128, T))
    with ExitStack() as es:
        p = Prog(nc, es)
        cx = Ctx(nc, p, es)
        G = setup_globals(cx, es, cst)
        G.vstage = Ring([cx.sb(es, [128, 128], F32, "vst") for _ in range(2)])
        G.modT, G.modTB = cx.sb(es, [128, 96, 2], F32, "modT")
        G.hyT = hyT_d
        G.Kd = Kd_d
        phase_cond(cx, G, es, c_ctx, c_s)
        phase_x0(cx, G, x_p, x_s, xT)
        for L in range(nlayers):
            phase_mod(cx, G, L, W["w_mod"], W["b_mod"])
            with ExitStack() as les:
                loraT, loraTB = cx.sb(les, [128, 2, T], BF16, "loraT")
                with ExitStack() as hes:
                    hT, hTB = cx.sb(hes, [128, KT, T], BF16, "hT")
                    phase_h(cx, G, xT, hT, hTB)
                    phase_proj(cx, G, L, hT, hTB, W, projT, projTok, loraT, loraTB, new_k, new_v)
                if debug and debug.get("stop") == "proj":
                    break
                if debug and debug.get("mix_override"):
                    with ExitStack() as mes:
                        mt, mb = cx.sb(mes, [128, T], BF16, "mov")
                        for ft in range(KT):
                            p.dma("pool", mt[:], mix_ov[ft], wr=[mb])
                            p.dma("sp", mixT[ft], mt[:], rd=[mb])
                    p.barrier()
                else:
                    phase_mixers(cx, G, L, W, projT, projTok, loraT, loraTB, mixT, st_in, ck_in, cv_in, new_st, cst,
                                 which=(debug or {}).get("which", ("rwkv", "hyena", "attn")))
                    if debug and debug.get("stop") == "mix":
                        with ExitStack() as mes:
                            mt, mb = cx.sb(mes, [128, T], BF16, "mdb")
                            for ft in range(KT):
                                p.dma("sp", mt[:], mixT[ft], wr=[mb])
                                p.dma("pool", mix_dbg[ft], mt[:], rd=[mb])
                        break
            phase_out(cx, G, L, W, mixT, xT, yT)
            with ExitStack() as fes:
                h2T, h2TB = cx.sb(fes, [128, KT, T], BF16, "h2T")
                ln_pass(cx, G, L, yT, W["ln1_g"][L], W["ln1_b"][L], xT, h2T, h2TB)
                if debug and debug.get("stop") == "ln1":
                    break
                phase_ffn(cx, G, L, W, h2T, h2TB, gT_d, xT, yT)
            phase_down(cx, G, L, W, gT_d, xT, yT)
            if L == nlayers - 1:
                with ExitStack() as oes:
                    st = cx.sb(oes, [128, 4, KT, 128], F32, "fstage")
                    ln_pass(cx, G, L, yT, W["ln2_g"][L], W["ln2_b"][L], xT, None, None,
                            final={"stage": st, "y_p": y_p, "y_s": y_s})
            else:
                ln_pass(cx, G, L, yT, W["ln2_g"][L], W["ln2_b"][L], xT, None, None)
        p.barrier()
        print("instructions:", p.nins)
    return nc, consts


def phase_mixers(cx, G, L, W, projT, projTok, loraT, loraTB, mixT, st_in, ck_in, cv_in, new_st, cst, which=("rwkv", "hyena", "attn")):
    if "attn" in which:
        phase_attn(cx, G, L, W, projT, projTok, mixT, ck_in, cv_in, cst)
    if "hyena" in which:
        phase_hyena(cx, G, L, W, projT, mixT, cst)
    if "rwkv" in which:
        phase_rwkv(cx, G, L, W, projT, projTok, loraT, loraTB, mixT, st_in, new_st, cst)


def make_in_maps(inputs, consts):
    maps = []
    for c in range(8):
        m = {}
        m["x_p"] = np.ascontiguousarray(inputs["x_prompt"][4 * c:4 * c + 4]).reshape(TP, D)
        m["x_s"] = np.ascontiguousarray(inputs["x_sample"][c])
        m["c_s"] = np.ascontiguousarray(inputs["c"][c:c + 1])
        m["c_ctx"] = np.ascontiguousarray(inputs["c_ctx"]).reshape(1, D)
        m["st_in"] = np.ascontiguousarray(inputs["state_rwkv"][c])
        m["ck_in"] = np.ascontiguousarray(inputs["cache_k"][c]).reshape(NL, 512, KVW)
        m["cv_in"] = np.ascontiguousarray(inputs["cache_v"][c]).reshape(NL, 512, KVW)
        for n in W_NAMES:
            m[n] = np.ascontiguousarray(inputs[n]).reshape(W_SHAPES[n])
        for n, v in consts.items():
            m["cst_" + n] = v
        maps.append(m)
    return maps


def kernel(**inputs):
    nc, consts = build()
    maps = make_in_maps(inputs, consts)
    res = run_bass_kernel_spmd(nc, maps, core_ids=list(range(8)))
    r = res.results
    y_prompt = np.concatenate([r[c]["y_p"].reshape(4, 256, D) for c in range(8)], 0)
    y_sample = np.stack([r[c]["y_s"] for c in range(8)], 0)
    new_state = np.concatenate([r[c]["new_st"] for c in range(8)], 0)
    new_k = np.concatenate([r[c]["new_k"].reshape(4, NL, 256, 4, 64) for c in range(8)], 0)
    new_v = np.concatenate([r[c]["new_v"].reshape(4, NL, 256, 4, 64) for c in range(8)], 0)
    return (y_prompt.astype(np.float32), y_sample.astype(np.float32), new_state.astype(np.float32),
            new_k.astype(np.float32), new_v.astype(np.float32))


def attn_consts(c):
    t = np.arange(TS)
    row = (t // 64).astype(np.float32)
    col = (t % 64).astype(np.float32)
    nf = 16
    inv = (10000.0 ** (-np.arange(nf, dtype=np.float32) / nf)).astype(np.float32)
    ang = np.concatenate([row[:, None] * inv, col[:, None] * inv], -1).astype(np.float32)
    cs, sn = np.cos(ang).T, np.sin(ang).T
    c["ropeC"] = np.concatenate([cs, cs], 0).astype(np.float32)
    c["ropeS"] = np.concatenate([-sn, sn], 0).astype(np.float32)
    kk = np.arange(128)[:, None]
    qq = np.arange(128)[None, :]
    mprev = (qq <= kk).astype(np.float32)
    mnext = (kk <= qq).astype(np.float32)
    c["mprev"] = np.tile(mprev[:, None, :], (1, 3, 1)).reshape(128, 384)
    c["mnext"] = np.tile(mnext[:, None, :], (1, 3, 1)).reshape(128, 384)


def phase_attn(cx, G, L, W, projT, projTok, mixT, ck_in, cv_in, cst):
    p = cx.p
    pflat = projT.rearrange("a p t -> (a p) t")
    mflat = mixT.rearrange("a p t -> (a p) t")
    with ExitStack() as es:
        ropeC, rcB = cx.sb(es, [64, TS], F32, "ropeC")
        ropeS, rsB = cx.sb(es, [64, TS], F32, "ropeS")
        p.dma("sp", ropeC[:], cst["ropeC"], wr=[rcB])
        p.dma("sp", ropeS[:], cst["ropeS"], wr=[rsB])
        mprev, mpB = cx.sb(es, [128, 384], BF16, "mprev")
        mnext, mnB = cx.sb(es, [128, 384], BF16, "mnext")
        p.dma("pool", mprev[:], cst["mprev"], wr=[mpB])
        p.dma("pool", mnext[:], cst["mnext"], wr=[mnB])
        esk, eskB = cx.sb(es, [128, 12], F32, "esk")
        p.dma("sp", esk[:], W["attn_sink"][L].partition_broadcast(128), wr=[eskB])
        p.ins("act", lambda e: e.activation(esk[:], esk[:], AF.Exp), rd=[eskB], wr=[eskB])
        qraw, qrB = cx.sb(es, [64, 3, TS], F32, "qraw")
        qsw, qsB = cx.sb(es, [64, 3, TS], F32, "qsw")
        qb, qbB = cx.sb(es, [64, 16, 3, 128], BF16, "qb")
        kraw, krB = cx.sb(es, [64, TS], F32, "kraw")
        ksw, ksB = cx.sb(es, [64, TS], F32, "ksw")
        kb, kbB = cx.sb(es, [64, TS], BF16, "kb")
        kc, kcB = cx.sb(es, [64, 512], BF16, "kc")
        ckraw, ckB = cx.sb(es, [128, 4, 64], F32, "ckraw")
        v, vB = cx.sb(es, [128, 16, 65], BF16, "v")
        vc, vcB = cx.sb(es, [128, 4, 65], BF16, "vc")
        oacc, oB = cx.sb(es, [64, 3, TS], BF16, "oacc")
        pTr = Ring([cx.sb(es, [128, 384], BF16, "pT") for _ in range(3)])
        dbr = Ring([(cx.sb(es, [128, 384], F32, "den"), cx.sb(es, [64, 384], F32, "bc")) for _ in range(2)])
        pending = [None]
        p.ins("pool", lambda e: e.memset(v[:, :, 64:65], 1.0), wr=[vB])
        p.ins("pool", lambda e: e.memset(vc[:, :, 64:65], 1.0), wr=[vcB])

        for (tok0, Ls) in SEGS:
            sample = Ls == TS
            nb = Ls // 128
            for g in range(4):
                for h in range(3):
                    r0 = C_Q + (3 * g + h) * 64
                    p.dma("sp", qraw[:, h, 0:Ls], pflat[r0:r0 + 64, tok0:tok0 + Ls], wr=[qrB])
                    if sample:
                        p.dma("sp", qsw[0:32, h, 0:Ls], pflat[r0 + 32:r0 + 64, tok0:tok0 + Ls], wr=[qsB])
                        p.dma("sp", qsw[32:64, h, 0:Ls], pflat[r0:r0 + 32, tok0:tok0 + Ls], wr=[qsB])
                r0 = C_AK + g * 64
                p.dma("sp", kraw[:, 0:Ls], pflat[r0:r0 + 64, tok0:tok0 + Ls], wr=[krB])
                if sample:
                    p.dma("sp", ksw[0:32, 0:Ls], pflat[r0 + 32:r0 + 64, tok0:tok0 + Ls], wr=[ksB])
                    p.dma("sp", ksw[32:64, 0:Ls], pflat[r0:r0 + 32, tok0:tok0 + Ls], wr=[ksB])
                p.dma("pool", v[:, 0:nb, 0:64],
                      projTok[tok0:tok0 + Ls, 1792 + g * 64:1792 + g * 64 + 64].rearrange("(tt p) d -> p tt d", p=128), wr=[vB])
                for h in range(3):
                    dst = qb[:, 0:nb, h, :]
                    if sample:
                        p.ins("dve", lambda e, h=h: e.tensor_tensor(qraw[:, h, :], qraw[:, h, :], ropeC[:], ALU.mult), rd=[qrB, rcB], wr=[qrB])
                        p.ins("pool", lambda e, h=h: e.tensor_tensor(qsw[:, h, :], qsw[:, h, :], ropeS[:], ALU.mult), rd=[qsB, rsB], wr=[qsB])
                        p.ins("dve", lambda e, h=h, dst=dst: e.tensor_tensor(dst, qraw[:, h, :].rearrange("p (b t) -> p b t", t=128),
                                                                           qsw[:, h, :].rearrange("p (b t) -> p b t", t=128), ALU.add),
                              rd=[qrB, qsB], wr=[qbB])
                    else:
                        p.ins("dve", lambda e, h=h, dst=dst: e.tensor_copy(dst, qraw[:, h, 0:Ls].rearrange("p (b t) -> p b t", t=128)),
                              rd=[qrB], wr=[qbB])
                if sample:
                    p.ins("dve", lambda e: e.tensor_tensor(kraw[:], kraw[:], ropeC[:], ALU.mult), rd=[krB, rcB], wr=[krB])
                    p.ins("pool", lambda e: e.tensor_tensor(ksw[:], ksw[:], ropeS[:], ALU.mult), rd=[ksB, rsB], wr=[ksB])
                    p.ins("dve", lambda e: e.tensor_tensor(kb[:], kraw[:], ksw[:], ALU.add), rd=[krB, ksB], wr=[kbB])
                    p.dma("sp", ckraw[:], ck_in[L][:, g * 64:(g + 1) * 64].rearrange("(tt p) d -> p tt d", p=128), wr=[ckB])
                    ps, pb = G.ps.next()
                    for j in range(4):
                        mm(p, ps[0:64, j * 128:(j + 1) * 128], [(ckraw[:, j, :], G.ident[:])], rd=[ckB, G.identB], wr=[pb])
                    p.ins("act", lambda e: e.activation(kc[:], ps[0:64, :], AF.Copy), rd=[pb], wr=[kcB])
                    p.dma("pool", vc[:, :, 0:64], cv_in[L][:, g * 64:(g + 1) * 64].rearrange("(tt p) d -> p tt d", p=128), wr=[vcB])
                else:
                    p.ins("act", lambda e: e.activation(kb[:, 0:Ls], kraw[:, 0:Ls], AF.Copy), rd=[krB], wr=[kbB])
                for i in range(nb):
                    keys = []
                    if sample:
                        for j in (i - 1, i, i + 1):
                            if 0 <= j < nb:
                                keys.append((kb[:, j * 128:(j + 1) * 128], kbB, v[:, j, :], vB,
                                             (mprev, mpB) if j == i - 1 else ((mnext, mnB) if j == i + 1 else None)))
                        for j in range(4):
                            keys.append((kc[:, j * 128:(j + 1) * 128], kcB, vc[:, j, :], vcB, None))
                    else:
                        for j in range(nb):
                            keys.append((kb[:, j * 128:(j + 1) * 128], kbB, v[:, j, :], vB, None))
                    psO, pOB = G.psx.next()
                    nk = len(keys)

                    def score(ki):
                        kap, kB_ = keys[ki][0], keys[ki][1]
                        psS, pSB = G.ps.next()
                        mm(p, psS[:, 0:384], [(kap, qb[:, i, :, :].rearrange("p h t -> p (h t)"))], rd=[kB_, qbB], wr=[pSB])
                        return psS, pSB
                    nxt = score(0)
                    for ki, (kap, kB_, vap, vB_, msk) in enumerate(keys):
                        psS, pSB = nxt
                        if ki + 1 < nk:
                            nxt = score(ki + 1)
                        if ki == min(3, nk - 1) and pending[0] is not None:
                            pending[0]()
                            pending[0] = None
                        pT, pTB = pTr.next()
                        p.ins("act", lambda e, pT=pT, psS=psS: e.activation(pT[:], psS[:, 0:384], AF.Exp, scale=0.125), rd=[pSB], wr=[pTB])
                        if msk is not None:
                            p.ins("dve", lambda e, pT=pT, msk=msk: e.tensor_tensor(pT[:], pT[:], msk[0][:], ALU.mult), rd=[pTB, msk[1]], wr=[pTB])
                        p.ins("pe", lambda e, vap=vap, pT=pT, ki=ki, psO=psO: e.matmul(psO[0:65, 0:384], vap, pT[:], start=(ki == 0), stop=(ki == nk - 1)),
                              rd=[vB_, pTB], wr=[pOB])

                    def tail(i=i, g=g, psO=psO, pOB=pOB):
                        (den, denB), (bc, bcB) = dbr.next()
                        for h in range(3):
                            hh = 3 * g + h
                            p.ins("dve", lambda e, h=h, hh=hh: e.tensor_scalar(den[64:65, h * 128:(h + 1) * 128], psO[64:65, h * 128:(h + 1) * 128],
                                                                             esk[64:65, hh:hh + 1], None, ALU.add), rd=[pOB, eskB], wr=[denB])
                        p.ins("dve", lambda e: e.reciprocal(den[64:65, :], den[64:65, :]), rd=[denB], wr=[denB])
                        psB, pBB = G.ps.next()
                        mm(p, psB[0:64, 0:384], [(G.ones[64:65, 0:64], den[64:65, :])], rd=[G.onesB, denB], wr=[pBB])
                        p.ins("act", lambda e: e.activation(bc[:], psB[0:64, 0:384], AF.Copy), rd=[pBB], wr=[bcB])
                        p.ins("dve", lambda e: e.tensor_tensor(oacc[:, :, i * 128:(i + 1) * 128], psO[0:64, 0:384].rearrange("p (h t) -> p h t", h=3),
                                                              bc[:].rearrange("p (h t) -> p h t", h=3), ALU.mult), rd=[pOB, bcB], wr=[oB])
                    pending[0] = tail
                if pending[0] is not None:
                    pending[0]()
                    pending[0] = None
                for h in range(3):
                    r0 = 1280 + (3 * g + h) * 64
                    p.dma("sp", mflat[r0:r0 + 64, tok0:tok0 + Ls], oacc[:, h, 0:Ls], rd=[oB])
    p.barrier()


HY_LS = (256, 2048)


def hyena_consts(c):
    for Lq in HY_LS:
        LT = Lq // 128
        FTn = LT + 1
        TBL = min(512, Lq)
        NTB = Lq // TBL
        f = np.arange(FTn * 128, dtype=np.float64)
        t = np.arange(Lq, dtype=np.float64)
        valid = (f <= Lq).astype(np.float64)
        ang = np.pi * np.outer(t, f) / Lq
        Cf = np.cos(ang) * valid[None, :]
        Sf = np.sin(ang) * valid[None, :]
        lay = lambda X: np.ascontiguousarray(X.reshape(LT, 128, FTn, 128).transpose(2, 1, 0, 3)).astype(np.float32)
        c["hyCf%d" % Lq] = lay(Cf)
        c["hySf%d" % Lq] = lay(Sf)
        wf = np.where((f == 0) | (f == Lq), 1.0, 2.0) * valid / (2.0 * Lq)
        Ci = (np.cos(ang) * wf[None, :]).T
        Si = (np.sin(ang) * wf[None, :]).T
        layi = lambda X: np.ascontiguousarray(X.reshape(FTn, 128, NTB, TBL).transpose(2, 1, 0, 3)).astype(np.float32)
        c["hyCi%d" % Lq] = layi(Ci)
        c["hySi%d" % Lq] = layi(Si)
        tt = np.linspace(0.0, 1.0, Lq, dtype=np.float32)[:, None]
        bands = 16
        t_res = np.arange(Lq, dtype=np.float32)[:, None]
        fr = np.linspace(1e-4, bands - 1, bands, dtype=np.float32)[None, :]
        a = (2.0 * math.pi * t_res * fr / Lq).astype(np.float32)
        z = np.concatenate([tt, np.cos(a), np.sin(a)], -1).astype(np.float32)
        c["hyz%d" % Lq] = np.ascontiguousarray(z.T)
        mn, mx = math.log(1e-2) / 1.5, math.log(1e-2) / 0.3
        deltas = np.abs(np.linspace(mn, mx, HW, dtype=np.float32))
        env = np.exp(-tt * deltas[None, :]).astype(np.float32)
        c["hyenv%d" % Lq] = np.ascontiguousarray(env.reshape(LT, 128, HW).transpose(1, 0, 2))


def wrap_pi(p, x, xB, tmp, tmpB, n=2):
    for _ in range(n):
        p.ins("dve", lambda e: e.tensor_scalar(tmp, x, -PI, 2 * PI, ALU.is_lt, ALU.mult), rd=[xB], wr=[tmpB])
        p.ins("dve", lambda e: e.tensor_tensor(x, x, tmp, ALU.add), rd=[xB, tmpB], wr=[xB])
        p.ins("dve", lambda e: e.tensor_scalar(tmp, x, PI, 2 * PI, ALU.is_gt, ALU.mult), rd=[xB], wr=[tmpB])
        p.ins("dve", lambda e: e.tensor_tensor(x, x, tmp, ALU.subtract), rd=[xB, tmpB], wr=[xB])


def spectrum(cx, G, X, XB, Lq, cst, which, tabr, sink):
    p = cx.p
    LT = Lq // 128
    for ft in range(LT + 1):
        for cs in which:
            tb, tbB = tabr.next()
            p.dma("pool", tb[:, 0:LT, :], cst["hy%sf%d" % (cs, Lq)][ft], wr=[tbB], max_dma_last_dim=4096)
            ps, pb = G.ps.next()
            mm(p, ps[:], [(tb[:, tt, :], X[:, tt, :]) for tt in range(LT)], rd=[tbB, XB], wr=[pb])
            sink(ft, cs, ps, pb)


def hyena_filters(cx, G, L, W, Lq, cst, Kd):
    p = cx.p
    LT = Lq // 128
    CH = min(512, Lq)
    with ExitStack() as es:
        zT, zB = cx.sb(es, [33, Lq], F32, "hzT")
        p.dma("sp", zT[:], cst["hyz%d" % Lq], wr=[zB])
        w1, w1B = cx.sb(es, [33, 64], F32, "hw1")
        p.dma("sp", w1[:], W["hy_f_w1"][L], wr=[w1B])
        w2, w2B = cx.sb(es, [64, 64], F32, "hw2")
        p.dma("sp", w2[:], W["hy_f_w2"][L], wr=[w2B])
        w3, w3B = cx.sb(es, [64, 2048], F32, "hw3")
        p.dma("sp", w3[:], W["hy_f_w3"][L], wr=[w3B])
        sc, scB = cx.sb(es, [64, 4], F32, "hsc")
        for i, nm in enumerate(("hy_f_b1", "hy_f_freq1", "hy_f_b2", "hy_f_freq2")):
            p.dma("sp", sc[:, i:i + 1], W[nm][L].rearrange("(p o) -> p o", o=1), wr=[scB])
        hm1, h1B = cx.sb(es, [64, Lq], F32, "hm1")
        hm2, h2B = cx.sb(es, [64, Lq], F32, "hm2")
        tmp, tmpB = cx.sb(es, [64, CH], F32, "hwtmp")
        for (src, sB, wgt, wB, dst, dB, bi) in ((zT, zB, w1, w1B, hm1, h1B, 0), (hm1, h1B, w2, w2B, hm2, h2B, 2)):
            for c0 in range(0, Lq, CH):
                ps, pb = G.ps.next()
                mm(p, ps[0:64, 0:CH], [(wgt[:], src[:, c0:c0 + CH])], rd=[wB, sB], wr=[pb])
                d = dst[:, c0:c0 + CH]
                p.ins("dve", lambda e, d=d, ps=ps, bi=bi: e.tensor_scalar(d, ps[0:64, 0:CH], sc[:, bi:bi + 1], sc[:, bi + 1:bi + 2], ALU.add, ALU.mult),
                      rd=[pb, scB], wr=[dB])
                wrap_pi(p, d, dB, tmp[:], tmpB)
                p.ins("act", lambda e, d=d: e.activation(d, d, AF.Sin), rd=[dB], wr=[dB])
        env, envB = cx.sb(es, [128, LT, HW], F32, "henv")
        p.dma("sp", env[:], cst["hyenv%d" % Lq], wr=[envB])
        PM = [cx.sb(es, [128, LT, HW], BF16, "hPM%d" % i) for i in range(4)]
        hfb = [cx.sb(es, [128, HW], F32, "hfb%d" % i) for i in range(4)]
        for lt in range(LT):
            for cb in range(4):
                ps, pb = G.ps.next()
                mm(p, ps[:], [(hm2[:, lt * 128:(lt + 1) * 128], w3[:, cb * 512:(cb + 1) * 512])], rd=[h2B, w3B], wr=[pb])
                hb, hbB = hfb[cb]
                p.ins("dve", lambda e, hb=hb, ps=ps, lt=lt: e.tensor_tensor(hb[:], ps[:], env[:, lt, :], ALU.mult), rd=[pb, envB], wr=[hbB])
                if lt == 0 and cb % 2 == 1:
                    p.ins("dve", lambda e, hb=hb: e.memset(hb[0:1, :], 0.0), rd=[], wr=[hbB])
            for n in range(2):
                (hc, hcB), (ha, haB) = hfb[2 * n], hfb[2 * n + 1]
                p.ins("pool", lambda e, hc=hc, ha=ha, n=n, lt=lt: e.tensor_tensor(PM[2 * n][0][:, lt, :], hc[:], ha[:], ALU.add),
                      rd=[hcB, haB], wr=[PM[2 * n][1]])
                p.ins("pool", lambda e, hc=hc, ha=ha, n=n, lt=lt: e.tensor_tensor(PM[2 * n + 1][0][:, lt, :], hc[:], ha[:], ALU.subtract),
                      rd=[hcB, haB], wr=[PM[2 * n + 1][1]])
        tabr = Ring([cx.sb(es, [128, LT, 128], BF16, "hftab") for _ in range(3)])
        stg = Ring([cx.sb(es, [128, HW], F32, "hkst") for _ in range(3)])
        for n in range(2):
            for ci, cs in enumerate(("C", "S")):
                X, XB = PM[2 * n + ci]

                def sink(ft, cs_, ps, pb, n=n, ci=ci):
                    st, sB_ = stg.next()
                    p.ins("act", lambda e: e.activation(st[:], ps[:], AF.Copy), rd=[pb], wr=[sB_])
                    p.dma("sp", Kd[n, ci, ft], st[:], rd=[sB_])
                spectrum(cx, G, X, XB, Lq, cst, [cs], tabr, sink)
    p.barrier()


def hyena_short(cx, G, L, W, projT, hyT):
    p = cx.p
    GL = T + 6
    with ExitStack() as es:
        sw, swB = cx.sb(es, [128, 3, 12], F32, "hsw")
        for k in range(3):
            vecT(cx, G, es, rows128(W["hy_short_w"][L, k]), 12, sw[:, k, :], swB)
        ur = Ring([cx.sb(es, [128, GL], F32, "hu") for _ in range(2)])
        cr = Ring([cx.sb(es, [128, GL], F32, "hc") for _ in range(2)])
        for (t_, b_) in ur.items:
            p.ins("pool", lambda e, t_=t_: e.memset(t_[:], 0.0), wr=[b_])
        for i in range(12):
            u, uB = ur.next()
            c, cB = cr.next()
            for s, (a, n) in enumerate(SEGS):
                p.dma("sp", u[:, a + s + 1:a + n + s + 1], projT[24 + i, :, a:a + n], wr=[uB])
            p.ins("dve", lambda e: e.tensor_scalar(c[:, 1:GL - 1], u[:, 1:GL - 1], sw[:, 1, i:i + 1], None, ALU.mult), rd=[uB, swB], wr=[cB])
            p.ins("dve", lambda e: e.scalar_tensor_tensor(c[:, 1:GL - 1], u[:, 0:GL - 2], sw[:, 0, i:i + 1], c[:, 1:GL - 1], ALU.mult, ALU.add),
                  rd=[uB, swB, cB], wr=[cB])
            p.ins("dve", lambda e: e.scalar_tensor_tensor(c[:, 1:GL - 1], u[:, 2:GL], sw[:, 2, i:i + 1], c[:, 1:GL - 1], ALU.mult, ALU.add),
                  rd=[uB, swB, cB], wr=[cB])
            for s, (a, n) in enumerate(SEGS):
                p.dma("sp", hyT[i, :, a:a + n], c[:, a + s + 1:a + n + s + 1], rd=[cB])
    p.barrier()


def hyena_seq(cx, G, L, W, Lq, tok0, hyT, Kd, mixT, cst, biasT, biasB):
    p = cx.p
    LT = Lq // 128
    FTn = LT + 1
    TBL = min(512, Lq)
    NTB = Lq // TBL
    with ExitStack() as es:
        zf = [cx.sb(es, [128, 4, Lq], F32, "hzf%d" % i) for i in range(2)]
        Z, ZB = cx.sb(es, [128, LT, HW], BF16, "hZ")
        Yc, YcB = cx.sb(es, [128, FTn, HW], BF16, "hYc")
        Ys, YsB = cx.sb(es, [128, FTn, HW], BF16, "hYs")
        tabr = Ring([cx.sb(es, [128, LT, 128], BF16, "hftab") for _ in range(4)])
        itab = [cx.sb(es, [128, FTn, TBL], BF16, "hitab%d" % i) for i in range(2)]
        kt = Ring([(cx.sb(es, [128, HW], F32, "hKc"), cx.sb(es, [128, HW], F32, "hKs")) for _ in range(2)])
        tmps = Ring([cx.sb(es, [128, HW], F32, "htmp") for _ in range(4)])
        xr = Ring([cx.sb(es, [128, TBL], F32, "hx") for _ in range(2)])
        ob, obB = cx.sb(es, [128, TBL], BF16, "hob")
        cur, curB = zf[0]
        for ct in range(4):
            p.dma("sp", cur[:, ct, :], hyT[ct, :, tok0:tok0 + Lq], wr=[curB])
        for n in range(2):
            cur, curB = zf[n % 2]
            nxt, nxtB = zf[(n + 1) % 2]
            for tt in range(LT):
                ps, pb = G.ps.next()
                for ct in range(4):
                    mm(p, ps[:, ct * 128:(ct + 1) * 128], [(cur[:, ct, tt * 128:(tt + 1) * 128], G.ident[:])], rd=[curB, G.identB], wr=[pb])
                p.ins("act", lambda e, tt=tt, ps=ps: e.activation(Z[:, tt, :], ps[:], AF.Copy), rd=[pb], wr=[ZB])
            state = {}

            def sink(ft, cs, ps, pb, n=n):
                if cs == "C":
                    state["c"] = (ps, pb)
                    return
                (psc, pbc), (pss, pbs) = state["c"], (ps, pb)
                (Kc, KcB), (Ks, KsB) = kt.next()
                p.dma("sp", Kc[:], Kd[n, 0, ft], wr=[KcB])
                p.dma("sp", Ks[:], Kd[n, 1, ft], wr=[KsB])
                (t1, t1B), (t2, t2B), (t3, t3B), (t4, t4B) = tmps.next(), tmps.next(), tmps.next(), tmps.next()
                p.ins("dve", lambda e: e.tensor_tensor(t1[:], psc[:], Kc[:], ALU.mult), rd=[pbc, KcB], wr=[t1B])
                p.ins("dve", lambda e: e.tensor_tensor(t2[:], pss[:], Ks[:], ALU.mult), rd=[pbs, KsB], wr=[t2B])
                p.ins("dve", lambda e: e.tensor_tensor(t3[:], psc[:], Ks[:], ALU.mult), rd=[pbc, KsB], wr=[t3B])
                p.ins("dve", lambda e: e.tensor_tensor(t4[:], pss[:], Kc[:], ALU.mult), rd=[pbs, KcB], wr=[t4B])
                p.ins("pool", lambda e: e.tensor_tensor(Yc[:, ft, :], t1[:], t2[:], ALU.subtract), rd=[t1B, t2B], wr=[YcB])
                p.ins("pool", lambda e: e.tensor_tensor(Ys[:, ft, :], t3[:], t4[:], ALU.add), rd=[t3B, t4B], wr=[YsB])
            spectrum(cx, G, Z, ZB, Lq, cst, ["C", "S"], tabr, sink)
            for tb in range(NTB):
                (Ci, CiB), (Si, SiB) = itab
                p.dma("pool", Ci[:], cst["hyCi%d" % Lq][tb], wr=[CiB], max_dma_last_dim=4096)
                p.dma("pool", Si[:], cst["hySi%d" % Lq][tb], wr=[SiB], max_dma_last_dim=4096)
                for ct in range(4):
                    ps, pb = G.ps.next()
                    pairs = [(Yc[:, ft, ct * 128:(ct + 1) * 128], Ci[:, ft, :]) for ft in range(FTn)] + \
                            [(Ys[:, ft, ct * 128:(ct + 1) * 128], Si[:, ft, :]) for ft in range(FTn)]
                    mm(p, ps[:, 0:TBL], pairs, rd=[YcB, YsB, CiB, SiB], wr=[pb])
                    x, xB = xr.next()
                    p.dma("sp", x[:], hyT[4 * (n + 1) + ct, :, tok0 + tb * TBL:tok0 + (tb + 1) * TBL], wr=[xB])
                    (t1, t1B) = tmps.next()
                    zsl = cur[:, ct, tb * TBL:(tb + 1) * TBL]
                    p.ins("dve", lambda e, t1=t1, zsl=zsl, ps=ps, ct=ct, n=n: e.scalar_tensor_tensor(t1[:, 0:TBL], zsl, biasT[:, n * 4 + ct:n * 4 + ct + 1],
                                                                                                 ps[:, 0:TBL], ALU.mult, ALU.add),
                          rd=[curB, pb, biasB], wr=[t1B])
                    if n == 0:
                        p.ins("pool", lambda e, t1=t1, x=x, ct=ct, tb=tb: e.tensor_tensor(nxt[:, ct, tb * TBL:(tb + 1) * TBL], t1[:, 0:TBL], x[:], ALU.mult),
                              rd=[t1B, xB], wr=[nxtB])
                    else:
                        p.ins("pool", lambda e, t1=t1, x=x: e.tensor_tensor(ob[:], t1[:, 0:TBL], x[:], ALU.mult), rd=[t1B, xB], wr=[obB])
                        p.dma("sp", mixT[6 + ct, :, tok0 + tb * TBL:tok0 + (tb + 1) * TBL], ob[:], rd=[obB])


def phase_hyena(cx, G, L, W, projT, mixT, cst):
    p = cx.p
    hyT = G.hyT
    hyena_short(cx, G, L, W, projT, hyT)
    with ExitStack() as es:
        biasT, biasB = cx.sb(es, [128, 8], F32, "hbias")
        vecT(cx, G, es, rows128(W["hy_bias"][L]), 8, biasT[:], biasB)
        for Lq in HY_LS:
            hyena_filters(cx, G, L, W, Lq, cst, G.Kd[Lq])
        for (tok0, Ls) in SEGS:
            hyena_seq(cx, G, L, W, Ls, tok0, hyT, G.Kd[Ls], mixT, cst, biasT, biasB)
            p.barrier()


def rwkv_consts(c):
    i = np.arange(128)[:, None]
    t = np.arange(128)[None, :]
    sf, inf_ = (i < t).astype(np.float32), (i <= t).astype(np.float32)
    sb_, inb = (i > t).astype(np.float32), (i >= t).astype(np.float32)
    c["rmaskF"] = np.concatenate([sf, inf_, sf, inf_], 1)
    c["rmaskB"] = np.concatenate([sb_, inb, sb_, inb], 1)
    bo = np.zeros((128, 128), np.float32)
    bo[:64, :64] = 1
    bo[64:, 64:] = 1
    c["blockones"] = bo
    hi = np.zeros((128, 2), np.float32)
    hi[:64, 0] = 1
    hi[64:, 1] = 1
    c["hind"] = hi


def phase_rwkv(cx, G, L, W, projT, projTok, loraT, loraTB, mixT, st_in, new_st, cst):
    p = cx.p
    nc = cx.nc
    EM = math.exp(-0.5)
    banks = G.banks
    ps4 = Ring(banks[0:4])
    quarters = []
    for j in range(4):
        for (bt, bb) in banks[4:8]:
            quarters.append((bt[:, j * 128:(j + 1) * 128], bb))
    psq = Ring(quarters)
    with ExitStack() as es:
        def arr(name, dt=F32, n=TS):
            return cx.sb(es, [128, n], dt, name)
        maskF, mFB = cx.sb(es, [128, 512], F32, "rmF")
        maskB, mBB = cx.sb(es, [128, 512], F32, "rmB")
        p.dma("sp", maskF[:], cst["rmaskF"], wr=[mFB])
        p.dma("sp", maskB[:], cst["rmaskB"], wr=[mBB])
        bones, boB = cx.sb(es, [128, 128], F32, "bones")
        p.dma("sp", bones[:], cst["blockones"], wr=[boB])
        hind, hiB = cx.sb(es, [128, 2], F32, "hind")
        p.dma("sp", hind[:], cst["hind"], wr=[hiB])
        w2a, w2B = cx.sb(es, [128, RW], BF16, "w2a")
        a2a, a2B = cx.sb(es, [128, RW], BF16, "a2a")
        p.dma("pool", w2a[:], W["rwkv_w2"][L].rearrange("d k n -> (d k) n"), wr=[w2B])
        p.dma("pool", a2a[:], W["rwkv_a2"][L].rearrange("d k n -> (d k) n"), wr=[a2B])
        pv, pvB = cx.sb(es, [128, 48], F32, "rpv")
        for d in range(2):
            vecT(cx, G, es, rows128(W["rwkv_w0"][L, d]), 6, pv[:, d * 6:d * 6 + 6], pvB)
            vecT(cx, G, es, rows128(W["rwkv_a0"][L, d]), 6, pv[:, 12 + d * 6:12 + d * 6 + 6], pvB)
        vecT(cx, G, es, rows128(W["rwkv_k_k"][L]), 6, pv[:, 24:30], pvB)
        vecT(cx, G, es, rows128(W["rwkv_k_a"][L]), 6, pv[:, 30:36], pvB)
        vecT(cx, G, es, rows128(W["rwkv_r_k"][L]), 6, pv[:, 42:48], pvB)
        p.ins("dve", lambda e: e.tensor_scalar(pv[:, 36:42], pv[:, 30:36], -1.0, 1.0, ALU.mult, ALU.add), rd=[pvB], wr=[pvB])
        lg, lgB = cx.sb(es, [128, 128], F32, "lnxg")
        lb, lbB = cx.sb(es, [128, 128], F32, "lnxb")
        R, RB = arr("R")
        Kf, KfB = arr("Kf")
        KAP, KAPB = arr("KAP")
        KDS, KDSB = arr("KDS")
        A1, A1B = arr("A1")
        A2, A2B = arr("A2")
        A3, A3B = arr("A3")
        A4, A4B = arr("A4")
        A5, A5B = arr("A5")
        A6, A6B = arr("A6")
        A7, A7B = arr("A7")
        A8, A8B = arr("A8")
        comb, combB = cx.sb(es, [128, 16, 2, 128], BF16, "comb")
        BH, BHB = arr("BH", BF16)
        KH, KHB = arr("KH", BF16)
        BTOK, BTB = cx.sb(es, [128, 16, 128], BF16, "BTOK")
        KTOK, KTB = cx.sb(es, [128, 16, 128], BF16, "KTOK")
        VTOK, VTB = cx.sb(es, [128, 16, 128], BF16, "VTOK")
        OSUM, OSB = cx.sb(es, [128, 16, 128], F32, "OSUM")
        ATs = [[cx.sb(es, [128, 384], BF16, "ATs") for h in range(2)] for c in range(16)]
        TTb = [[cx.sb(es, [128, 128], BF16, "TTb") for h in range(2)] for c in range(16)]
        NB = 5
        chain = []
        for _ in range(NB):
            d_ = {"N": [cx.sb(es, [128, 128], BF16, "chN") for _ in range(2)],
                  "TT": [cx.sb(es, [128, 128], F32, "chTT") for _ in range(2)], "MT": []}
            for _k in range(2):
                t_, b1_ = cx.sb(es, [128, 256], BF16, "chMT")
                d_["MT"].append((t_, b1_, Buf("ts")))
            chain.append(d_)
        halves = []
        for j in range(2):
            for (bt, bb) in banks[4:8]:
                halves.append((bt[:, j * 256:(j + 1) * 256], bb))
        psh = Ring(halves)
        cols, colsB = cx.sb(es, [128, 4, 16], F32, "cols")
        sin1 = cx.sb(es, [64, 128], F32, "sin")
        seqbufs = [(cx.sb(es, [128, 64], F32, "S"), cx.sb(es, [128, 64], BF16, "Sb"), sin1,
                    cx.sb(es, [128, 128], BF16, "Xs"), cx.sb(es, [128, 128], BF16, "SAs")) for _ in range(4)]
        st4 = cx.sb(es, [128, 64], F32, "gst")
        bs_s, bsB = cx.sb(es, [128, 32], F32, "bs")

        def v3(a, n):
            return a[:, 0:n * 128].rearrange("p (c t) -> p c t", t=128)

        RSTOP = 9
        RSUB = 99
        RUNIT_DEFS = [(0, 1024, [(0, 256, 0), (256, 256, 1), (512, 256, 2), (768, 256, 3)], False),
                      (1024, 2048, [(0, 2048, None)], True)]
        for (tok0, Ls, seqs, sample) in RUNIT_DEFS:
            nch = Ls // 128
            CH = min(512, Ls)
            for hp in range(6):
                p.dma("sp", R[:, 0:Ls], projT[hp, :, tok0:tok0 + Ls], wr=[RB])
                p.dma("sp", Kf[:, 0:Ls], projT[6 + hp, :, tok0:tok0 + Ls], wr=[KfB])
                p.dma("sp", v3(A3, nch), projTok[tok0:tok0 + Ls, hp * 128:(hp + 1) * 128].rearrange("(c p) f -> p c f", p=128), wr=[A3B])
                p.ins("act", lambda e: e.activation(VTOK[:, 0:nch, :], v3(A3, nch), AF.Copy), rd=[A3B], wr=[VTB])
                p.ins("dve", lambda e: e.tensor_scalar(A7[:, 0:Ls], Kf[:, 0:Ls], pv[:, 24 + hp:25 + hp], None, ALU.mult), rd=[KfB, pvB], wr=[A7B])
                p.ins("pool", lambda e: e.tensor_tensor(A8[:, 0:Ls], A7[:, 0:Ls], A7[:, 0:Ls], ALU.mult), rd=[A7B], wr=[A8B])
                for c0 in range(0, Ls, CH):
                    ps, pb = ps4.next()
                    mm(p, ps[:, 0:CH], [(bones[:], A8[:, c0:c0 + CH])], rd=[boB, A8B], wr=[pb])
                    p.ins("dve", lambda e, ps=ps, c0=c0: e.tensor_scalar(KAP[:, c0:c0 + CH], ps[:, 0:CH], 1e-30, None, ALU.add), rd=[pb], wr=[KAPB])
                p.ins("act", lambda e: e.activation(KAP[:, 0:Ls], KAP[:, 0:Ls], AF.Sqrt), rd=[KAPB], wr=[KAPB])
                p.ins("dve", lambda e: e.reciprocal(KAP[:, 0:Ls], KAP[:, 0:Ls]), rd=[KAPB], wr=[KAPB])
                p.ins("dve", lambda e: e.tensor_tensor(KAP[:, 0:Ls], KAP[:, 0:Ls], A7[:, 0:Ls], ALU.mult), rd=[KAPB, A7B], wr=[KAPB])
                for d in range(2):
                    if RSTOP < 1:
                        break
                    dr = slice(64 * d, 64 * d + 64)
                    mask, mB = (maskF, mFB) if d == 0 else (maskB, mBB)
                    maskN, mNB = (maskB, mBB) if d == 0 else (maskF, mFB)
                    for c0 in range(0, Ls, CH):
                        ps, pb = ps4.next()
                        mm(p, ps[:, 0:CH], [(w2a[dr, hp * 128:(hp + 1) * 128], loraT[dr, 0, tok0 + c0:tok0 + c0 + CH])], rd=[w2B, loraTB], wr=[pb])
                        p.ins("act", lambda e, ps=ps, c0=c0: e.activation(A1[:, c0:c0 + CH], ps[:, 0:CH], AF.Sigmoid, bias=pv[:, d * 6 + hp:d * 6 + hp + 1]),
                              rd=[pb, pvB], wr=[A1B])
                        ps, pb = ps4.next()
                        mm(p, ps[:, 0:CH], [(a2a[dr, hp * 128:(hp + 1) * 128], loraT[dr, 1, tok0 + c0:tok0 + c0 + CH])], rd=[a2B, loraTB], wr=[pb])
                        p.ins("act", lambda e, ps=ps, c0=c0: e.activation(A4[:, c0:c0 + CH], ps[:, 0:CH], AF.Sigmoid, bias=pv[:, 12 + d * 6 + hp:12 + d * 6 + hp + 1]),
                              rd=[pb, pvB], wr=[A4B])
                    if RSUB <= 1:
                        continue
                    p.ins("dve", lambda e: e.tensor_scalar(A1[:, 0:Ls], A1[:, 0:Ls], -EM, None, ALU.mult), rd=[A1B], wr=[A1B])
                    for (sa_, sn_, _) in seqs:
                        p.ins("dve", lambda e, sa_=sa_, sn_=sn_: e.tensor_tensor_scan(A2[:, sa_:sa_ + sn_], A1[:, sa_:sa_ + sn_], A1[:, sa_:sa_ + sn_], 0.0, ALU.add, ALU.min),
                              rd=[A1B], wr=[A2B])
                    p.ins("pool", lambda e: e.tensor_tensor(A3[:, 0:Ls], A2[:, 0:Ls], A1[:, 0:Ls], ALU.subtract), rd=[A2B, A1B], wr=[A3B])
                    p.ins("dve", lambda e: e.tensor_copy(cols[:, 0, 0:nch], v3(A3, nch)[:, :, 0]), rd=[A3B], wr=[colsB])
                    p.ins("dve", lambda e: e.tensor_copy(cols[:, 1, 0:nch], v3(A2, nch)[:, :, 127]), rd=[A2B], wr=[colsB])
                    p.ins("dve", lambda e: e.tensor_scalar(cols[:, 2:4, 0:nch], cols[:, 0:2, 0:nch], -1.0, None, ALU.mult), rd=[colsB], wr=[colsB])
                    if RSUB <= 2:
                        continue
                    p.ins("dve", lambda e: e.tensor_scalar(A5[:, 0:Ls], A4[:, 0:Ls], pv[:, 30 + hp:31 + hp], pv[:, 36 + hp:37 + hp], ALU.mult, ALU.add),
                          rd=[A4B, pvB], wr=[A5B])
                    p.ins("pool", lambda e: e.tensor_tensor(A5[:, 0:Ls], A5[:, 0:Ls], Kf[:, 0:Ls], ALU.mult), rd=[A5B, KfB], wr=[A5B])
                    if d == 0:
                        p.ins("pool", lambda e: e.tensor_copy(KDS[:, 0:Ls], A5[:, 0:Ls]), rd=[A5B], wr=[KDSB])
                    else:
                        p.ins("pool", lambda e: e.tensor_tensor(KDS[:, 0:Ls], KDS[:, 0:Ls], A5[:, 0:Ls], ALU.add), rd=[A5B, KDSB], wr=[KDSB])
                    p.ins("dve", lambda e: e.tensor_tensor(A6[:, 0:Ls], KAP[:, 0:Ls], A4[:, 0:Ls], ALU.mult), rd=[KAPB, A4B], wr=[A6B])
                    if RSUB <= 3:
                        continue
                    CS, CE, NCS, NCE = 0, 1, 2, 3
                    if d == 0:
                        spec = ((A1, A1B, A2, A2B, 1.0, NCS), (A4, A4B, A3, A3B, 1.0, NCS), (A7, A7B, A2, A2B, -1.0, CE), (A8, A8B, A2, A2B, -1.0, CS))
                    else:
                        spec = ((A1, A1B, A3, A3B, -1.0, CE), (A4, A4B, A2, A2B, -1.0, CE), (A7, A7B, A3, A3B, 1.0, NCS), (A8, A8B, A3, A3B, 1.0, NCE))
                    for (dst, dB, src, sB_, scl, ci) in spec:
                        for c in range(nch):
                            cs_ = slice(c * 128, (c + 1) * 128)
                            p.ins("act", lambda e, dst=dst, src=src, scl=scl, ci=ci, c=c, cs_=cs_: e.activation(dst[:, cs_], src[:, cs_], AF.Exp, bias=cols[:, ci, c:c + 1], scale=scl),
                                  rd=[sB_, colsB], wr=[dB])
                    if RSUB <= 4:
                        continue
                    p.ins("dve", lambda e: e.scalar_tensor_tensor(comb[:, 0:nch, 0, :], v3(KAP, nch), -1.0, v3(A4, nch), ALU.mult, ALU.mult), rd=[KAPB, A4B], wr=[combB])
                    p.ins("pool", lambda e: e.tensor_tensor(comb[:, 0:nch, 1, :], v3(R, nch), v3(A1, nch), ALU.mult), rd=[RB, A1B], wr=[combB])
                    p.ins("dve", lambda e: e.tensor_tensor(BH[:, 0:Ls], A6[:, 0:Ls], A8[:, 0:Ls], ALU.mult), rd=[A6B, A8B], wr=[BHB])
                    p.ins("pool", lambda e: e.tensor_tensor(KH[:, 0:Ls], A5[:, 0:Ls], A8[:, 0:Ls], ALU.mult), rd=[A5B, A8B], wr=[KHB])
                    p.ins("dve", lambda e: e.tensor_tensor(A6[:, 0:Ls], A6[:, 0:Ls], A7[:, 0:Ls], ALU.mult), rd=[A6B, A7B], wr=[A6B])
                    p.ins("pool", lambda e: e.tensor_tensor(A5[:, 0:Ls], A5[:, 0:Ls], A7[:, 0:Ls], ALU.mult), rd=[A5B, A7B], wr=[A5B])
                    if RSUB <= 5:
                        continue
                    for c in range(nch):
                        cs_ = slice(c * 128, (c + 1) * 128)
                        for (src, sB_, dst, dB) in ((A6, A6B, BTOK, BTB), (A5, A5B, KTOK, KTB)):
                            q, qB = psq.next()
                            mm(p, q, [(src[:, cs_], G.ident[:])], rd=[sB_, G.identB], wr=[qB])
                            p.ins("act", lambda e, dst=dst, q=q, c=c: e.activation(dst[:, c, :], q, AF.Copy), rd=[qB], wr=[dB])
                    if RSTOP < 2:
                        continue
                    units = [(c, h) for c in range(nch) for h in range(2)]
                    for b0 in range(0, len(units), NB):
                        batch = units[b0:b0 + NB]
                        for ui, (c, h) in enumerate(batch):
                            hr = slice(64 * h, 64 * h + 64)
                            cs_ = slice(c * 128, (c + 1) * 128)
                            ch_ = chain[ui]
                            psA, pAB = ps4.next()
                            rhs = comb[hr, c, :, :].rearrange("p a t -> p (a t)")
                            mm(p, psA[:, 0:256], [(BH[hr, cs_], rhs)], rd=[BHB, combB], wr=[pAB])
                            mm(p, psA[:, 256:512], [(KH[hr, cs_], rhs)], rd=[KHB, combB], wr=[pAB])
                            MT0, MB0, TSB0 = ch_["MT"][0]
                            p.ins("dve", lambda e, MT0=MT0, psA=psA: e.tensor_tensor(MT0[:, 0:128], psA[:, 0:128], mask[:, 0:128], ALU.mult), rd=[pAB, mB], wr=[MB0])
                            at, atB = ATs[c][h]
                            p.ins("dve", lambda e, at=at, psA=psA: e.tensor_tensor(at[:], psA[:, 128:512], mask[:, 128:512], ALU.mult), rd=[pAB, mB], wr=[atB])
                            q, qB = psh.next()
                            mm(p, q[:, 0:128], [(comb[hr, c, 0, :], BH[hr, cs_])], rd=[BHB, combB], wr=[qB])
                            (N0, N0B) = ch_["N"][0]
                            p.ins("dve", lambda e, N0=N0, q=q: e.tensor_tensor(N0[:], q[:, 0:128], maskN[:, 0:128], ALU.mult), rd=[qB, mNB], wr=[N0B])
                            p.ins("pool", lambda e, MT0=MT0: e.tensor_copy(MT0[:, 128:256], G.identb[:]), rd=[G.identbB], wr=[TSB0])
                        for i_ in range(0, 7):
                            a_, b_ = i_ % 2, (i_ + 1) % 2
                            for ui, (c, h) in enumerate(batch):
                                ch_ = chain[ui]
                                (Np, NpB), (Nn, NnB) = ch_["N"][a_], ch_["N"][b_]
                                MTp, MBp, TSBp = ch_["MT"][a_]
                                MTn, MBn, TSBn = ch_["MT"][b_]
                                (TTp, TTpB), (TTn, TTnB) = ch_["TT"][a_], ch_["TT"][b_]
                                if i_ < 6:
                                    q, qB = psh.next()
                                    mm(p, q[:, 0:128], [(MTp[:, 0:128], Np[:])], rd=[MBp, NpB], wr=[qB])
                                    p.ins("act", lambda e, Nn=Nn, q=q: e.activation(Nn[:], q[:, 0:128], AF.Copy), rd=[qB], wr=[NnB])
                                    f, fB = psh.next()
                                    mm(p, f[:, 0:256], [(Np[:], MTp[:, 0:256])], rd=[NpB, MBp, TSBp], wr=[fB])
                                    p.ins("act", lambda e, MTn=MTn, f=f: e.activation(MTn[:, 0:128], f[:, 0:128], AF.Copy), rd=[], wr=[fB, MBn])
                                    dT = f[:, 128:256]
                                else:
                                    f, fB = psh.next()
                                    mm(p, f[:, 0:128], [(Np[:], MTp[:, 128:256])], rd=[NpB, TSBp], wr=[fB])
                                    dT = f[:, 0:128]
                                if i_ == 0:
                                    p.ins("dve", lambda e, TTn=TTn, dT=dT: e.tensor_tensor(TTn[:], dT, G.ident[:], ALU.add), rd=[fB, G.identB], wr=[TTnB])
                                    p.ins("pool", lambda e, MTn=MTn, TTn=TTn: e.tensor_copy(MTn[:, 128:256], TTn[:]), rd=[TTnB], wr=[TSBn])
                                elif i_ < 6:
                                    p.ins("dve", lambda e, TTn=TTn, dT=dT, TTp=TTp: e.tensor_tensor(TTn[:], dT, TTp[:], ALU.add), rd=[fB, TTpB], wr=[TTnB])
                                    p.ins("pool", lambda e, MTn=MTn, TTn=TTn: e.tensor_copy(MTn[:, 128:256], TTn[:]), rd=[TTnB], wr=[TSBn])
                                else:
                                    tb_, tbB = TTb[c][h]
                                    p.ins("dve", lambda e, tb_=tb_, dT=dT, TTp=TTp: e.tensor_tensor(tb_[:], dT, TTp[:], ALU.add), rd=[fB, TTpB], wr=[tbB])
                    if RSTOP < 3:
                        continue
                    chains = []
                    for qi, (sa_, sn_, sidx) in enumerate(seqs):
                        (S, SB), (Sb, SbB), (sin_, sinB), (Xs, XsB), (SAs, SAsB) = seqbufs[qi]
                        if sample:
                            p.dma("sp", sin_[:].rearrange("v (h k) -> v h k", h=2), st_in[L, d, 2 * hp:2 * hp + 2].rearrange("h v k -> v h k"), wr=[sinB])
                            q, qB = psq.next()
                            mm(p, q[:, 0:64], [(sin_[:], G.ident[0:64, 0:64])], rd=[sinB, G.identB], wr=[qB])
                            p.ins("dve", lambda e, q=q, S=S: e.tensor_copy(S[:], q[:, 0:64]), rd=[qB], wr=[SB])
                        else:
                            p.ins("dve", lambda e, S=S: e.memset(S[:], 0.0), wr=[SB])
                        p.ins("act", lambda e, S=S, Sb=Sb: e.activation(Sb[:], S[:], AF.Copy), rd=[SB], wr=[SbB])
                        c0_, cn_ = sa_ // 128, sn_ // 128
                        order = list(range(c0_, c0_ + cn_)) if d == 0 else list(range(c0_ + cn_ - 1, c0_ - 1, -1))
                        chains.append((qi, sidx, order))
                    for step in range(max(len(o_) for (_, _, o_) in chains)):
                        for (qi, sidx, order) in chains:
                            if step >= len(order):
                                continue
                            c = order[step]
                            (S, SB), (Sb, SbB), (sin_, sinB), (Xs, XsB), (SAs, SAsB) = seqbufs[qi]
                            Xq, XqB = psq.next()
                            for h in range(2):
                                hr = slice(64 * h, 64 * h + 64)
                                hc = slice(64 * h, 64 * h + 64)
                                at, atB = ATs[c][h]
                                mm(p, Xq[:, hc], [(comb[hr, c, 0, :], Sb[hr, :]), (at[:, 128:256], VTOK[:, c, hc])], rd=[combB, SbB, atB, VTB], wr=[XqB])
                            p.ins("act", lambda e, Xq=Xq, Xs=Xs: e.activation(Xs[:], Xq, AF.Copy), rd=[XqB], wr=[XsB])
                            Sq, SqB = psq.next()
                            for h in range(2):
                                hc = slice(64 * h, 64 * h + 64)
                                tb_, tbB = TTb[c][h]
                                mm(p, Sq[:, hc], [(tb_[:], Xs[:, hc])], rd=[tbB, XsB], wr=[SqB])
                            p.ins("dve", lambda e, Sq=Sq, SAs=SAs: e.tensor_copy(SAs[:], Sq), rd=[SqB], wr=[SAsB])
                            Oq, OqB = psq.next()
                            for h in range(2):
                                hr = slice(64 * h, 64 * h + 64)
                                hc = slice(64 * h, 64 * h + 64)
                                at, atB = ATs[c][h]
                                mm(p, Oq[:, hc], [(comb[hr, c, 1, :], Sb[hr, :]), (at[:, 0:128], SAs[:, hc]), (at[:, 256:384], VTOK[:, c, hc])],
                                   rd=[combB, SbB, atB, SAsB, VTB], wr=[OqB])
                            if d == 0:
                                p.ins("act", lambda e, Oq=Oq, c=c: e.activation(OSUM[:, c, :], Oq, AF.Copy), rd=[OqB], wr=[OSB])
                            else:
                                p.ins("dve", lambda e, Oq=Oq, c=c: e.tensor_tensor(OSUM[:, c, :], OSUM[:, c, :], Oq, ALU.add), rd=[OqB, OSB], wr=[OSB])
                            Uq, UqB = psq.next()
                            for h in range(2):
                                hc = slice(64 * h, 64 * h + 64)
                                mm(p, Uq[:, hc], [(BTOK[:, c, :], SAs[:, hc]), (KTOK[:, c, :], VTOK[:, c, hc])], rd=[BTB, KTB, SAsB, VTB], wr=[UqB])
                            tcol = c * 128 + (127 if d == 0 else 0)
                            for h in range(2):
                                hr = slice(64 * h, 64 * h + 64)
                                hc = slice(64 * h, 64 * h + 64)
                                p.ins("dve", lambda e, hr=hr, hc=hc, Uq=Uq, tcol=tcol, S=S: e.scalar_tensor_tensor(S[hr, :], S[hr, :], A1[hr, tcol:tcol + 1], Uq[hr, hc], ALU.mult, ALU.add),
                                      rd=[SB, A1B, UqB], wr=[SB])
                            p.ins("act", lambda e, S=S, Sb=Sb: e.activation(Sb[:], S[:], AF.Copy), rd=[SB], wr=[SbB])
                    if not sample:
                        for (qi, sidx, order) in chains:
                            (S, SB), (Sb, SbB), (sin_, sinB), (Xs, XsB), (SAs, SAsB) = seqbufs[qi]
                            q, qB = psq.next()
                            mm(p, q[0:64, :], [(S[:], G.ident[:])], rd=[SB, G.identB], wr=[qB])
                            p.ins("dve", lambda e, q=q, sin_=sin_: e.tensor_copy(sin_[:], q[0:64, :]), rd=[qB], wr=[sinB])
                            p.dma("sp", new_st[sidx, L, d, 2 * hp:2 * hp + 2].rearrange("h v k -> v h k"), sin_[:].rearrange("v (h k) -> v h k", h=2), rd=[sinB])
                if RSTOP < 4:
                    continue
                p.dma("sp", lg[:], W["rwkv_lnx_g"][L][hp * 128:(hp + 1) * 128].partition_broadcast(128), wr=[lgB])
                p.dma("sp", lb[:], W["rwkv_lnx_b"][L][hp * 128:(hp + 1) * 128].partition_broadcast(128), wr=[lbB])
                VF = v3(A5, nch)
                VFB = A5B
                p.dma("sp", VF, projTok[tok0:tok0 + Ls, hp * 128:(hp + 1) * 128].rearrange("(c p) f -> p c f", p=128), wr=[VFB])
                n2 = nch * 2
                o3 = OSUM[:, 0:nch, :].rearrange("p c (h v) -> p (c h) v", h=2)
                p.ins("dve", lambda e: e.tensor_reduce(st4[0][:, 0:n2], o3, AX.X, ALU.add), rd=[OSB], wr=[st4[1]])
                p.ins("pool", lambda e: e.tensor_tensor(v3(A7, nch), OSUM[:, 0:nch, :], OSUM[:, 0:nch, :], ALU.mult), rd=[OSB], wr=[A7B])
                p.ins("dve", lambda e: e.tensor_reduce(st4[0][:, 32:32 + n2], A7[:, 0:Ls].rearrange("p (a v) -> p a v", v=64), AX.X, ALU.add), rd=[A7B], wr=[st4[1]])
                sm, sq = st4[0][:, 0:n2], st4[0][:, 32:32 + n2]
                p.ins("dve", lambda e: e.tensor_scalar(sm, sm, 1.0 / 64, None, ALU.mult), rd=[st4[1]], wr=[st4[1]])
                p.ins("dve", lambda e: e.tensor_scalar(sq, sq, 1.0 / 64, GN_EPS, ALU.mult, ALU.add), rd=[st4[1]], wr=[st4[1]])
                p.ins("dve", lambda e: e.tensor_tensor(bs_s[:, 0:n2], sm, sm, ALU.mult), rd=[st4[1]], wr=[bsB])
                p.ins("dve", lambda e: e.tensor_tensor(sq, sq, bs_s[:, 0:n2], ALU.subtract), rd=[st4[1], bsB], wr=[st4[1]])
                p.ins("act", lambda e: e.activation(sq, sq, AF.Sqrt), rd=[st4[1]], wr=[st4[1]])
                p.ins("dve", lambda e: e.reciprocal(sq, sq), rd=[st4[1]], wr=[st4[1]])
                p.ins("dve", lambda e: e.scalar_tensor_tensor(A8[:, 0:Ls], R[:, 0:Ls], pv[:, 42 + hp:43 + hp], KDS[:, 0:Ls], ALU.mult, ALU.mult), rd=[RB, pvB, KDSB], wr=[A8B])
                for c in range(nch):
                    q, qB = psq.next()
                    mm(p, q[:, 0:2], [(A8[:, c * 128:(c + 1) * 128], hind[:])], rd=[A8B, hiB], wr=[qB])
                    p.ins("act", lambda e, q=q, c=c: e.activation(bs_s[:, 2 * c:2 * c + 2], q[:, 0:2], AF.Copy), rd=[qB], wr=[bsB])
                G6 = v3(A6, nch)
                p.dma("sp", G6, projTok[tok0:tok0 + Ls, 768 + hp * 128:768 + (hp + 1) * 128].rearrange("(c p) f -> p c f", p=128), wr=[A6B])
                p.ins("act", lambda e: e.activation(A6[:, 0:Ls], A6[:, 0:Ls], AF.Sigmoid), rd=[A6B], wr=[A6B])
                for c in range(nch):
                    for h in range(2):
                        idx = 2 * c + h
                        hc = slice(64 * h, 64 * h + 64)
                        p.ins("dve", lambda e, c=c, hc=hc, idx=idx: e.tensor_scalar(OSUM[:, c, hc], OSUM[:, c, hc], st4[0][:, idx:idx + 1], st4[0][:, 32 + idx:33 + idx], ALU.subtract, ALU.mult),
                              rd=[OSB, st4[1]], wr=[OSB])
                    p.ins("pool", lambda e, c=c: e.tensor_tensor(OSUM[:, c, :], OSUM[:, c, :], lg[:], ALU.mult), rd=[OSB, lgB], wr=[OSB])
                    p.ins("pool", lambda e, c=c: e.tensor_tensor(OSUM[:, c, :], OSUM[:, c, :], lb[:], ALU.add), rd=[OSB, lbB], wr=[OSB])
                    for h in range(2):
                        idx = 2 * c + h
                        hc = slice(64 * h, 64 * h + 64)
                        p.ins("dve", lambda e, c=c, hc=hc, idx=idx: e.scalar_tensor_tensor(OSUM[:, c, hc], VF[:, c, hc], bs_s[:, idx:idx + 1], OSUM[:, c, hc], ALU.mult, ALU.add),
                              rd=[OSB, VFB, bsB], wr=[OSB])
                    p.ins("pool", lambda e, c=c: e.tensor_tensor(OSUM[:, c, :], OSUM[:, c, :], G6[:, c, :], ALU.mult), rd=[OSB, A6B], wr=[OSB])
                    q, qB = psq.next()
                    mm(p, q, [(OSUM[:, c, :], G.ident[:])], rd=[OSB, G.identB], wr=[qB])
                    p.ins("act", lambda e, q=q, c=c: e.activation(BH[:, c * 128:(c + 1) * 128], q, AF.Copy), rd=[qB], wr=[BHB])
                p.dma("sp", mixT[hp, :, tok0:tok0 + Ls], BH[:, 0:Ls], rd=[BHB])
    p.barrier()
```

```python
import math
import numpy as np
from contextlib import ExitStack
import concourse.bass as bass
import concourse.mybir as mybir
from concourse.bass_utils import run_bass_kernel_spmd

F32 = mybir.dt.float32
BF16 = mybir.dt.bfloat16
AF = mybir.ActivationFunctionType
ALU = mybir.AluOpType
AX = mybir.AxisListType

D = 2048
KT = 16
NL = 4
TP = 1024
TS = 2048
T = 3072
SEGS = [(0, 256), (256, 256), (512, 256), (768, 256), (1024, 2048)]
RW = 768
HW = 512
AW = 768
KVW = 256
INW = 5888
DFF = 5504
FT = 43
ALPHA = (2 * NL) ** 0.25
LN_EPS = 1e-5
GN_EPS = 64e-5
C_R = 0
C_K = 768
C_V = 1536
C_G = 2304
C_HY = 3072
C_Q = 4608
C_AK = 5376
C_AV = 5632
PI = math.pi


class Buf:
    __slots__ = ("name", "w", "r")

    def __init__(self, name=""):
        self.name = name
        self.w = None
        self.r = {}


class Prog:
    NDMA = 6

    def __init__(self, nc, es):
        self.nc = nc
        self.eng = {"pe": nc.tensor, "act": nc.scalar, "dve": nc.vector,
                    "pool": nc.gpsimd, "sp": nc.sync}
        self.sems = {}
        self.semval = {}
        for e in self.eng:
            self.sems[e] = es.enter_context(nc.semaphore("s_" + e))
            self.semval[e] = 0
        self.dq = {}
        for q in ("sp", "act", "pool"):
            ring = []
            for i in range(self.NDMA):
                k = "d_%s_%d" % (q, i)
                self.sems[k] = es.enter_context(nc.semaphore(k))
                self.semval[k] = 0
                ring.append(k)
            self.dq[q] = [ring, 0]
        self.waited = {e: {} for e in self.eng}
        self.nins = 0
        self.uid = 0

    def _wait(self, e, tok):
        if tok is None:
            return
        k, v = tok
        if self.waited[e].get(k, 0) >= v:
            return
        if e == "pe" and k == "pe":
            return
        self.eng[e].wait_ge(self.sems[k], v)
        self.waited[e][k] = v
        self.nins += 1

    def _deps(self, e, rd, wr):
        for b in rd:
            self._wait(e, b.w)
        for b in wr:
            self._wait(e, b.w)
            for k, v in b.r.items():
                self._wait(e, (k, v))

    def _mark(self, tok, rd, wr):
        for b in wr:
            b.w = tok
            b.r = {}
        for b in rd:
            if b.r.get(tok[0], 0) < tok[1]:
                b.r[tok[0]] = tok[1]

    def ins(self, e, fn, rd=(), wr=()):
        self._deps(e, rd, wr)
        i = fn(self.eng[e])
        self.semval[e] += 1
        i.then_inc(self.sems[e], 1)
        tok = (e, self.semval[e])
        self._mark(tok, rd, wr)
        self.nins += 1
        return tok

    def group(self, e, fns, rd=(), wr=()):
        self._deps(e, rd, wr)
        for fn in fns[:-1]:
            fn(self.eng[e])
        i = fns[-1](self.eng[e])
        self.semval[e] += 1
        i.then_inc(self.sems[e], 1)
        tok = (e, self.semval[e])
        self._mark(tok, rd, wr)
        self.nins += len(fns)
        return tok

    def dma(self, q, out, in_, rd=(), wr=(), **kw):
        ring, idx = self.dq[q]
        k = ring[idx % self.NDMA]
        self.dq[q][1] = idx + 1
        if self.semval[k] > 0:
            self._wait(q, (k, self.semval[k]))
        self._deps(q, rd, wr)
        i = self.eng[q].dma_start(out=out, in_=in_, **kw)
        self.semval[k] += 16
        i.then_inc(self.sems[k], 16)
        tok = (k, self.semval[k])
        self._mark(tok, rd, wr)
        self.nins += 1
        return tok

    def barrier(self):
        for e in self.eng:
            for k, v in self.semval.items():
                if v > 0:
                    self._wait(e, (k, v))


class Ctx:
    def __init__(self, nc, p, es):
        self.nc = nc
        self.p = p
        self.es = es
        self.n = 0

    def sb(self, es, shape, dt, name="t"):
        self.n += 1
        t = es.enter_context(self.nc.sbuf_tensor("%s_%d" % (name, self.n), list(shape), dt))
        return t, Buf(name)

    def dram(self, name, shape, dt, kind="Internal"):
        return self.nc.dram_tensor(name, list(shape), dt, kind=kind).ap()


class Ring:
    def __init__(self, items):
        self.items = items
        self.i = 0

    def next(self):
        it = self.items[self.i % len(self.items)]
        self.i += 1
        return it


def mm(p, out_ap, pairs, rd, wr):
    n = len(pairs)
    fns = []
    for i, (a, b) in enumerate(pairs):
        fns.append(lambda e, a=a, b=b, i=i: e.matmul(out_ap, a, b, start=(i == 0), stop=(i == n - 1)))
    return p.group("pe", fns, rd=rd, wr=wr)


class Glob:
    pass


def setup_globals(cx, es, cst):
    nc, p = cx.nc, cx.p
    G = Glob()
    banks = []
    for i in range(8):
        t = es.enter_context(nc.psum_tensor("psb%d" % i, [128, 512], F32))
        banks.append((t, Buf("ps%d" % i)))
    G.banks = banks
    G.ps = Ring(banks[0:6])
    G.psx = Ring(banks[6:8])
    G.ident, G.identB = cx.sb(es, [128, 128], F32, "ident")
    G.identb, G.identbB = cx.sb(es, [128, 128], BF16, "identb")
    p.dma("sp", G.ident[:], cst["ident"], wr=[G.identB])
    p.dma("pool", G.identb[:], cst["ident"], wr=[G.identbB])
    G.ones, G.onesB = cx.sb(es, [128, 128], F32, "ones")
    p.ins("dve", lambda e: e.memset(G.ones[:], 1.0), wr=[G.onesB])
    return G


def vecT(cx, G, es, vec2d, n, dst_ap, dstB, eng="dve"):
    p = cx.p
    st, stB = G.vstage.next()
    p.dma("sp", st[0:n, :], vec2d, wr=[stB])
    ps, pb = G.ps.next()
    mm(p, ps[:, 0:n], [(st[0:n, :], G.ident[0:n, 0:n])], rd=[stB, G.identB], wr=[pb])
    if eng == "dve":
        p.ins("dve", lambda e: e.tensor_copy(dst_ap, ps[:, 0:n]), rd=[pb], wr=[dstB])
    else:
        p.ins("act", lambda e: e.activation(dst_ap, ps[:, 0:n], AF.Copy), rd=[pb], wr=[dstB])


def rows128(vec1d):
    return vec1d.rearrange("(t p) -> t p", p=128)


def phase_x0(cx, G, x_p, x_s, xT):
    p = cx.p
    with ExitStack() as es:
        xin = Ring([cx.sb(es, [128, D], F32, "xin") for _ in range(2)])
        stg = Ring([cx.sb(es, [128, 4, 128], F32, "x0s") for _ in range(3)])
        for tt in range(T // 128):
            xt, xb = xin.next()
            src = x_p[tt * 128:(tt + 1) * 128, :] if tt < 8 else x_s[(tt - 8) * 128:(tt - 7) * 128, :]
            p.dma("sp", xt[:], src, wr=[xb])
            for f4 in range(4):
                ps, pb = G.ps.next()
                for j in range(4):
                    ft = f4 * 4 + j
                    mm(p, ps[:, j * 128:(j + 1) * 128], [(xt[:, ft * 128:(ft + 1) * 128], G.ident[:])],
                       rd=[xb, G.identB], wr=[pb])
                st, sb_ = stg.next()
                if f4 % 2 == 0:
                    p.ins("dve", lambda e: e.tensor_copy(st[:].rearrange("p a b -> p (a b)"), ps[:]), rd=[pb], wr=[sb_])
                else:
                    p.ins("act", lambda e: e.activation(st[:].rearrange("p a b -> p (a b)"), ps[:], AF.Copy), rd=[pb], wr=[sb_])
                p.dma("sp", xT[f4 * 4:(f4 + 1) * 4, :, tt * 128:(tt + 1) * 128].rearrange("f p t -> p f t"),
                      st[:], rd=[sb_])
    p.barrier()


def phase_cond(cx, G, es, c_ctx, c_s):
    p = cx.p
    G.sT, G.sTB = cx.sb(es, [128, KT, 2], BF16, "sT")
    tmp, tmpB = cx.sb(es, [128, KT], F32, "ctmp")
    for c, src in enumerate((c_ctx, c_s)):
        vecT(cx, G, es, src.rearrange("o (t p) -> (o t) p", p=128), KT, tmp[:], tmpB)
        p.ins("act", lambda e: e.activation(G.sT[:, :, c], tmp[:], AF.Silu), rd=[tmpB], wr=[G.sTB])


def phase_mod(cx, G, L, w_mod, b_mod):
    p = cx.p
    with ExitStack() as es:
        wb = Ring([cx.sb(es, [128, KT, 512], BF16, "wmod") for _ in range(2)])
        bT, bTB = cx.sb(es, [128, 96], F32, "bmodT")
        vecT(cx, G, es, rows128(b_mod[L]), 96, bT[:], bTB)
        psM, psMB = G.ps.next()
        for cb in range(24):
            wt, wtb = wb.next()
            p.dma("pool", wt[:], w_mod[L][:, cb * 512:(cb + 1) * 512].rearrange("(kt p) n -> p kt n", p=128), wr=[wtb])
            for oc in range(4):
                j = cb * 4 + oc
                mm(p, psM[:, j * 2:j * 2 + 2],
                   [(wt[:, kt, oc * 128:(oc + 1) * 128], G.sT[:, kt, :]) for kt in range(KT)],
                   rd=[wtb, G.sTB], wr=[psMB])
        psv = psM[:, 0:192].rearrange("p (j c) -> p j c", c=2)
        for c in range(2):
            p.ins("dve", lambda e: e.tensor_tensor(G.modT[:, :, c], psv[:, :, c], bT[:], ALU.add),
                  rd=[psMB, bTB], wr=[G.modTB])
        for j0 in (16, 64):
            p.ins("dve", lambda e: e.tensor_scalar(G.modT[:, j0:j0 + 16, :], G.modT[:, j0:j0 + 16, :], 1.0, None, ALU.add),
                  rd=[G.modTB], wr=[G.modTB])
    p.barrier()


def modulate(cx, G, src, srcB, dst, dstB, ft, j_scale, j_shift, t0=0, tn=T, base=0):
    p = cx.p
    for c, (a, b) in enumerate(((0, TP), (TP, T))):
        lo, hi = max(a, t0), min(b, t0 + tn)
        if lo >= hi:
            continue
        s = src[:, lo - t0:hi - t0]
        d = dst[:, lo - t0:hi - t0]
        sc = G.modT[:, j_scale + ft, c:c + 1]
        sh = G.modT[:, j_shift + ft, c:c + 1]
        if c == 0:
            p.ins("act", lambda e, s=s, d=d, sc=sc, sh=sh: e.activation(d, s, AF.Identity, bias=sh, scale=sc),
                  rd=[srcB, G.modTB], wr=[dstB])
        else:
            p.ins("dve", lambda e, s=s, d=d, sc=sc, sh=sh: e.tensor_scalar(d, s, sc, sh, ALU.mult, ALU.add),
                  rd=[srcB, G.modTB], wr=[dstB])


def phase_h(cx, G, xT, hT, hTB):
    p = cx.p
    with ExitStack() as es:
        xin = Ring([cx.sb(es, [128, T], F32, "hx") for _ in range(2)])
        for ft in range(KT):
            xt, xb = xin.next()
            p.dma("sp", xt[:], xT[ft], wr=[xb])
            modulate(cx, G, xt[:], xb, hT[:, ft, :], hTB, ft, 16, 0)
    p.barrier()


def linear_fm(cx, G, inT, inB, ktn, groups, evac, tblocks, wbufs):
    p = cx.p
    for pieces, tags in groups:
        wt, wb = wbufs.next()
        for ap, c0 in pieces:
            n = ap.shape[1]
            p.dma("pool", wt[:, 0:ktn, c0:c0 + n], ap.rearrange("(kt p) n -> p kt n", p=128), wr=[wb])
        for oc, tag in enumerate(tags):
            for (t0, tn) in tblocks:
                ps, pb = G.ps.next()
                mm(p, ps[:, 0:tn],
                   [(wt[:, kt, oc * 128:(oc + 1) * 128], inT[:, kt, t0:t0 + tn]) for kt in range(ktn)],
                   rd=[wb, inB], wr=[pb])
                evac(tag, t0, tn, ps, pb)


TB512 = [(i * 512, 512) for i in range(6)]


def phase_proj(cx, G, L, hT, hTB, W, projT, projTok, loraT, loraTB, new_k, new_v):
    p = cx.p
    w_in = W["w_in"][L]
    with ExitStack() as es:
        wbufs = Ring([cx.sb(es, [128, KT, 512], BF16, "win") for _ in range(2)])
        stg = Ring([cx.sb(es, [128, 512], F32, "pst") for _ in range(4)])
        cnt = [0]

        def evac_fm(tag, t0, tn, ps, pb):
            cnt[0] += 1
            if isinstance(tag, str):
                li = int(tag[1])
                fn = AF.Tanh if li == 0 else AF.Copy
                p.ins("act", lambda e: e.activation(loraT[:, li, t0:t0 + tn], ps[:, 0:tn], fn), rd=[pb], wr=[loraTB])
                return
            st, sb_ = stg.next()
            if cnt[0] % 2 == 0:
                p.ins("dve", lambda e: e.tensor_copy(st[:, 0:tn], ps[:, 0:tn]), rd=[pb], wr=[sb_])
            else:
                p.ins("act", lambda e: e.activation(st[:, 0:tn], ps[:, 0:tn], AF.Copy), rd=[pb], wr=[sb_])
            p.dma("sp", projT[tag, :, t0:t0 + tn], st[:, 0:tn], rd=[sb_])

        groups = []
        for c0 in list(range(0, 1536, 512)) + list(range(3072, 5632, 512)):
            groups.append(([(w_in[:, c0:c0 + 512], 0)], [c0 // 128 + i for i in range(4)]))
        groups.append(([(W["rwkv_w1"][L, 0], 0), (W["rwkv_w1"][L, 1], 64),
                        (W["rwkv_a1"][L, 0], 128), (W["rwkv_a1"][L, 1], 192)], ["l0", "l1"]))
        linear_fm(cx, G, hT, hTB, KT, groups, evac_fm, TB512, wbufs)

        for gi, c0 in enumerate((C_V, C_V + 512, C_V + 1024, C_AK)):
            wt, wb = wbufs.next()
            p.dma("pool", wt[:], w_in[:, c0:c0 + 512].rearrange("(kt p) n -> p kt n", p=128), wr=[wb])
            for tt in range(T // 128):
                ps, pb = G.ps.next()
                mm(p, ps[:], [(hT[:, kt, tt * 128:(tt + 1) * 128], wt[:, kt, :]) for kt in range(KT)],
                   rd=[wb, hTB], wr=[pb])
                st, sb_ = stg.next()
                if tt % 2 == 0:
                    p.ins("dve", lambda e: e.tensor_copy(st[:], ps[:]), rd=[pb], wr=[sb_])
                else:
                    p.ins("act", lambda e: e.activation(st[:], ps[:], AF.Copy), rd=[pb], wr=[sb_])
                p.dma("sp", projTok[tt * 128:(tt + 1) * 128, gi * 512:(gi + 1) * 512], st[:], rd=[sb_])
                if gi == 3 and tt < 8:
                    s, r0 = tt // 2, (tt % 2) * 128
                    p.dma("sp", new_k[s, L, r0:r0 + 128, :], st[:, 0:256], rd=[sb_])
                    p.dma("sp", new_v[s, L, r0:r0 + 128, :], st[:, 256:512], rd=[sb_])
    p.barrier()


def resid_evac(cx, G, stg, xT, yT, j_gate):
    p = cx.p

    def evac(ot, t0, tn, ps, pb):
        c = 0 if t0 < TP else 1
        (xt, xb), (tm, tb) = stg.next()
        p.dma("sp", xt[:, 0:tn], xT[ot, :, t0:t0 + tn], wr=[xb])
        p.ins("act", lambda e: e.activation(tm[:, 0:tn], ps[:, 0:tn], AF.Copy, scale=G.modT[:, j_gate + ot, c:c + 1]),
              rd=[pb, G.modTB], wr=[tb])
        p.ins("dve", lambda e: e.scalar_tensor_tensor(xt[:, 0:tn], xt[:, 0:tn], ALPHA, tm[:, 0:tn], ALU.mult, ALU.add),
              rd=[xb, tb], wr=[xb])
        p.dma("sp", yT[ot, :, t0:t0 + tn], xt[:, 0:tn], rd=[xb])
    return evac


def phase_out(cx, G, L, W, mixT, xT, yT):
    p = cx.p
    with ExitStack() as es:
        mT, mB = cx.sb(es, [128, KT, T], BF16, "mixres")
        for ft in range(KT):
            p.dma("sp", mT[:, ft, :], mixT[ft], wr=[mB])
        wbufs = Ring([cx.sb(es, [128, KT, 512], BF16, "wout") for _ in range(2)])
        stg = Ring([(cx.sb(es, [128, 512], F32, "ox"), cx.sb(es, [128, 512], F32, "ot")) for _ in range(3)])
        groups = [([(W["w_out"][L][:, c0:c0 + 512], 0)], [c0 // 128 + i for i in range(4)]) for c0 in range(0, D, 512)]
        linear_fm(cx, G, mT, mB, KT, groups, resid_evac(cx, G, stg, xT, yT, 32), TB512, wbufs)
    p.barrier()


def ln_pass(cx, G, L, yT, g_vec, b_vec, xT, h2T, h2TB, final=None):
    p = cx.p
    with ExitStack() as es:
        gT, gB = cx.sb(es, [128, KT], F32, "lng")
        bT, bB = cx.sb(es, [128, KT], F32, "lnb")
        vecT(cx, G, es, rows128(g_vec), KT, gT[:], gB)
        vecT(cx, G, es, rows128(b_vec), KT, bT[:], bB)
        ybuf = Ring([cx.sb(es, [128, KT, 512], F32, "lny") for _ in range(2)])
        sqr = Ring([cx.sb(es, [128, 512], F32, "lnsq") for _ in range(3)])
        mean, meanB = cx.sb(es, [128, 512], F32, "mean")
        rstd, rstdB = cx.sb(es, [128, 512], F32, "rstd")
        msq, msqB = cx.sb(es, [128, 512], F32, "msq")
        zr = Ring([cx.sb(es, [128, 512], F32, "lnz") for _ in range(3)])
        if final is not None:
            osb = Ring([cx.sb(es, [128, D], F32, "lno") for _ in range(2)])
        for (t0, tn) in TB512:
            yt, yb = ybuf.next()
            p.dma("sp", yt[:], yT[:, :, t0:t0 + tn].rearrange("f p t -> p f t"), wr=[yb])
            ps1, pb1 = G.ps.next()
            mm(p, ps1[:], [(G.ones[:], yt[:, ft, :]) for ft in range(KT)], rd=[yb, G.onesB], wr=[pb1])
            ps2, pb2 = G.ps.next()
            toks = []
            sq_list = []
            n = KT
            for ft in range(KT):
                sq, sqb = sqr.next()
                p.ins("act", lambda e, sq=sq, ft=ft: e.activation(sq[:], yt[:, ft, :], AF.Square), rd=[yb], wr=[sqb])
                p.ins("pe", lambda e, sq=sq, ft=ft: e.matmul(ps2[:], G.ones[:], sq[:], start=(ft == 0), stop=(ft == n - 1)),
                      rd=[sqb, G.onesB], wr=[pb2])
            p.ins("act", lambda e: e.activation(mean[:], ps1[:], AF.Copy, scale=1.0 / D), rd=[pb1], wr=[meanB])
            p.ins("dve", lambda e: e.tensor_tensor(msq[:], mean[:], mean[:], ALU.mult), rd=[meanB], wr=[msqB])
            p.ins("dve", lambda e: e.scalar_tensor_tensor(msq[:], ps2[:], 1.0 / D, msq[:], ALU.mult, ALU.subtract),
                  rd=[pb2, msqB], wr=[msqB])
            p.ins("dve", lambda e: e.tensor_scalar(msq[:], msq[:], LN_EPS, None, ALU.add), rd=[msqB], wr=[msqB])
            p.ins("act", lambda e: e.activation(msq[:], msq[:], AF.Sqrt), rd=[msqB], wr=[msqB])
            p.ins("dve", lambda e: e.reciprocal(rstd[:], msq[:]), rd=[msqB], wr=[rstdB])
            for ft in range(KT):
                z, zb = zr.next()
                p.ins("dve", lambda e, z=z, ft=ft: e.tensor_tensor(z[:], yt[:, ft, :], mean[:], ALU.subtract),
                      rd=[yb, meanB], wr=[zb])
                p.ins("dve", lambda e, z=z: e.tensor_tensor(z[:], z[:], rstd[:], ALU.mult), rd=[zb, rstdB], wr=[zb])
                p.ins("act", lambda e, z=z, ft=ft: e.activation(z[:], z[:], AF.Identity, bias=bT[:, ft:ft + 1], scale=gT[:, ft:ft + 1]),
                      rd=[zb, gB, bB], wr=[zb])
                if final is None:
                    p.dma("sp", xT[ft, :, t0:t0 + tn], z[:], rd=[zb])
                    if h2T is not None:
                        modulate(cx, G, z[:], zb, h2T[:, ft, t0:t0 + tn], h2TB, ft, 64, 48, t0=t0, tn=tn)
                else:
                    ps, pb = G.ps.next()
                    for j in range(4):
                        mm(p, ps[:, j * 128:(j + 1) * 128], [(z[:, j * 128:(j + 1) * 128], G.ident[:])], rd=[zb, G.identB], wr=[pb])
                    for j in range(4):
                        pass
                    final_store(cx, G, ps, pb, ft, t0, final, osb)
    p.barrier()


def final_store(cx, G, ps, pb, ft, t0, final, osb):
    p = cx.p
    st = final["stage"]
    p.ins("dve" if ft % 2 == 0 else "act",
          (lambda e: e.tensor_copy(st[0][:, :, ft, :], ps[:].rearrange("p (j f) -> p j f", j=4))) if ft % 2 == 0 else
          (lambda e: e.activation(st[0][:, :, ft, :], ps[:].rearrange("p (j f) -> p j f", j=4), AF.Copy)),
          rd=[pb], wr=[st[1]])
    if ft == KT - 1:
        for j in range(4):
            tt = t0 // 128 + j
            dst = final["y_p"][tt * 128:(tt + 1) * 128, :] if tt < 8 else final["y_s"][(tt - 8) * 128:(tt - 7) * 128, :]
            p.dma("sp", dst, st[0][:, j, :, :].rearrange("p a b -> p (a b)"), rd=[st[1]])


def phase_ffn(cx, G, L, W, h2T, h2TB, gT_d, xT, yT):
    p = cx.p
    w_up = W["ffn_w_up"][L]
    GL = T + 6

    def gcol(t):
        for s, (a, n) in enumerate(SEGS):
            if a <= t < a + n:
                return t + s + 1
        raise ValueError

    with ExitStack() as es:
        wbufs = Ring([cx.sb(es, [128, KT, 512], BF16, "wup") for _ in range(2)])
        cw, cwB = cx.sb(es, [128, 3, 86], F32, "convw")
        for k in range(3):
            vecT(cx, G, es, rows128(W["ffn_conv_w"][L, k]), 86, cw[:, k, :], cwB)
        ubuf = {}
        for nm in ("a", "b"):
            ubuf[nm] = Ring([cx.sb(es, [128, GL], F32, "u" + nm) for _ in range(2 if nm == "a" else 1)])
            for (t_, b_) in ubuf[nm].items:
                p.ins("pool", lambda e, t_=t_: e.memset(t_[:], 0.0), wr=[b_])
        cbuf = {nm: cx.sb(es, [128, GL], F32, "c" + nm) for nm in ("a", "b")}
        gout = Ring([cx.sb(es, [128, T], BF16, "gout") for _ in range(1)])
        cur = {}

        def evac(tag, t0, tn, ps, pb):
            nm, j = tag
            if t0 == 0:
                cur[nm] = ubuf[nm].next()
            u, ub = cur[nm]
            for s, (a, n) in enumerate(SEGS):
                lo, hi = max(a, t0), min(a + n, t0 + tn)
                if lo >= hi:
                    continue
                eng = "act" if nm == "a" else "dve"
                src = ps[:, lo - t0:hi - t0]
                dst = u[:, lo + s + 1:hi + s + 1]
                if eng == "act":
                    p.ins("act", lambda e, src=src, dst=dst: e.activation(dst, src, AF.Copy), rd=[pb], wr=[ub])
                else:
                    p.ins("dve", lambda e, src=src, dst=dst: e.tensor_copy(dst, src), rd=[pb], wr=[ub])
            if t0 + tn == T:
                fi = j if nm == "a" else FT + j
                c, cb = cbuf[nm]
                eng = "dve"
                p.ins(eng, lambda e: e.tensor_scalar(c[:, 1:GL - 1], u[:, 1:GL - 1], cw[:, 1, fi:fi + 1], None, ALU.mult),
                      rd=[ub, cwB], wr=[cb])
                p.ins(eng, lambda e: e.scalar_tensor_tensor(c[:, 1:GL - 1], u[:, 0:GL - 2], cw[:, 0, fi:fi + 1], c[:, 1:GL - 1], ALU.mult, ALU.add),
                      rd=[ub, cwB, cb], wr=[cb])
                p.ins(eng, lambda e: e.scalar_tensor_tensor(c[:, 1:GL - 1], u[:, 2:GL], cw[:, 2, fi:fi + 1], c[:, 1:GL - 1], ALU.mult, ALU.add),
                      rd=[ub, cwB, cb], wr=[cb])
                if nm == "b":
                    ca, cab = cbuf["a"]
                    g, gb = gout.next()
                    p.ins("act", lambda e: e.activation(ca[:, 1:GL - 1], ca[:, 1:GL - 1], AF.Silu), rd=[cab], wr=[cab])
                    for s, (a, n) in enumerate(SEGS):
                        p.ins("dve", lambda e, s=s, a=a, n=n: e.tensor_tensor(g[:, a:a + n], ca[:, a + s + 1:a + n + s + 1],
                                                                             c[:, a + s + 1:a + n + s + 1], ALU.mult),
                              rd=[cab, cb], wr=[gb])
                    p.dma("sp", gT_d[j], g[:], rd=[gb])

        groups = []
        for j0 in range(0, FT, 2):
            nj = min(2, FT - j0)
            pieces = [(w_up[:, j0 * 128:(j0 + nj) * 128], 0), (w_up[:, DFF + j0 * 128:DFF + (j0 + nj) * 128], 256)]
            tags = []
            for i in range(nj):
                tags.append((("a", j0 + i), i * 128))
                tags.append((("b", j0 + i), 256 + i * 128))
            groups.append((pieces, tags))
        for pieces, tags in groups:
            wt, wb = wbufs.next()
            for ap, c0 in pieces:
                n = ap.shape[1]
                p.dma("pool", wt[:, :, c0:c0 + n], ap.rearrange("(kt p) n -> p kt n", p=128), wr=[wb])
            for tag, co in tags:
                for (t0, tn) in TB512:
                    ps, pb = G.ps.next()
                    mm(p, ps[:, 0:tn], [(wt[:, kt, co:co + 128], h2T[:, kt, t0:t0 + tn]) for kt in range(KT)],
                       rd=[wb, h2TB], wr=[pb])
                    evac(tag, t0, tn, ps, pb)
    p.barrier()


def phase_down(cx, G, L, W, gT_d, xT, yT):
    p = cx.p
    w_dn = W["ffn_w_down"][L]
    TC = 1536
    with ExitStack() as es:
        gres, gB = cx.sb(es, [128, FT, TC], BF16, "gres")
        wbufs = Ring([cx.sb(es, [128, FT, 256], BF16, "wdn") for _ in range(3)])
        stg = Ring([(cx.sb(es, [128, 512], F32, "dx"), cx.sb(es, [128, 512], F32, "dt")) for _ in range(2)])
        ev = resid_evac(cx, G, stg, xT, yT, 80)
        for tc in range(T // TC):
            for ft in range(FT):
                p.dma("sp", gres[:, ft, :], gT_d[ft, :, tc * TC:(tc + 1) * TC], wr=[gB])
            for c0 in range(0, D, 256):
                wt, wb = wbufs.next()
                p.dma("pool", wt[:], w_dn[:, c0:c0 + 256].rearrange("(kt p) n -> p kt n", p=128), wr=[wb])
                for oc in range(2):
                    ot = c0 // 128 + oc
                    for tb in range(TC // 512):
                        ps, pb = G.ps.next()
                        mm(p, ps[:], [(wt[:, kt, oc * 128:(oc + 1) * 128], gres[:, kt, tb * 512:(tb + 1) * 512]) for kt in range(FT)],
                           rd=[wb, gB], wr=[pb])
                        ev(ot, tc * TC + tb * 512, 512, ps, pb)
    p.barrier()


W_NAMES = ['w_mod', 'b_mod', 'w_in', 'rwkv_w0', 'rwkv_w1', 'rwkv_w2', 'rwkv_a0', 'rwkv_a1', 'rwkv_a2',
           'rwkv_k_k', 'rwkv_k_a', 'rwkv_r_k', 'rwkv_lnx_g', 'rwkv_lnx_b', 'hy_short_w', 'hy_f_w1', 'hy_f_b1',
           'hy_f_freq1', 'hy_f_w2', 'hy_f_b2', 'hy_f_freq2', 'hy_f_w3', 'hy_bias', 'attn_sink', 'w_out',
           'ln1_g', 'ln1_b', 'ffn_w_up', 'ffn_conv_w', 'ffn_w_down', 'ln2_g', 'ln2_b']
W_SHAPES = {
    'w_mod': (NL, D, 6 * D), 'b_mod': (NL, 6 * D), 'w_in': (NL, D, INW), 'rwkv_w0': (NL, 2, RW),
    'rwkv_w1': (NL, 2, D, 64), 'rwkv_w2': (NL, 2, 64, RW), 'rwkv_a0': (NL, 2, RW), 'rwkv_a1': (NL, 2, D, 64),
    'rwkv_a2': (NL, 2, 64, RW), 'rwkv_k_k': (NL, RW), 'rwkv_k_a': (NL, RW), 'rwkv_r_k': (NL, RW),
    'rwkv_lnx_g': (NL, RW), 'rwkv_lnx_b': (NL, RW), 'hy_short_w': (NL, 3, 1536), 'hy_f_w1': (NL, 33, 64),
    'hy_f_b1': (NL, 64), 'hy_f_freq1': (NL, 64), 'hy_f_w2': (NL, 64, 64), 'hy_f_b2': (NL, 64), 'hy_f_freq2': (NL, 64),
    'hy_f_w3': (NL, 64, 2048), 'hy_bias': (NL, 2 * HW), 'attn_sink': (NL, 12), 'w_out': (NL, D, D),
    'ln1_g': (NL, D), 'ln1_b': (NL, D), 'ffn_w_up': (NL, D, 2 * DFF), 'ffn_conv_w': (NL, 3, 2 * DFF),
    'ffn_w_down': (NL, DFF, D), 'ln2_g': (NL, D), 'ln2_b': (NL, D)}


def make_consts():
    c = {}
    c["ident"] = np.eye(128, dtype=np.float32)
    attn_consts(c)
    hyena_consts(c)
    rwkv_consts(c)
    return c


def build(nlayers=NL, debug=None):
    nc = bass.Bass("TRN2", target_bir_lowering=False)
    ext = lambda n, s: nc.dram_tensor(n, list(s), F32, kind="ExternalInput").ap()
    x_p = ext("x_p", (TP, D))
    x_s = ext("x_s", (TS, D))
    c_s = ext("c_s", (1, D))
    c_ctx = ext("c_ctx", (1, D))
    st_in = ext("st_in", (NL, 2, 12, 64, 64))
    ck_in = ext("ck_in", (NL, 512, KVW))
    cv_in = ext("cv_in", (NL, 512, KVW))
    W = {n: ext(n, W_SHAPES[n]) for n in W_NAMES}
    consts = make_consts()
    cst = {n: ext("cst_" + n, v.shape) for n, v in consts.items()}
    out = lambda n, s: nc.dram_tensor(n, list(s), F32, kind="ExternalOutput").ap()
    y_p = out("y_p", (TP, D))
    y_s = out("y_s", (TS, D))
    new_st = out("new_st", (4, NL, 2, 12, 64, 64))
    new_k = out("new_k", (4, NL, 256, KVW))
    new_v = out("new_v", (4, NL, 256, KVW))
    dk = "ExternalOutput" if debug else "Internal"
    xT = nc.dram_tensor("xT", [KT, 128, T], F32, kind=dk).ap()
    yT = nc.dram_tensor("yT", [KT, 128, T], F32, kind=dk).ap()
    projT = nc.dram_tensor("projT", [46, 128, T], F32, kind=dk).ap()
    projTok = nc.dram_tensor("projTok", [T, 2048], F32, kind=dk).ap()
    mixT = nc.dram_tensor("mixT", [KT, 128, T], BF16, kind="Internal").ap()
    gT_d = nc.dram_tensor("gT_d", [FT, 128, T], BF16, kind="Internal").ap()
    hyT_d = nc.dram_tensor("hyT", [12, 128, T], F32, kind="Internal").ap()
    Kd_d = {Lq: nc.dram_tensor("Kd%d" % Lq, [2, 2, Lq // 128 + 1, 128, HW], F32, kind="Internal").ap() for Lq in (256, 2048)}
    if debug and debug.get("mix_override"):
        mix_ov = ext("mix_ov", (KT, 128, T))
    if debug and debug.get("stop") == "mix":
        mix_dbg = out("mix_dbg", (KT, 128, T))
    with ExitStack() as es:
        p = Prog(nc, es)
        cx = Ctx(nc, p, es)
        G = setup_globals(cx, es, cst)
        G.vstage = Ring([cx.sb(es, [128, 128], F32, "vst") for _ in range(2)])
        G.modT, G.modTB = cx.sb(es, [128, 96, 2], F32, "modT")
        G.hyT = hyT_d
        G.Kd = Kd_d
        phase_cond(cx, G, es, c_ctx, c_s)
        phase_x0(cx, G, x_p, x_s, xT)
        for L in range(nlayers):
            phase_mod(cx, G, L, W["w_mod"], W["b_mod"])
            with ExitStack() as les:
                loraT, loraTB = cx.sb(les, [128, 2, T], BF16, "loraT")
                with ExitStack() as hes:
                    hT, hTB = cx.sb(hes, [128, KT, T], BF16, "hT")
                    phase_h(cx, G, xT, hT, hTB)
                    phase_proj(cx, G, L, hT, hTB, W, projT, projTok, loraT, loraTB, new_k, new_v)
                if debug and debug.get("stop") == "proj":
                    break
                if debug and debug.get("mix_override"):
                    with ExitStack() as mes:
                        mt, mb = cx.sb(mes, [128, T], BF16, "mov")
                        for ft in range(KT):
                            p.dma("pool", mt[:], mix_ov[ft], wr=[mb])
                            p.dma("sp", mixT[ft], mt[:], rd=[mb])
                    p.barrier()
                else:
                    phase_mixers(cx, G, L, W, projT, projTok, loraT, loraTB, mixT, st_in, ck_in, cv_in, new_st, cst,
                                 which=(debug or {}).get("which", ("rwkv", "hyena", "attn")))
                    if debug and debug.get("stop") == "mix":
                        with ExitStack() as mes:
                            mt, mb = cx.sb(mes, [128, T], BF16, "mdb")
                            for ft in range(KT):
                                p.dma("sp", mt[:], mixT[ft], wr=[mb])
                                p.dma("pool", mix_dbg[ft], mt[:], rd=[mb])
                        break
            phase_out(cx, G, L, W, mixT, xT, yT)
            with ExitStack() as fes:
                h2T, h2TB = cx.sb(fes, [128, KT, T], BF16, "h2T")
                ln_pass(cx, G, L, yT, W["ln1_g"][L], W["ln1_b"][L], xT, h2T, h2TB)
                if debug and debug.get("stop") == "ln1":
                    break
                phase_ffn(cx, G, L, W, h2T, h2TB, gT_d, xT, yT)
            phase_down(cx, G, L, W, gT_d, xT, yT)
            if L == nlayers - 1:
                with ExitStack() as oes:
                    st = cx.sb(oes, [128, 4, KT, 128], F32, "fstage")
                    ln_pass(cx, G, L, yT, W["ln2_g"][L], W["ln2_b"][L], xT, None, None,
                            final={"stage": st, "y_p": y_p, "y_s": y_s})
            else:
                ln_pass(cx, G, L, yT, W["ln2_g"][L], W["ln2_b"][L], xT, None, None)
        p.barrier()
        print("instructions:", p.nins)
    return nc, consts


def phase_mixers(cx, G, L, W, projT, projTok, loraT, loraTB, mixT, st_in, ck_in, cv_in, new_st, cst, which=("rwkv", "hyena", "attn")):
    if "attn" in which:
        phase_attn(cx, G, L, W, projT, projTok, mixT, ck_in, cv_in, cst)
    if "hyena" in which:
        phase_hyena(cx, G, L, W, projT, mixT, cst)
    if "rwkv" in which:
        phase_rwkv(cx, G, L, W, projT, projTok, loraT, loraTB, mixT, st_in, new_st, cst)


def make_in_maps(inputs, consts):
    maps = []
    for c in range(8):
        m = {}
        m["x_p"] = np.ascontiguousarray(inputs["x_prompt"][4 * c:4 * c + 4]).reshape(TP, D)
        m["x_s"] = np.ascontiguousarray(inputs["x_sample"][c])
        m["c_s"] = np.ascontiguousarray(inputs["c"][c:c + 1])
        m["c_ctx"] = np.ascontiguousarray(inputs["c_ctx"]).reshape(1, D)
        m["st_in"] = np.ascontiguousarray(inputs["state_rwkv"][c])
        m["ck_in"] = np.ascontiguousarray(inputs["cache_k"][c]).reshape(NL, 512, KVW)
        m["cv_in"] = np.ascontiguousarray(inputs["cache_v"][c]).reshape(NL, 512, KVW)
        for n in W_NAMES:
            m[n] = np.ascontiguousarray(inputs[n]).reshape(W_SHAPES[n])
        for n, v in consts.items():
            m["cst_" + n] = v
        maps.append(m)
    return maps


def kernel(**inputs):
    nc, consts = build()
    maps = make_in_maps(inputs, consts)
    res = run_bass_kernel_spmd(nc, maps, core_ids=list(range(8)))
    r = res.results
    y_prompt = np.concatenate([r[c]["y_p"].reshape(4, 256, D) for c in range(8)], 0)
    y_sample = np.stack([r[c]["y_s"] for c in range(8)], 0)
    new_state = np.concatenate([r[c]["new_st"] for c in range(8)], 0)
    new_k = np.concatenate([r[c]["new_k"].reshape(4, NL, 256, 4, 64) for c in range(8)], 0)
    new_v = np.concatenate([r[c]["new_v"].reshape(4, NL, 256, 4, 64) for c in range(8)], 0)
    return (y_prompt.astype(np.float32), y_sample.astype(np.float32), new_state.astype(np.float32),
            new_k.astype(np.float32), new_v.astype(np.float32))


def attn_consts(c):
    t = np.arange(TS)
    row = (t // 64).astype(np.float32)
    col = (t % 64).astype(np.float32)
    nf = 16
    inv = (10000.0 ** (-np.arange(nf, dtype=np.float32) / nf)).astype(np.float32)
    ang = np.concatenate([row[:, None] * inv, col[:, None] * inv], -1).astype(np.float32)
    cs, sn = np.cos(ang).T, np.sin(ang).T
    c["ropeC"] = np.concatenate([cs, cs], 0).astype(np.float32)
    c["ropeS"] = np.concatenate([-sn, sn], 0).astype(np.float32)
    kk = np.arange(128)[:, None]
    qq = np.arange(128)[None, :]
    mprev = (qq <= kk).astype(np.float32)
    mnext = (kk <= qq).astype(np.float32)
    c["mprev"] = np.tile(mprev[:, None, :], (1, 3, 1)).reshape(128, 384)
    c["mnext"] = np.tile(mnext[:, None, :], (1, 3, 1)).reshape(128, 384)


def phase_attn(cx, G, L, W, projT, projTok, mixT, ck_in, cv_in, cst):
    p = cx.p
    pflat = projT.rearrange("a p t -> (a p) t")
    mflat = mixT.rearrange("a p t -> (a p) t")
    with ExitStack() as es:
        ropeC, rcB = cx.sb(es, [64, TS], F32, "ropeC")
        ropeS, rsB = cx.sb(es, [64, TS], F32, "ropeS")
        p.dma("sp", ropeC[:], cst["ropeC"], wr=[rcB])
        p.dma("sp", ropeS[:], cst["ropeS"], wr=[rsB])
        mprev, mpB = cx.sb(es, [128, 384], BF16, "mprev")
        mnext, mnB = cx.sb(es, [128, 384], BF16, "mnext")
        p.dma("pool", mprev[:], cst["mprev"], wr=[mpB])
        p.dma("pool", mnext[:], cst["mnext"], wr=[mnB])
        esk, eskB = cx.sb(es, [128, 12], F32, "esk")
        p.dma("sp", esk[:], W["attn_sink"][L].partition_broadcast(128), wr=[eskB])
        p.ins("act", lambda e: e.activation(esk[:], esk[:], AF.Exp), rd=[eskB], wr=[eskB])
        qraw, qrB = cx.sb(es, [64, 3, TS], F32, "qraw")
        qsw, qsB = cx.sb(es, [64, 3, TS], F32, "qsw")
        qb, qbB = cx.sb(es, [64, 16, 3, 128], BF16, "qb")
        kraw, krB = cx.sb(es, [64, TS], F32, "kraw")
        ksw, ksB = cx.sb(es, [64, TS], F32, "ksw")
        kb, kbB = cx.sb(es, [64, TS], BF16, "kb")
        kc, kcB = cx.sb(es, [64, 512], BF16, "kc")
        ckraw, ckB = cx.sb(es, [128, 4, 64], F32, "ckraw")
        v, vB = cx.sb(es, [128, 16, 65], BF16, "v")
        vc, vcB = cx.sb(es, [128, 4, 65], BF16, "vc")
        oacc, oB = cx.sb(es, [64, 3, TS], BF16, "oacc")
        pTr = Ring([cx.sb(es, [128, 384], BF16, "pT") for _ in range(3)])
        dbr = Ring([(cx.sb(es, [128, 384], F32, "den"), cx.sb(es, [64, 384], F32, "bc")) for _ in range(2)])
        pending = [None]
        p.ins("pool", lambda e: e.memset(v[:, :, 64:65], 1.0), wr=[vB])
        p.ins("pool", lambda e: e.memset(vc[:, :, 64:65], 1.0), wr=[vcB])

        for (tok0, Ls) in SEGS:
            sample = Ls == TS
            nb = Ls // 128
            for g in range(4):
                for h in range(3):
                    r0 = C_Q + (3 * g + h) * 64
                    p.dma("sp", qraw[:, h, 0:Ls], pflat[r0:r0 + 64, tok0:tok0 + Ls], wr=[qrB])
                    if sample:
                        p.dma("sp", qsw[0:32, h, 0:Ls], pflat[r0 + 32:r0 + 64, tok0:tok0 + Ls], wr=[qsB])
                        p.dma("sp", qsw[32:64, h, 0:Ls], pflat[r0:r0 + 32, tok0:tok0 + Ls], wr=[qsB])
                r0 = C_AK + g * 64
                p.dma("sp", kraw[:, 0:Ls], pflat[r0:r0 + 64, tok0:tok0 + Ls], wr=[krB])
                if sample:
                    p.dma("sp", ksw[0:32, 0:Ls], pflat[r0 + 32:r0 + 64, tok0:tok0 + Ls], wr=[ksB])
                    p.dma("sp", ksw[32:64, 0:Ls], pflat[r0:r0 + 32, tok0:tok0 + Ls], wr=[ksB])
                p.dma("pool", v[:, 0:nb, 0:64],
                      projTok[tok0:tok0 + Ls, 1792 + g * 64:1792 + g * 64 + 64].rearrange("(tt p) d -> p tt d", p=128), wr=[vB])
                for h in range(3):
                    dst = qb[:, 0:nb, h, :]
                    if sample:
                        p.ins("dve", lambda e, h=h: e.tensor_tensor(qraw[:, h, :], qraw[:, h, :], ropeC[:], ALU.mult), rd=[qrB, rcB], wr=[qrB])
                        p.ins("pool", lambda e, h=h: e.tensor_tensor(qsw[:, h, :], qsw[:, h, :], ropeS[:], ALU.mult), rd=[qsB, rsB], wr=[qsB])
                        p.ins("dve", lambda e, h=h, dst=dst: e.tensor_tensor(dst, qraw[:, h, :].rearrange("p (b t) -> p b t", t=128),
                                                                           qsw[:, h, :].rearrange("p (b t) -> p b t", t=128), ALU.add),
                              rd=[qrB, qsB], wr=[qbB])
                    else:
                        p.ins("dve", lambda e, h=h, dst=dst: e.tensor_copy(dst, qraw[:, h, 0:Ls].rearrange("p (b t) -> p b t", t=128)),
                              rd=[qrB], wr=[qbB])
                if sample:
                    p.ins("dve", lambda e: e.tensor_tensor(kraw[:], kraw[:], ropeC[:], ALU.mult), rd=[krB, rcB], wr=[krB])
                    p.ins("pool", lambda e: e.tensor_tensor(ksw[:], ksw[:], ropeS[:], ALU.mult), rd=[ksB, rsB], wr=[ksB])
                    p.ins("dve", lambda e: e.tensor_tensor(kb[:], kraw[:], ksw[:], ALU.add), rd=[krB, ksB], wr=[kbB])
                    p.dma("sp", ckraw[:], ck_in[L][:, g * 64:(g + 1) * 64].rearrange("(tt p) d -> p tt d", p=128), wr=[ckB])
                    ps, pb = G.ps.next()
                    for j in range(4):
                        mm(p, ps[0:64, j * 128:(j + 1) * 128], [(ckraw[:, j, :], G.ident[:])], rd=[ckB, G.identB], wr=[pb])
                    p.ins("act", lambda e: e.activation(kc[:], ps[0:64, :], AF.Copy), rd=[pb], wr=[kcB])
                    p.dma("pool", vc[:, :, 0:64], cv_in[L][:, g * 64:(g + 1) * 64].rearrange("(tt p) d -> p tt d", p=128), wr=[vcB])
                else:
                    p.ins("act", lambda e: e.activation(kb[:, 0:Ls], kraw[:, 0:Ls], AF.Copy), rd=[krB], wr=[kbB])
                for i in range(nb):
                    keys = []
                    if sample:
                        for j in (i - 1, i, i + 1):
                            if 0 <= j < nb:
                                keys.append((kb[:, j * 128:(j + 1) * 128], kbB, v[:, j, :], vB,
                                             (mprev, mpB) if j == i - 1 else ((mnext, mnB) if j == i + 1 else None)))
                        for j in range(4):
                            keys.append((kc[:, j * 128:(j + 1) * 128], kcB, vc[:, j, :], vcB, None))
                    else:
                        for j in range(nb):
                            keys.append((kb[:, j * 128:(j + 1) * 128], kbB, v[:, j, :], vB, None))
                    psO, pOB = G.psx.next()
                    nk = len(keys)

                    def score(ki):
                        kap, kB_ = keys[ki][0], keys[ki][1]
                        psS, pSB = G.ps.next()
                        mm(p, psS[:, 0:384], [(kap, qb[:, i, :, :].rearrange("p h t -> p (h t)"))], rd=[kB_, qbB], wr=[pSB])
                        return psS, pSB
                    nxt = score(0)
                    for ki, (kap, kB_, vap, vB_, msk) in enumerate(keys):
                        psS, pSB = nxt
                        if ki + 1 < nk:
                            nxt = score(ki + 1)
                        if ki == min(3, nk - 1) and pending[0] is not None:
                            pending[0]()
                            pending[0] = None
                        pT, pTB = pTr.next()
                        p.ins("act", lambda e, pT=pT, psS=psS: e.activation(pT[:], psS[:, 0:384], AF.Exp, scale=0.125), rd=[pSB], wr=[pTB])
                        if msk is not None:
                            p.ins("dve", lambda e, pT=pT, msk=msk: e.tensor_tensor(pT[:], pT[:], msk[0][:], ALU.mult), rd=[pTB, msk[1]], wr=[pTB])
                        p.ins("pe", lambda e, vap=vap, pT=pT, ki=ki, psO=psO: e.matmul(psO[0:65, 0:384], vap, pT[:], start=(ki == 0), stop=(ki == nk - 1)),
                              rd=[vB_, pTB], wr=[pOB])

                    def tail(i=i, g=g, psO=psO, pOB=pOB):
                        (den, denB), (bc, bcB) = dbr.next()
                        for h in range(3):
                            hh = 3 * g + h
                            p.ins("dve", lambda e, h=h, hh=hh: e.tensor_scalar(den[64:65, h * 128:(h + 1) * 128], psO[64:65, h * 128:(h + 1) * 128],
                                                                             esk[64:65, hh:hh + 1], None, ALU.add), rd=[pOB, eskB], wr=[denB])
                        p.ins("dve", lambda e: e.reciprocal(den[64:65, :], den[64:65, :]), rd=[denB], wr=[denB])
                        psB, pBB = G.ps.next()
                        mm(p, psB[0:64, 0:384], [(G.ones[64:65, 0:64], den[64:65, :])], rd=[G.onesB, denB], wr=[pBB])
                        p.ins("act", lambda e: e.activation(bc[:], psB[0:64, 0:384], AF.Copy), rd=[pBB], wr=[bcB])
                        p.ins("dve", lambda e: e.tensor_tensor(oacc[:, :, i * 128:(i + 1) * 128], psO[0:64, 0:384].rearrange("p (h t) -> p h t", h=3),
                                                              bc[:].rearrange("p (h t) -> p h t", h=3), ALU.mult), rd=[pOB, bcB], wr=[oB])
                    pending[0] = tail
                if pending[0] is not None:
                    pending[0]()
                    pending[0] = None
                for h in range(3):
                    r0 = 1280 + (3 * g + h) * 64
                    p.dma("sp", mflat[r0:r0 + 64, tok0:tok0 + Ls], oacc[:, h, 0:Ls], rd=[oB])
    p.barrier()


HY_LS = (256, 2048)


def hyena_consts(c):
    for Lq in HY_LS:
        LT = Lq // 128
        FTn = LT + 1
        TBL = min(512, Lq)
        NTB = Lq // TBL
        f = np.arange(FTn * 128, dtype=np.float64)
        t = np.arange(Lq, dtype=np.float64)
        valid = (f <= Lq).astype(np.float64)
        ang = np.pi * np.outer(t, f) / Lq
        Cf = np.cos(ang) * valid[None, :]
        Sf = np.sin(ang) * valid[None, :]
        lay = lambda X: np.ascontiguousarray(X.reshape(LT, 128, FTn, 128).transpose(2, 1, 0, 3)).astype(np.float32)
        c["hyCf%d" % Lq] = lay(Cf)
        c["hySf%d" % Lq] = lay(Sf)
        wf = np.where((f == 0) | (f == Lq), 1.0, 2.0) * valid / (2.0 * Lq)
        Ci = (np.cos(ang) * wf[None, :]).T
        Si = (np.sin(ang) * wf[None, :]).T
        layi = lambda X: np.ascontiguousarray(X.reshape(FTn, 128, NTB, TBL).transpose(2, 1, 0, 3)).astype(np.float32)
        c["hyCi%d" % Lq] = layi(Ci)
        c["hySi%d" % Lq] = layi(Si)
        tt = np.linspace(0.0, 1.0, Lq, dtype=np.float32)[:, None]
        bands = 16
        t_res = np.arange(Lq, dtype=np.float32)[:, None]
        fr = np.linspace(1e-4, bands - 1, bands, dtype=np.float32)[None, :]
        a = (2.0 * math.pi * t_res * fr / Lq).astype(np.float32)
        z = np.concatenate([tt, np.cos(a), np.sin(a)], -1).astype(np.float32)
        c["hyz%d" % Lq] = np.ascontiguousarray(z.T)
        mn, mx = math.log(1e-2) / 1.5, math.log(1e-2) / 0.3
        deltas = np.abs(np.linspace(mn, mx, HW, dtype=np.float32))
        env = np.exp(-tt * deltas[None, :]).astype(np.float32)
        c["hyenv%d" % Lq] = np.ascontiguousarray(env.reshape(LT, 128, HW).transpose(1, 0, 2))


def wrap_pi(p, x, xB, tmp, tmpB, n=2):
    for _ in range(n):
        p.ins("dve", lambda e: e.tensor_scalar(tmp, x, -PI, 2 * PI, ALU.is_lt, ALU.mult), rd=[xB], wr=[tmpB])
        p.ins("dve", lambda e: e.tensor_tensor(x, x, tmp, ALU.add), rd=[xB, tmpB], wr=[xB])
        p.ins("dve", lambda e: e.tensor_scalar(tmp, x, PI, 2 * PI, ALU.is_gt, ALU.mult), rd=[xB], wr=[tmpB])
        p.ins("dve", lambda e: e.tensor_tensor(x, x, tmp, ALU.subtract), rd=[xB, tmpB], wr=[xB])


def spectrum(cx, G, X, XB, Lq, cst, which, tabr, sink):
    p = cx.p
    LT = Lq // 128
    for ft in range(LT + 1):
        for cs in which:
            tb, tbB = tabr.next()
            p.dma("pool", tb[:, 0:LT, :], cst["hy%sf%d" % (cs, Lq)][ft], wr=[tbB], max_dma_last_dim=4096)
            ps, pb = G.ps.next()
            mm(p, ps[:], [(tb[:, tt, :], X[:, tt, :]) for tt in range(LT)], rd=[tbB, XB], wr=[pb])
            sink(ft, cs, ps, pb)


def hyena_filters(cx, G, L, W, Lq, cst, Kd):
    p = cx.p
    LT = Lq // 128
    CH = min(512, Lq)
    with ExitStack() as es:
        zT, zB = cx.sb(es, [33, Lq], F32, "hzT")
        p.dma("sp", zT[:], cst["hyz%d" % Lq], wr=[zB])
        w1, w1B = cx.sb(es, [33, 64], F32, "hw1")
        p.dma("sp", w1[:], W["hy_f_w1"][L], wr=[w1B])
        w2, w2B = cx.sb(es, [64, 64], F32, "hw2")
        p.dma("sp", w2[:], W["hy_f_w2"][L], wr=[w2B])
        w3, w3B = cx.sb(es, [64, 2048], F32, "hw3")
        p.dma("sp", w3[:], W["hy_f_w3"][L], wr=[w3B])
        sc, scB = cx.sb(es, [64, 4], F32, "hsc")
        for i, nm in enumerate(("hy_f_b1", "hy_f_freq1", "hy_f_b2", "hy_f_freq2")):
            p.dma("sp", sc[:, i:i + 1], W[nm][L].rearrange("(p o) -> p o", o=1), wr=[scB])
        hm1, h1B = cx.sb(es, [64, Lq], F32, "hm1")
        hm2, h2B = cx.sb(es, [64, Lq], F32, "hm2")
        tmp, tmpB = cx.sb(es, [64, CH], F32, "hwtmp")
        for (src, sB, wgt, wB, dst, dB, bi) in ((zT, zB, w1, w1B, hm1, h1B, 0), (hm1, h1B, w2, w2B, hm2, h2B, 2)):
            for c0 in range(0, Lq, CH):
                ps, pb = G.ps.next()
                mm(p, ps[0:64, 0:CH], [(wgt[:], src[:, c0:c0 + CH])], rd=[wB, sB], wr=[pb])
                d = dst[:, c0:c0 + CH]
                p.ins("dve", lambda e, d=d, ps=ps, bi=bi: e.tensor_scalar(d, ps[0:64, 0:CH], sc[:, bi:bi + 1], sc[:, bi + 1:bi + 2], ALU.add, ALU.mult),
                      rd=[pb, scB], wr=[dB])
                wrap_pi(p, d, dB, tmp[:], tmpB)
                p.ins("act", lambda e, d=d: e.activation(d, d, AF.Sin), rd=[dB], wr=[dB])
        env, envB = cx.sb(es, [128, LT, HW], F32, "henv")
        p.dma("sp", env[:], cst["hyenv%d" % Lq], wr=[envB])
        PM = [cx.sb(es, [128, LT, HW], BF16, "hPM%d" % i) for i in range(4)]
        hfb = [cx.sb(es, [128, HW], F32, "hfb%d" % i) for i in range(4)]
        for lt in range(LT):
            for cb in range(4):
                ps, pb = G.ps.next()
                mm(p, ps[:], [(hm2[:, lt * 128:(lt + 1) * 128], w3[:, cb * 512:(cb + 1) * 512])], rd=[h2B, w3B], wr=[pb])
                hb, hbB = hfb[cb]
                p.ins("dve", lambda e, hb=hb, ps=ps, lt=lt: e.tensor_tensor(hb[:], ps[:], env[:, lt, :], ALU.mult), rd=[pb, envB], wr=[hbB])
                if lt == 0 and cb % 2 == 1:
                    p.ins("dve", lambda e, hb=hb: e.memset(hb[0:1, :], 0.0), rd=[], wr=[hbB])
            for n in range(2):
                (hc, hcB), (ha, haB) = hfb[2 * n], hfb[2 * n + 1]
                p.ins("pool", lambda e, hc=hc, ha=ha, n=n, lt=lt: e.tensor_tensor(PM[2 * n][0][:, lt, :], hc[:], ha[:], ALU.add),
                      rd=[hcB, haB], wr=[PM[2 * n][1]])
                p.ins("pool", lambda e, hc=hc, ha=ha, n=n, lt=lt: e.tensor_tensor(PM[2 * n + 1][0][:, lt, :], hc[:], ha[:], ALU.subtract),
                      rd=[hcB, haB], wr=[PM[2 * n + 1][1]])
        tabr = Ring([cx.sb(es, [128, LT, 128], BF16, "hftab") for _ in range(3)])
        stg = Ring([cx.sb(es, [128, HW], F32, "hkst") for _ in range(3)])
        for n in range(2):
            for ci, cs in enumerate(("C", "S")):
                X, XB = PM[2 * n + ci]

                def sink(ft, cs_, ps, pb, n=n, ci=ci):
                    st, sB_ = stg.next()
                    p.ins("act", lambda e: e.activation(st[:], ps[:], AF.Copy), rd=[pb], wr=[sB_])
                    p.dma("sp", Kd[n, ci, ft], st[:], rd=[sB_])
                spectrum(cx, G, X, XB, Lq, cst, [cs], tabr, sink)
    p.barrier()


def hyena_short(cx, G, L, W, projT, hyT):
    p = cx.p
    GL = T + 6
    with ExitStack() as es:
        sw, swB = cx.sb(es, [128, 3, 12], F32, "hsw")
        for k in range(3):
            vecT(cx, G, es, rows128(W["hy_short_w"][L, k]), 12, sw[:, k, :], swB)
        ur = Ring([cx.sb(es, [128, GL], F32, "hu") for _ in range(2)])
        cr = Ring([cx.sb(es, [128, GL], F32, "hc") for _ in range(2)])
        for (t_, b_) in ur.items:
            p.ins("pool", lambda e, t_=t_: e.memset(t_[:], 0.0), wr=[b_])
        for i in range(12):
            u, uB = ur.next()
            c, cB = cr.next()
            for s, (a, n) in enumerate(SEGS):
                p.dma("sp", u[:, a + s + 1:a + n + s + 1], projT[24 + i, :, a:a + n], wr=[uB])
            p.ins("dve", lambda e: e.tensor_scalar(c[:, 1:GL - 1], u[:, 1:GL - 1], sw[:, 1, i:i + 1], None, ALU.mult), rd=[uB, swB], wr=[cB])
            p.ins("dve", lambda e: e.scalar_tensor_tensor(c[:, 1:GL - 1], u[:, 0:GL - 2], sw[:, 0, i:i + 1], c[:, 1:GL - 1], ALU.mult, ALU.add),
                  rd=[uB, swB, cB], wr=[cB])
            p.ins("dve", lambda e: e.scalar_tensor_tensor(c[:, 1:GL - 1], u[:, 2:GL], sw[:, 2, i:i + 1], c[:, 1:GL - 1], ALU.mult, ALU.add),
                  rd=[uB, swB, cB], wr=[cB])
            for s, (a, n) in enumerate(SEGS):
                p.dma("sp", hyT[i, :, a:a + n], c[:, a + s + 1:a + n + s + 1], rd=[cB])
    p.barrier()


def hyena_seq(cx, G, L, W, Lq, tok0, hyT, Kd, mixT, cst, biasT, biasB):
    p = cx.p
    LT = Lq // 128
    FTn = LT + 1
    TBL = min(512, Lq)
    NTB = Lq // TBL
    with ExitStack() as es:
        zf = [cx.sb(es, [128, 4, Lq], F32, "hzf%d" % i) for i in range(2)]
        Z, ZB = cx.sb(es, [128, LT, HW], BF16, "hZ")
        Yc, YcB = cx.sb(es, [128, FTn, HW], BF16, "hYc")
        Ys, YsB = cx.sb(es, [128, FTn, HW], BF16, "hYs")
        tabr = Ring([cx.sb(es, [128, LT, 128], BF16, "hftab") for _ in range(4)])
        itab = [cx.sb(es, [128, FTn, TBL], BF16, "hitab%d" % i) for i in range(2)]
        kt = Ring([(cx.sb(es, [128, HW], F32, "hKc"), cx.sb(es, [128, HW], F32, "hKs")) for _ in range(2)])
        tmps = Ring([cx.sb(es, [128, HW], F32, "htmp") for _ in range(4)])
        xr = Ring([cx.sb(es, [128, TBL], F32, "hx") for _ in range(2)])
        ob, obB = cx.sb(es, [128, TBL], BF16, "hob")
        cur, curB = zf[0]
        for ct in range(4):
            p.dma("sp", cur[:, ct, :], hyT[ct, :, tok0:tok0 + Lq], wr=[curB])
        for n in range(2):
            cur, curB = zf[n % 2]
            nxt, nxtB = zf[(n + 1) % 2]
            for tt in range(LT):
                ps, pb = G.ps.next()
                for ct in range(4):
                    mm(p, ps[:, ct * 128:(ct + 1) * 128], [(cur[:, ct, tt * 128:(tt + 1) * 128], G.ident[:])], rd=[curB, G.identB], wr=[pb])
                p.ins("act", lambda e, tt=tt, ps=ps: e.activation(Z[:, tt, :], ps[:], AF.Copy), rd=[pb], wr=[ZB])
            state = {}

            def sink(ft, cs, ps, pb, n=n):
                if cs == "C":
                    state["c"] = (ps, pb)
                    return
                (psc, pbc), (pss, pbs) = state["c"], (ps, pb)
                (Kc, KcB), (Ks, KsB) = kt.next()
                p.dma("sp", Kc[:], Kd[n, 0, ft], wr=[KcB])
                p.dma("sp", Ks[:], Kd[n, 1, ft], wr=[KsB])
                (t1, t1B), (t2, t2B), (t3, t3B), (t4, t4B) = tmps.next(), tmps.next(), tmps.next(), tmps.next()
                p.ins("dve", lambda e: e.tensor_tensor(t1[:], psc[:], Kc[:], ALU.mult), rd=[pbc, KcB], wr=[t1B])
                p.ins("dve", lambda e: e.tensor_tensor(t2[:], pss[:], Ks[:], ALU.mult), rd=[pbs, KsB], wr=[t2B])
                p.ins("dve", lambda e: e.tensor_tensor(t3[:], psc[:], Ks[:], ALU.mult), rd=[pbc, KsB], wr=[t3B])
                p.ins("dve", lambda e: e.tensor_tensor(t4[:], pss[:], Kc[:], ALU.mult), rd=[pbs, KcB], wr=[t4B])
                p.ins("pool", lambda e: e.tensor_tensor(Yc[:, ft, :], t1[:], t2[:], ALU.subtract), rd=[t1B, t2B], wr=[YcB])
                p.ins("pool", lambda e: e.tensor_tensor(Ys[:, ft, :], t3[:], t4[:], ALU.add), rd=[t3B, t4B], wr=[YsB])
            spectrum(cx, G, Z, ZB, Lq, cst, ["C", "S"], tabr, sink)
            for tb in range(NTB):
                (Ci, CiB), (Si, SiB) = itab
                p.dma("pool", Ci[:], cst["hyCi%d" % Lq][tb], wr=[CiB], max_dma_last_dim=4096)
                p.dma("pool", Si[:], cst["hySi%d" % Lq][tb], wr=[SiB], max_dma_last_dim=4096)
                for ct in range(4):
                    ps, pb = G.ps.next()
                    pairs = [(Yc[:, ft, ct * 128:(ct + 1) * 128], Ci[:, ft, :]) for ft in range(FTn)] + \
                            [(Ys[:, ft, ct * 128:(ct + 1) * 128], Si[:, ft, :]) for ft in range(FTn)]
                    mm(p, ps[:, 0:TBL], pairs, rd=[YcB, YsB, CiB, SiB], wr=[pb])
                    x, xB = xr.next()
                    p.dma("sp", x[:], hyT[4 * (n + 1) + ct, :, tok0 + tb * TBL:tok0 + (tb + 1) * TBL], wr=[xB])
                    (t1, t1B) = tmps.next()
                    zsl = cur[:, ct, tb * TBL:(tb + 1) * TBL]
                    p.ins("dve", lambda e, t1=t1, zsl=zsl, ps=ps, ct=ct, n=n: e.scalar_tensor_tensor(t1[:, 0:TBL], zsl, biasT[:, n * 4 + ct:n * 4 + ct + 1],
                                                                                                 ps[:, 0:TBL], ALU.mult, ALU.add),
                          rd=[curB, pb, biasB], wr=[t1B])
                    if n == 0:
                        p.ins("pool", lambda e, t1=t1, x=x, ct=ct, tb=tb: e.tensor_tensor(nxt[:, ct, tb * TBL:(tb + 1) * TBL], t1[:, 0:TBL], x[:], ALU.mult),
                              rd=[t1B, xB], wr=[nxtB])
                    else:
                        p.ins("pool", lambda e, t1=t1, x=x: e.tensor_tensor(ob[:], t1[:, 0:TBL], x[:], ALU.mult), rd=[t1B, xB], wr=[obB])
                        p.dma("sp", mixT[6 + ct, :, tok0 + tb * TBL:tok0 + (tb + 1) * TBL], ob[:], rd=[obB])


def phase_hyena(cx, G, L, W, projT, mixT, cst):
    p = cx.p
    hyT = G.hyT
    hyena_short(cx, G, L, W, projT, hyT)
    with ExitStack() as es:
        biasT, biasB = cx.sb(es, [128, 8], F32, "hbias")
        vecT(cx, G, es, rows128(W["hy_bias"][L]), 8, biasT[:], biasB)
        for Lq in HY_LS:
            hyena_filters(cx, G, L, W, Lq, cst, G.Kd[Lq])
        for (tok0, Ls) in SEGS:
            hyena_seq(cx, G, L, W, Ls, tok0, hyT, G.Kd[Ls], mixT, cst, biasT, biasB)
            p.barrier()


def rwkv_consts(c):
    i = np.arange(128)[:, None]
    t = np.arange(128)[None, :]
    sf, inf_ = (i < t).astype(np.float32), (i <= t).astype(np.float32)
    sb_, inb = (i > t).astype(np.float32), (i >= t).astype(np.float32)
    c["rmaskF"] = np.concatenate([sf, inf_, sf, inf_], 1)
    c["rmaskB"] = np.concatenate([sb_, inb, sb_, inb], 1)
    bo = np.zeros((128, 128), np.float32)
    bo[:64, :64] = 1
    bo[64:, 64:] = 1
    c["blockones"] = bo
    hi = np.zeros((128, 2), np.float32)
    hi[:64, 0] = 1
    hi[64:, 1] = 1
    c["hind"] = hi


def phase_rwkv(cx, G, L, W, projT, projTok, loraT, loraTB, mixT, st_in, new_st, cst):
    p = cx.p
    nc = cx.nc
    EM = math.exp(-0.5)
    banks = G.banks
    ps4 = Ring(banks[0:4])
    quarters = []
    for j in range(4):
        for (bt, bb) in banks[4:8]:
            quarters.append((bt[:, j * 128:(j + 1) * 128], bb))
    psq = Ring(quarters)
    with ExitStack() as es:
        def arr(name, dt=F32, n=TS):
            return cx.sb(es, [128, n], dt, name)
        maskF, mFB = cx.sb(es, [128, 512], F32, "rmF")
        maskB, mBB = cx.sb(es, [128, 512], F32, "rmB")
        p.dma("sp", maskF[:], cst["rmaskF"], wr=[mFB])
        p.dma("sp", maskB[:], cst["rmaskB"], wr=[mBB])
        bones, boB = cx.sb(es, [128, 128], F32, "bones")
        p.dma("sp", bones[:], cst["blockones"], wr=[boB])
        hind, hiB = cx.sb(es, [128, 2], F32, "hind")
        p.dma("sp", hind[:], cst["hind"], wr=[hiB])
        w2a, w2B = cx.sb(es, [128, RW], BF16, "w2a")
        a2a, a2B = cx.sb(es, [128, RW], BF16, "a2a")
        p.dma("pool", w2a[:], W["rwkv_w2"][L].rearrange("d k n -> (d k) n"), wr=[w2B])
        p.dma("pool", a2a[:], W["rwkv_a2"][L].rearrange("d k n -> (d k) n"), wr=[a2B])
        pv, pvB = cx.sb(es, [128, 48], F32, "rpv")
        for d in range(2):
            vecT(cx, G, es, rows128(W["rwkv_w0"][L, d]), 6, pv[:, d * 6:d * 6 + 6], pvB)
            vecT(cx, G, es, rows128(W["rwkv_a0"][L, d]), 6, pv[:, 12 + d * 6:12 + d * 6 + 6], pvB)
        vecT(cx, G, es, rows128(W["rwkv_k_k"][L]), 6, pv[:, 24:30], pvB)
        vecT(cx, G, es, rows128(W["rwkv_k_a"][L]), 6, pv[:, 30:36], pvB)
        vecT(cx, G, es, rows128(W["rwkv_r_k"][L]), 6, pv[:, 42:48], pvB)
        p.ins("dve", lambda e: e.tensor_scalar(pv[:, 36:42], pv[:, 30:36], -1.0, 1.0, ALU.mult, ALU.add), rd=[pvB], wr=[pvB])
        lg, lgB = cx.sb(es, [128, 128], F32, "lnxg")
        lb, lbB = cx.sb(es, [128, 128], F32, "lnxb")
        R, RB = arr("R")
        Kf, KfB = arr("Kf")
        KAP, KAPB = arr("KAP")
        KDS, KDSB = arr("KDS")
        A1, A1B = arr("A1")
        A2, A2B = arr("A2")
        A3, A3B = arr("A3")
        A4, A4B = arr("A4")
        A5, A5B = arr("A5")
        A6, A6B = arr("A6")
        A7, A7B = arr("A7")
        A8, A8B = arr("A8")
        comb, combB = cx.sb(es, [128, 16, 2, 128], BF16, "comb")
        BH, BHB = arr("BH", BF16)
        KH, KHB = arr("KH", BF16)
        BTOK, BTB = cx.sb(es, [128, 16, 128], BF16, "BTOK")
        KTOK, KTB = cx.sb(es, [128, 16, 128], BF16, "KTOK")
        VTOK, VTB = cx.sb(es, [128, 16, 128], BF16, "VTOK")
        OSUM, OSB = cx.sb(es, [128, 16, 128], F32, "OSUM")
        ATs = [[cx.sb(es, [128, 384], BF16, "ATs") for h in range(2)] for c in range(16)]
        TTb = [[cx.sb(es, [128, 128], BF16, "TTb") for h in range(2)] for c in range(16)]
        NB = 5
        chain = [{k: [cx.sb(es, [128, 128], (F32 if k == "TT" else BF16), "ch" + k) for _ in range(2)] for k in ("N", "NT", "TT", "TS")} for _ in range(NB)]
        cols, colsB = cx.sb(es, [128, 4, 16], F32, "cols")
        sin1 = cx.sb(es, [64, 128], F32, "sin")
        sb2 = [cx.sb(es, [128, 64], BF16, "Sb2") for _ in range(4)]
        seqbufs = [(cx.sb(es, [128, 64], F32, "S"), cx.sb(es, [128, 64], BF16, "Sb"), sin1,
                    cx.sb(es, [128, 128], BF16, "Xs"), cx.sb(es, [128, 128], BF16, "SAs")) for _ in range(4)]
        st4 = cx.sb(es, [128, 64], F32, "gst")
        bs_s, bsB = cx.sb(es, [128, 32], F32, "bs")

        def v3(a, n):
            return a[:, 0:n * 128].rearrange("p (c t) -> p c t", t=128)

        RSTOP = 9
        RSUB = 99
        RUNIT_DEFS = [(0, 1024, [(0, 256, 0), (256, 256, 1), (512, 256, 2), (768, 256, 3)], False),
                      (1024, 2048, [(0, 2048, None)], True)]
        for (tok0, Ls, seqs, sample) in RUNIT_DEFS:
            nch = Ls // 128
            CH = min(512, Ls)
            for hp in range(6):
                p.dma("sp", R[:, 0:Ls], projT[hp, :, tok0:tok0 + Ls], wr=[RB])
                p.dma("sp", Kf[:, 0:Ls], projT[6 + hp, :, tok0:tok0 + Ls], wr=[KfB])
                p.dma("sp", v3(A3, nch), projTok[tok0:tok0 + Ls, hp * 128:(hp + 1) * 128].rearrange("(c p) f -> p c f", p=128), wr=[A3B])
                p.ins("act", lambda e: e.activation(VTOK[:, 0:nch, :], v3(A3, nch), AF.Copy), rd=[A3B], wr=[VTB])
                p.ins("dve", lambda e: e.tensor_scalar(A7[:, 0:Ls], Kf[:, 0:Ls], pv[:, 24 + hp:25 + hp], None, ALU.mult), rd=[KfB, pvB], wr=[A7B])
                p.ins("pool", lambda e: e.tensor_tensor(A8[:, 0:Ls], A7[:, 0:Ls], A7[:, 0:Ls], ALU.mult), rd=[A7B], wr=[A8B])
                for c0 in range(0, Ls, CH):
                    ps, pb = ps4.next()
                    mm(p, ps[:, 0:CH], [(bones[:], A8[:, c0:c0 + CH])], rd=[boB, A8B], wr=[pb])
                    p.ins("dve", lambda e, ps=ps, c0=c0: e.tensor_scalar(KAP[:, c0:c0 + CH], ps[:, 0:CH], 1e-30, None, ALU.add), rd=[pb], wr=[KAPB])
                p.ins("act", lambda e: e.activation(KAP[:, 0:Ls], KAP[:, 0:Ls], AF.Sqrt), rd=[KAPB], wr=[KAPB])
                p.ins("dve", lambda e: e.reciprocal(KAP[:, 0:Ls], KAP[:, 0:Ls]), rd=[KAPB], wr=[KAPB])
                p.ins("dve", lambda e: e.tensor_tensor(KAP[:, 0:Ls], KAP[:, 0:Ls], A7[:, 0:Ls], ALU.mult), rd=[KAPB, A7B], wr=[KAPB])
                for d in range(2):
                    if RSTOP < 1:
                        break
                    dr = slice(64 * d, 64 * d + 64)
                    mask, mB = (maskF, mFB) if d == 0 else (maskB, mBB)
                    maskN, mNB = (maskB, mBB) if d == 0 else (maskF, mFB)
                    for c0 in range(0, Ls, CH):
                        ps, pb = ps4.next()
                        mm(p, ps[:, 0:CH], [(w2a[dr, hp * 128:(hp + 1) * 128], loraT[dr, 0, tok0 + c0:tok0 + c0 + CH])], rd=[w2B, loraTB], wr=[pb])
                        p.ins("act", lambda e, ps=ps, c0=c0: e.activation(A1[:, c0:c0 + CH], ps[:, 0:CH], AF.Sigmoid, bias=pv[:, d * 6 + hp:d * 6 + hp + 1]),
                              rd=[pb, pvB], wr=[A1B])
                        ps, pb = ps4.next()
                        mm(p, ps[:, 0:CH], [(a2a[dr, hp * 128:(hp + 1) * 128], loraT[dr, 1, tok0 + c0:tok0 + c0 + CH])], rd=[a2B, loraTB], wr=[pb])
                        p.ins("act", lambda e, ps=ps, c0=c0: e.activation(A4[:, c0:c0 + CH], ps[:, 0:CH], AF.Sigmoid, bias=pv[:, 12 + d * 6 + hp:12 + d * 6 + hp + 1]),
                              rd=[pb, pvB], wr=[A4B])
                    if RSUB <= 1:
                        continue
                    p.ins("dve", lambda e: e.tensor_scalar(A1[:, 0:Ls], A1[:, 0:Ls], -EM, None, ALU.mult), rd=[A1B], wr=[A1B])
                    for (sa_, sn_, _) in seqs:
                        p.ins("dve", lambda e, sa_=sa_, sn_=sn_: e.tensor_tensor_scan(A2[:, sa_:sa_ + sn_], A1[:, sa_:sa_ + sn_], A1[:, sa_:sa_ + sn_], 0.0, ALU.add, ALU.min),
                              rd=[A1B], wr=[A2B])
                    p.ins("pool", lambda e: e.tensor_tensor(A3[:, 0:Ls], A2[:, 0:Ls], A1[:, 0:Ls], ALU.subtract), rd=[A2B, A1B], wr=[A3B])
                    p.ins("dve", lambda e: e.tensor_copy(cols[:, 0, 0:nch], v3(A3, nch)[:, :, 0]), rd=[A3B], wr=[colsB])
                    p.ins("dve", lambda e: e.tensor_copy(cols[:, 1, 0:nch], v3(A2, nch)[:, :, 127]), rd=[A2B], wr=[colsB])
                    p.ins("dve", lambda e: e.tensor_scalar(cols[:, 2:4, 0:nch], cols[:, 0:2, 0:nch], -1.0, None, ALU.mult), rd=[colsB], wr=[colsB])
                    if RSUB <= 2:
                        continue
                    p.ins("dve", lambda e: e.tensor_scalar(A5[:, 0:Ls], A4[:, 0:Ls], pv[:, 30 + hp:31 + hp], pv[:, 36 + hp:37 + hp], ALU.mult, ALU.add),
                          rd=[A4B, pvB], wr=[A5B])
                    p.ins("pool", lambda e: e.tensor_tensor(A5[:, 0:Ls], A5[:, 0:Ls], Kf[:, 0:Ls], ALU.mult), rd=[A5B, KfB], wr=[A5B])
                    if d == 0:
                        p.ins("pool", lambda e: e.tensor_copy(KDS[:, 0:Ls], A5[:, 0:Ls]), rd=[A5B], wr=[KDSB])
                    else:
                        p.ins("pool", lambda e: e.tensor_tensor(KDS[:, 0:Ls], KDS[:, 0:Ls], A5[:, 0:Ls], ALU.add), rd=[A5B, KDSB], wr=[KDSB])
                    p.ins("dve", lambda e: e.tensor_tensor(A6[:, 0:Ls], KAP[:, 0:Ls], A4[:, 0:Ls], ALU.mult), rd=[KAPB, A4B], wr=[A6B])
                    if RSUB <= 3:
                        continue
                    CS, CE, NCS, NCE = 0, 1, 2, 3
                    if d == 0:
                        spec = ((A1, A1B, A2, A2B, 1.0, NCS), (A4, A4B, A3, A3B, 1.0, NCS), (A7, A7B, A2, A2B, -1.0, CE), (A8, A8B, A2, A2B, -1.0, CS))
                    else:
                        spec = ((A1, A1B, A3, A3B, -1.0, CE), (A4, A4B, A2, A2B, -1.0, CE), (A7, A7B, A3, A3B, 1.0, NCS), (A8, A8B, A3, A3B, 1.0, NCE))
                    for (dst, dB, src, sB_, scl, ci) in spec:
                        for c in range(nch):
                            cs_ = slice(c * 128, (c + 1) * 128)
                            p.ins("act", lambda e, dst=dst, src=src, scl=scl, ci=ci, c=c, cs_=cs_: e.activation(dst[:, cs_], src[:, cs_], AF.Exp, bias=cols[:, ci, c:c + 1], scale=scl),
                                  rd=[sB_, colsB], wr=[dB])
                    if RSUB <= 4:
                        continue
                    p.ins("dve", lambda e: e.scalar_tensor_tensor(comb[:, 0:nch, 0, :], v3(KAP, nch), -1.0, v3(A4, nch), ALU.mult, ALU.mult), rd=[KAPB, A4B], wr=[combB])
                    p.ins("pool", lambda e: e.tensor_tensor(comb[:, 0:nch, 1, :], v3(R, nch), v3(A1, nch), ALU.mult), rd=[RB, A1B], wr=[combB])
                    p.ins("dve", lambda e: e.tensor_tensor(BH[:, 0:Ls], A6[:, 0:Ls], A8[:, 0:Ls], ALU.mult), rd=[A6B, A8B], wr=[BHB])
                    p.ins("pool", lambda e: e.tensor_tensor(KH[:, 0:Ls], A5[:, 0:Ls], A8[:, 0:Ls], ALU.mult), rd=[A5B, A8B], wr=[KHB])
                    p.ins("dve", lambda e: e.tensor_tensor(A6[:, 0:Ls], A6[:, 0:Ls], A7[:, 0:Ls], ALU.mult), rd=[A6B, A7B], wr=[A6B])
                    p.ins("pool", lambda e: e.tensor_tensor(A5[:, 0:Ls], A5[:, 0:Ls], A7[:, 0:Ls], ALU.mult), rd=[A5B, A7B], wr=[A5B])
                    if RSUB <= 5:
                        continue
                    for c in range(nch):
                        cs_ = slice(c * 128, (c + 1) * 128)
                        for (src, sB_, dst, dB) in ((A6, A6B, BTOK, BTB), (A5, A5B, KTOK, KTB)):
                            q, qB = psq.next()
                            mm(p, q, [(src[:, cs_], G.ident[:])], rd=[sB_, G.identB], wr=[qB])
                            p.ins("act", lambda e, dst=dst, q=q, c=c: e.activation(dst[:, c, :], q, AF.Copy), rd=[qB], wr=[dB])
                    if RSTOP < 2:
                        continue
                    units = [(c, h) for c in range(nch) for h in range(2)]
                    for b0 in range(0, len(units), NB):
                        batch = units[b0:b0 + NB]
                        for ui, (c, h) in enumerate(batch):
                            hr = slice(64 * h, 64 * h + 64)
                            cs_ = slice(c * 128, (c + 1) * 128)
                            ch_ = chain[ui]
                            psA, pAB = ps4.next()
                            rhs = comb[hr, c, :, :].rearrange("p a t -> p (a t)")
                            mm(p, psA[:, 0:256], [(BH[hr, cs_], rhs)], rd=[BHB, combB], wr=[pAB])
                            mm(p, psA[:, 256:512], [(KH[hr, cs_], rhs)], rd=[KHB, combB], wr=[pAB])
                            (NT0, NT0B) = ch_["NT"][0]
                            p.ins("dve", lambda e, NT0=NT0, psA=psA: e.tensor_tensor(NT0[:], psA[:, 0:128], mask[:, 0:128], ALU.mult), rd=[pAB, mB], wr=[NT0B])
                            at, atB = ATs[c][h]
                            p.ins("dve", lambda e, at=at, psA=psA: e.tensor_tensor(at[:], psA[:, 128:512], mask[:, 128:512], ALU.mult), rd=[pAB, mB], wr=[atB])
                            q, qB = psq.next()
                            mm(p, q, [(comb[hr, c, 0, :], BH[hr, cs_])], rd=[BHB, combB], wr=[qB])
                            (N0, N0B) = ch_["N"][0]
                            p.ins("dve", lambda e, N0=N0, q=q: e.tensor_tensor(N0[:], q, maskN[:, 0:128], ALU.mult), rd=[qB, mNB], wr=[N0B])
                            (TT0, TT0B) = ch_["TT"][0]
                            (TS0, TS0B) = ch_["TS"][0]
                            p.ins("pool", lambda e, TT0=TT0, NT0=NT0: e.tensor_tensor(TT0[:], NT0[:], G.ident[:], ALU.add), rd=[NT0B, G.identB], wr=[TT0B])
                            p.ins("pool", lambda e, TS0=TS0, NT0=NT0: e.tensor_tensor(TS0[:], NT0[:], G.ident[:], ALU.add), rd=[NT0B, G.identB], wr=[TS0B])
                        for j in range(1, 7):
                            a_, b_ = (j - 1) % 2, j % 2
                            for ui, (c, h) in enumerate(batch):
                                ch_ = chain[ui]
                                (Np, NpB), (Nn, NnB) = ch_["N"][a_], ch_["N"][b_]
                                (NTp, NTpB), (NTn, NTnB) = ch_["NT"][a_], ch_["NT"][b_]
                                q, qB = psq.next()
                                mm(p, q, [(NTp[:], Np[:])], rd=[NTpB, NpB], wr=[qB])
                                p.ins("act", lambda e, Nn=Nn, q=q: e.activation(Nn[:], q, AF.Copy), rd=[qB], wr=[NnB])
                                if j < 6:
                                    q2, q2B = psq.next()
                                    mm(p, q2, [(Np[:], NTp[:])], rd=[NTpB, NpB], wr=[q2B])
                                    p.ins("act", lambda e, NTn=NTn, q2=q2: e.activation(NTn[:], q2, AF.Copy), rd=[q2B], wr=[NTnB])
                            for ui, (c, h) in enumerate(batch):
                                ch_ = chain[ui]
                                (Nn, NnB) = ch_["N"][b_]
                                (TTp, TTpB), (TTn, TTnB) = ch_["TT"][a_], ch_["TT"][b_]
                                (TSp, TSpB), (TSn, TSnB) = ch_["TS"][a_], ch_["TS"][b_]
                                q, qB = psq.next()
                                mm(p, q, [(Nn[:], TSp[:])], rd=[NnB, TSpB], wr=[qB])
                                if j < 6:
                                    p.ins("dve", lambda e, TTn=TTn, q=q, TTp=TTp: e.tensor_tensor(TTn[:], q, TTp[:], ALU.add), rd=[qB, TTpB], wr=[TTnB])
                                    p.ins("pool", lambda e, TSn=TSn, TTn=TTn: e.tensor_copy(TSn[:], TTn[:]), rd=[TTnB], wr=[TSnB])
                                else:
                                    tb_, tbB = TTb[c][h]
                                    p.ins("dve", lambda e, tb_=tb_, q=q, TTp=TTp: e.tensor_tensor(tb_[:], q, TTp[:], ALU.add), rd=[qB, TTpB], wr=[tbB])
                    if RSTOP < 3:
                        continue
                    chains = []
                    for qi, (sa_, sn_, sidx) in enumerate(seqs):
                        (S, SB), (Sb, SbB), (sin_, sinB), (Xs, XsB), (SAs, SAsB) = seqbufs[qi]
                        if sample:
                            p.dma("sp", sin_[:].rearrange("v (h k) -> v h k", h=2), st_in[L, d, 2 * hp:2 * hp + 2].rearrange("h v k -> v h k"), wr=[sinB])
                            q, qB = psq.next()
                            mm(p, q[:, 0:64], [(sin_[:], G.ident[0:64, 0:64])], rd=[sinB, G.identB], wr=[qB])
                            p.ins("dve", lambda e, q=q, S=S: e.tensor_copy(S[:], q[:, 0:64]), rd=[qB], wr=[SB])
                        else:
                            p.ins("dve", lambda e, S=S: e.memset(S[:], 0.0), wr=[SB])
                        p.ins("act", lambda e, S=S, Sb=Sb: e.activation(Sb[:], S[:], AF.Copy), rd=[SB], wr=[SbB])
                        c0_, cn_ = sa_ // 128, sn_ // 128
                        order = list(range(c0_, c0_ + cn_)) if d == 0 else list(range(c0_ + cn_ - 1, c0_ - 1, -1))
                        chains.append((qi, sidx, order))
                    for step in range(max(len(o_) for (_, _, o_) in chains)):
                        for (qi, sidx, order) in chains:
                            if step >= len(order):
                                continue
                            c = order[step]
                            (S, SB), (Sb0, Sb0B), (sin_, sinB), (Xs, XsB), (SAs, SAsB) = seqbufs[qi]
                            (Sb1, Sb1B) = sb2[qi]
                            (Sb, SbB), (Sbn, SbnB) = ((Sb0, Sb0B), (Sb1, Sb1B)) if step % 2 == 0 else ((Sb1, Sb1B), (Sb0, Sb0B))
                            Xq, XqB = psq.next()
                            for h in range(2):
                                hr = slice(64 * h, 64 * h + 64)
                                hc = slice(64 * h, 64 * h + 64)
                                at, atB = ATs[c][h]
                                mm(p, Xq[:, hc], [(comb[hr, c, 0, :], Sb[hr, :]), (at[:, 128:256], VTOK[:, c, hc])], rd=[combB, SbB, atB, VTB], wr=[XqB])
                            p.ins("act", lambda e, Xq=Xq, Xs=Xs: e.activation(Xs[:], Xq, AF.Copy), rd=[XqB], wr=[XsB])
                            Sq, SqB = psq.next()
                            for h in range(2):
                                hc = slice(64 * h, 64 * h + 64)
                                tb_, tbB = TTb[c][h]
                                mm(p, Sq[:, hc], [(tb_[:], Xs[:, hc])], rd=[tbB, XsB], wr=[SqB])
                            p.ins("dve", lambda e, Sq=Sq, SAs=SAs: e.tensor_copy(SAs[:], Sq), rd=[SqB], wr=[SAsB])
                            Uq, UqB = psq.next()
                            for h in range(2):
                                hc = slice(64 * h, 64 * h + 64)
                                mm(p, Uq[:, hc], [(BTOK[:, c, :], SAs[:, hc]), (KTOK[:, c, :], VTOK[:, c, hc])], rd=[BTB, KTB, SAsB, VTB], wr=[UqB])
                            tcol = c * 128 + (127 if d == 0 else 0)
                            for h in range(2):
                                hr = slice(64 * h, 64 * h + 64)
                                hc = slice(64 * h, 64 * h + 64)
                                p.ins("dve", lambda e, hr=hr, hc=hc, Uq=Uq, tcol=tcol, S=S, Sbn=Sbn: e.scalar_tensor_tensor(Sbn[hr, :], S[hr, :], A1[hr, tcol:tcol + 1], Uq[hr, hc], ALU.mult, ALU.add),
                                      rd=[SB, A1B, UqB], wr=[SbnB])
                            Oq, OqB = psq.next()
                            for h in range(2):
                                hr = slice(64 * h, 64 * h + 64)
                                hc = slice(64 * h, 64 * h + 64)
                                at, atB = ATs[c][h]
                                mm(p, Oq[:, hc], [(comb[hr, c, 1, :], Sb[hr, :]), (at[:, 0:128], SAs[:, hc]), (at[:, 256:384], VTOK[:, c, hc])],
                                   rd=[combB, SbB, atB, SAsB, VTB], wr=[OqB])
                            if d == 0:
                                p.ins("act", lambda e, Oq=Oq, c=c: e.activation(OSUM[:, c, :], Oq, AF.Copy), rd=[OqB], wr=[OSB])
                            else:
                                p.ins("dve", lambda e, Oq=Oq, c=c: e.tensor_tensor(OSUM[:, c, :], OSUM[:, c, :], Oq, ALU.add), rd=[OqB, OSB], wr=[OSB])
                            for h in range(2):
                                hr = slice(64 * h, 64 * h + 64)
                                hc = slice(64 * h, 64 * h + 64)
                                p.ins("dve", lambda e, hr=hr, hc=hc, Uq=Uq, tcol=tcol, S=S: e.scalar_tensor_tensor(S[hr, :], S[hr, :], A1[hr, tcol:tcol + 1], Uq[hr, hc], ALU.mult, ALU.add),
                                      rd=[SB, A1B, UqB], wr=[SB])
                    if not sample:
                        for (qi, sidx, order) in chains:
                            (S, SB), (Sb, SbB), (sin_, sinB), (Xs, XsB), (SAs, SAsB) = seqbufs[qi]
                            q, qB = psq.next()
                            mm(p, q[0:64, :], [(S[:], G.ident[:])], rd=[SB, G.identB], wr=[qB])
                            p.ins("dve", lambda e, q=q, sin_=sin_: e.tensor_copy(sin_[:], q[0:64, :]), rd=[qB], wr=[sinB])
                            p.dma("sp", new_st[sidx, L, d, 2 * hp:2 * hp + 2].rearrange("h v k -> v h k"), sin_[:].rearrange("v (h k) -> v h k", h=2), rd=[sinB])
                if RSTOP < 4:
                    continue
                p.dma("sp", lg[:], W["rwkv_lnx_g"][L][hp * 128:(hp + 1) * 128].partition_broadcast(128), wr=[lgB])
                p.dma("sp", lb[:], W["rwkv_lnx_b"][L][hp * 128:(hp + 1) * 128].partition_broadcast(128), wr=[lbB])
                VF = v3(A5, nch)
                VFB = A5B
                p.dma("sp", VF, projTok[tok0:tok0 + Ls, hp * 128:(hp + 1) * 128].rearrange("(c p) f -> p c f", p=128), wr=[VFB])
                n2 = nch * 2
                o3 = OSUM[:, 0:nch, :].rearrange("p c (h v) -> p (c h) v", h=2)
                p.ins("dve", lambda e: e.tensor_reduce(st4[0][:, 0:n2], o3, AX.X, ALU.add), rd=[OSB], wr=[st4[1]])
                p.ins("pool", lambda e: e.tensor_tensor(v3(A7, nch), OSUM[:, 0:nch, :], OSUM[:, 0:nch, :], ALU.mult), rd=[OSB], wr=[A7B])
                p.ins("dve", lambda e: e.tensor_reduce(st4[0][:, 32:32 + n2], A7[:, 0:Ls].rearrange("p (a v) -> p a v", v=64), AX.X, ALU.add), rd=[A7B], wr=[st4[1]])
                sm, sq = st4[0][:, 0:n2], st4[0][:, 32:32 + n2]
                p.ins("dve", lambda e: e.tensor_scalar(sm, sm, 1.0 / 64, None, ALU.mult), rd=[st4[1]], wr=[st4[1]])
                p.ins("dve", lambda e: e.tensor_scalar(sq, sq, 1.0 / 64, GN_EPS, ALU.mult, ALU.add), rd=[st4[1]], wr=[st4[1]])
                p.ins("dve", lambda e: e.tensor_tensor(bs_s[:, 0:n2], sm, sm, ALU.mult), rd=[st4[1]], wr=[bsB])
                p.ins("dve", lambda e: e.tensor_tensor(sq, sq, bs_s[:, 0:n2], ALU.subtract), rd=[st4[1], bsB], wr=[st4[1]])
                p.ins("act", lambda e: e.activation(sq, sq, AF.Sqrt), rd=[st4[1]], wr=[st4[1]])
                p.ins("dve", lambda e: e.reciprocal(sq, sq), rd=[st4[1]], wr=[st4[1]])
                p.ins("dve", lambda e: e.scalar_tensor_tensor(A8[:, 0:Ls], R[:, 0:Ls], pv[:, 42 + hp:43 + hp], KDS[:, 0:Ls], ALU.mult, ALU.mult), rd=[RB, pvB, KDSB], wr=[A8B])
                for c in range(nch):
                    q, qB = psq.next()
                    mm(p, q[:, 0:2], [(A8[:, c * 128:(c + 1) * 128], hind[:])], rd=[A8B, hiB], wr=[qB])
                    p.ins("act", lambda e, q=q, c=c: e.activation(bs_s[:, 2 * c:2 * c + 2], q[:, 0:2], AF.Copy), rd=[qB], wr=[bsB])
                G6 = v3(A6, nch)
                p.dma("sp", G6, projTok[tok0:tok0 + Ls, 768 + hp * 128:768 + (hp + 1) * 128].rearrange("(c p) f -> p c f", p=128), wr=[A6B])
                p.ins("act", lambda e: e.activation(A6[:, 0:Ls], A6[:, 0:Ls], AF.Sigmoid), rd=[A6B], wr=[A6B])
                for c in range(nch):
                    for h in range(2):
                        idx = 2 * c + h
                        hc = slice(64 * h, 64 * h + 64)
                        p.ins("dve", lambda e, c=c, hc=hc, idx=idx: e.tensor_scalar(OSUM[:, c, hc], OSUM[:, c, hc], st4[0][:, idx:idx + 1], st4[0][:, 32 + idx:33 + idx], ALU.subtract, ALU.mult),
                              rd=[OSB, st4[1]], wr=[OSB])
                    p.ins("pool", lambda e, c=c: e.tensor_tensor(OSUM[:, c, :], OSUM[:, c, :], lg[:], ALU.mult), rd=[OSB, lgB], wr=[OSB])
                    p.ins("pool", lambda e, c=c: e.tensor_tensor(OSUM[:, c, :], OSUM[:, c, :], lb[:], ALU.add), rd=[OSB, lbB], wr=[OSB])
                    for h in range(2):
                        idx = 2 * c + h
                        hc = slice(64 * h, 64 * h + 64)
                        p.ins("dve", lambda e, c=c, hc=hc, idx=idx: e.scalar_tensor_tensor(OSUM[:, c, hc], VF[:, c, hc], bs_s[:, idx:idx + 1], OSUM[:, c, hc], ALU.mult, ALU.add),
                              rd=[OSB, VFB, bsB], wr=[OSB])
                    p.ins("pool", lambda e, c=c: e.tensor_tensor(OSUM[:, c, :], OSUM[:, c, :], G6[:, c, :], ALU.mult), rd=[OSB, A6B], wr=[OSB])
                    q, qB = psq.next()
                    mm(p, q, [(OSUM[:, c, :], G.ident[:])], rd=[OSB, G.identB], wr=[qB])
                    p.ins("act", lambda e, q=q, c=c: e.activation(BH[:, c * 128:(c + 1) * 128], q, AF.Copy), rd=[qB], wr=[BHB])
                p.dma("sp", mixT[hp, :, tok0:tok0 + Ls], BH[:, 0:Ls], rd=[BHB])
    p.barrier()
```

```python
import math
import numpy as np
from contextlib import ExitStack
import concourse.bass as bass
import concourse.mybir as mybir
from concourse.bass_utils import run_bass_kernel_spmd

F32 = mybir.dt.float32
BF16 = mybir.dt.bfloat16
AF = mybir.ActivationFunctionType
ALU = mybir.AluOpType
AX = mybir.AxisListType

D = 2048
KT = 16
NL = 4
TP = 1024
TS = 2048
T = 3072
SEGS = [(0, 256), (256, 256), (512, 256), (768, 256), (1024, 2048)]
RW = 768
HW = 512
AW = 768
KVW = 256
INW = 5888
DFF = 5504
FT = 43
ALPHA = (2 * NL) ** 0.25
LN_EPS = 1e-5
GN_EPS = 64e-5
C_R = 0
C_K = 768
C_V = 1536
C_G = 2304
C_HY = 3072
C_Q = 4608
C_AK = 5376
C_AV = 5632
PI = math.pi


class Buf:
    __slots__ = ("name", "w", "r")

    def __init__(self, name=""):
        self.name = name
        self.w = None
        self.r = {}


class Prog:
    NDMA = 6

    def __init__(self, nc, es):
        self.nc = nc
        self.eng = {"pe": nc.tensor, "act": nc.scalar, "dve": nc.vector,
                    "pool": nc.gpsimd, "sp": nc.sync}
        self.sems = {}
        self.semval = {}
        for e in self.eng:
            self.sems[e] = es.enter_context(nc.semaphore("s_" + e))
            self.semval[e] = 0
        self.dq = {}
        for q in ("sp", "act", "pool"):
            ring = []
            for i in range(self.NDMA):
                k = "d_%s_%d" % (q, i)
                self.sems[k] = es.enter_context(nc.semaphore(k))
                self.semval[k] = 0
                ring.append(k)
            self.dq[q] = [ring, 0]
        self.waited = {e: {} for e in self.eng}
        self.nins = 0
        self.uid = 0

    def _wait(self, e, tok):
        if tok is None:
            return
        k, v = tok
        if self.waited[e].get(k, 0) >= v:
            return
        if e == "pe" and k == "pe":
            return
        self.eng[e].wait_ge(self.sems[k], v)
        self.waited[e][k] = v
        self.nins += 1

    def _deps(self, e, rd, wr):
        for b in rd:
            self._wait(e, b.w)
        for b in wr:
            self._wait(e, b.w)
            for k, v in b.r.items():
                self._wait(e, (k, v))

    def _mark(self, tok, rd, wr):
        for b in wr:
            b.w = tok
            b.r = {}
        for b in rd:
            if b.r.get(tok[0], 0) < tok[1]:
                b.r[tok[0]] = tok[1]

    def ins(self, e, fn, rd=(), wr=()):
        self._deps(e, rd, wr)
        i = fn(self.eng[e])
        self.semval[e] += 1
        i.then_inc(self.sems[e], 1)
        tok = (e, self.semval[e])
        self._mark(tok, rd, wr)
        self.nins += 1
        return tok

    def group(self, e, fns, rd=(), wr=()):
        self._deps(e, rd, wr)
        for fn in fns[:-1]:
            fn(self.eng[e])
        i = fns[-1](self.eng[e])
        self.semval[e] += 1
        i.then_inc(self.sems[e], 1)
        tok = (e, self.semval[e])
        self._mark(tok, rd, wr)
        self.nins += len(fns)
        return tok

    def dma(self, q, out, in_, rd=(), wr=(), **kw):
        ring, idx = self.dq[q]
        k = ring[idx % self.NDMA]
        self.dq[q][1] = idx + 1
        if self.semval[k] > 0:
            self._wait(q, (k, self.semval[k]))
        self._deps(q, rd, wr)
        i = self.eng[q].dma_start(out=out, in_=in_, **kw)
        self.semval[k] += 16
        i.then_inc(self.sems[k], 16)
        tok = (k, self.semval[k])
        self._mark(tok, rd, wr)
        self.nins += 1
        return tok

    def barrier(self):
        for e in self.eng:
            for k, v in self.semval.items():
                if v > 0:
                    self._wait(e, (k, v))


class Ctx:
    def __init__(self, nc, p, es):
        self.nc = nc
        self.p = p
        self.es = es
        self.n = 0

    def sb(self, es, shape, dt, name="t"):
        self.n += 1
        t = es.enter_context(self.nc.sbuf_tensor("%s_%d" % (name, self.n), list(shape), dt))
        return t, Buf(name)

    def dram(self, name, shape, dt, kind="Internal"):
        return self.nc.dram_tensor(name, list(shape), dt, kind=kind).ap()


class Ring:
    def __init__(self, items):
        self.items = items
        self.i = 0

    def next(self):
        it = self.items[self.i % len(self.items)]
        self.i += 1
        return it


def mm(p, out_ap, pairs, rd, wr):
    n = len(pairs)
    fns = []
    for i, (a, b) in enumerate(pairs):
        fns.append(lambda e, a=a, b=b, i=i: e.matmul(out_ap, a, b, start=(i == 0), stop=(i == n - 1)))
    return p.group("pe", fns, rd=rd, wr=wr)


class Glob:
    pass


def setup_globals(cx, es, cst):
    nc, p = cx.nc, cx.p
    G = Glob()
    banks = []
    for i in range(8):
        t = es.enter_context(nc.psum_tensor("psb%d" % i, [128, 512], F32))
        banks.append((t, Buf("ps%d" % i)))
    G.banks = banks
    G.ps = Ring(banks[0:6])
    G.psx = Ring(banks[6:8])
    G.ident, G.identB = cx.sb(es, [128, 128], F32, "ident")
    G.identb, G.identbB = cx.sb(es, [128, 128], BF16, "identb")
    p.dma("sp", G.ident[:], cst["ident"], wr=[G.identB])
    p.dma("pool", G.identb[:], cst["ident"], wr=[G.identbB])
    G.ones, G.onesB = cx.sb(es, [128, 128], F32, "ones")
    p.ins("dve", lambda e: e.memset(G.ones[:], 1.0), wr=[G.onesB])
    return G


def vecT(cx, G, es, vec2d, n, dst_ap, dstB, eng="dve"):
    p = cx.p
    st, stB = G.vstage.next()
    p.dma("sp", st[0:n, :], vec2d, wr=[stB])
    ps, pb = G.ps.next()
    mm(p, ps[:, 0:n], [(st[0:n, :], G.ident[0:n, 0:n])], rd=[stB, G.identB], wr=[pb])
    if eng == "dve":
        p.ins("dve", lambda e: e.tensor_copy(dst_ap, ps[:, 0:n]), rd=[pb], wr=[dstB])
    else:
        p.ins("act", lambda e: e.activation(dst_ap, ps[:, 0:n], AF.Copy), rd=[pb], wr=[dstB])


def rows128(vec1d):
    return vec1d.rearrange("(t p) -> t p", p=128)


def phase_x0(cx, G, x_p, x_s, xT):
    p = cx.p
    with ExitStack() as es:
        xin = Ring([cx.sb(es, [128, D], F32, "xin") for _ in range(2)])
        stg = Ring([cx.sb(es, [128, 4, 128], F32, "x0s") for _ in range(3)])
        for tt in range(T // 128):
            xt, xb = xin.next()
            src = x_p[tt * 128:(tt + 1) * 128, :] if tt < 8 else x_s[(tt - 8) * 128:(tt - 7) * 128, :]
            p.dma("sp", xt[:], src, wr=[xb])
            for f4 in range(4):
                ps, pb = G.ps.next()
                for j in range(4):
                    ft = f4 * 4 + j
                    mm(p, ps[:, j * 128:(j + 1) * 128], [(xt[:, ft * 128:(ft + 1) * 128], G.ident[:])],
                       rd=[xb, G.identB], wr=[pb])
                st, sb_ = stg.next()
                if f4 % 2 == 0:
                    p.ins("dve", lambda e: e.tensor_copy(st[:].rearrange("p a b -> p (a b)"), ps[:]), rd=[pb], wr=[sb_])
                else:
                    p.ins("act", lambda e: e.activation(st[:].rearrange("p a b -> p (a b)"), ps[:], AF.Copy), rd=[pb], wr=[sb_])
                p.dma("sp", xT[f4 * 4:(f4 + 1) * 4, :, tt * 128:(tt + 1) * 128].rearrange("f p t -> p f t"),
                      st[:], rd=[sb_])
    p.barrier()


def phase_cond(cx, G, es, c_ctx, c_s):
    p = cx.p
    G.sT, G.sTB = cx.sb(es, [128, KT, 2], BF16, "sT")
    tmp, tmpB = cx.sb(es, [128, KT], F32, "ctmp")
    for c, src in enumerate((c_ctx, c_s)):
        vecT(cx, G, es, src.rearrange("o (t p) -> (o t) p", p=128), KT, tmp[:], tmpB)
        p.ins("act", lambda e: e.activation(G.sT[:, :, c], tmp[:], AF.Silu), rd=[tmpB], wr=[G.sTB])


def phase_mod(cx, G, L, w_mod, b_mod):
    p = cx.p
    with ExitStack() as es:
        wb = Ring([cx.sb(es, [128, KT, 512], BF16, "wmod") for _ in range(2)])
        bT, bTB = cx.sb(es, [128, 96], F32, "bmodT")
        vecT(cx, G, es, rows128(b_mod[L]), 96, bT[:], bTB)
        psM, psMB = G.ps.next()
        for cb in range(24):
            wt, wtb = wb.next()
            p.dma("pool", wt[:], w_mod[L][:, cb * 512:(cb + 1) * 512].rearrange("(kt p) n -> p kt n", p=128), wr=[wtb])
            for oc in range(4):
                j = cb * 4 + oc
                mm(p, psM[:, j * 2:j * 2 + 2],
                   [(wt[:, kt, oc * 128:(oc + 1) * 128], G.sT[:, kt, :]) for kt in range(KT)],
                   rd=[wtb, G.sTB], wr=[psMB])
        psv = psM[:, 0:192].rearrange("p (j c) -> p j c", c=2)
        for c in range(2):
            p.ins("dve", lambda e: e.tensor_tensor(G.modT[:, :, c], psv[:, :, c], bT[:], ALU.add),
                  rd=[psMB, bTB], wr=[G.modTB])
        for j0 in (16, 64):
            p.ins("dve", lambda e: e.tensor_scalar(G.modT[:, j0:j0 + 16, :], G.modT[:, j0:j0 + 16, :], 1.0, None, ALU.add),
                  rd=[G.modTB], wr=[G.modTB])
    p.barrier()


def modulate(cx, G, src, srcB, dst, dstB, ft, j_scale, j_shift, t0=0, tn=T, base=0):
    p = cx.p
    for c, (a, b) in enumerate(((0, TP), (TP, T))):
        lo, hi = max(a, t0), min(b, t0 + tn)
        if lo >= hi:
            continue
        s = src[:, lo - t0:hi - t0]
        d = dst[:, lo - t0:hi - t0]
        sc = G.modT[:, j_scale + ft, c:c + 1]
        sh = G.modT[:, j_shift + ft, c:c + 1]
        if c == 0:
            p.ins("act", lambda e, s=s, d=d, sc=sc, sh=sh: e.activation(d, s, AF.Identity, bias=sh, scale=sc),
                  rd=[srcB, G.modTB], wr=[dstB])
        else:
            p.ins("dve", lambda e, s=s, d=d, sc=sc, sh=sh: e.tensor_scalar(d, s, sc, sh, ALU.mult, ALU.add),
                  rd=[srcB, G.modTB], wr=[dstB])


def phase_h(cx, G, xT, hT, hTB):
    p = cx.p
    with ExitStack() as es:
        xin = Ring([cx.sb(es, [128, T], F32, "hx") for _ in range(2)])
        for ft in range(KT):
            xt, xb = xin.next()
            p.dma("sp", xt[:], xT[ft], wr=[xb])
            modulate(cx, G, xt[:], xb, hT[:, ft, :], hTB, ft, 16, 0)
    p.barrier()


def linear_fm(cx, G, inT, inB, ktn, groups, evac, tblocks, wbufs):
    p = cx.p
    for pieces, tags in groups:
        wt, wb = wbufs.next()
        for ap, c0 in pieces:
            n = ap.shape[1]
            p.dma("pool", wt[:, 0:ktn, c0:c0 + n], ap.rearrange("(kt p) n -> p kt n", p=128), wr=[wb])
        for oc, tag in enumerate(tags):
            for (t0, tn) in tblocks:
                ps, pb = G.ps.next()
                mm(p, ps[:, 0:tn],
                   [(wt[:, kt, oc * 128:(oc + 1) * 128], inT[:, kt, t0:t0 + tn]) for kt in range(ktn)],
                   rd=[wb, inB], wr=[pb])
                evac(tag, t0, tn, ps, pb)


TB512 = [(i * 512, 512) for i in range(6)]


def phase_proj(cx, G, L, hT, hTB, W, projT, projTok, loraT, loraTB, new_k, new_v):
    p = cx.p
    w_in = W["w_in"][L]
    with ExitStack() as es:
        wbufs = Ring([cx.sb(es, [128, KT, 512], BF16, "win") for _ in range(2)])
        stg = Ring([cx.sb(es, [128, 512], F32, "pst") for _ in range(4)])
        cnt = [0]

        def evac_fm(tag, t0, tn, ps, pb):
            cnt[0] += 1
            if isinstance(tag, str):
                li = int(tag[1])
                fn = AF.Tanh if li == 0 else AF.Copy
                p.ins("act", lambda e: e.activation(loraT[:, li, t0:t0 + tn], ps[:, 0:tn], fn), rd=[pb], wr=[loraTB])
                return
            st, sb_ = stg.next()
            if cnt[0] % 2 == 0:
                p.ins("dve", lambda e: e.tensor_copy(st[:, 0:tn], ps[:, 0:tn]), rd=[pb], wr=[sb_])
            else:
                p.ins("act", lambda e: e.activation(st[:, 0:tn], ps[:, 0:tn], AF.Copy), rd=[pb], wr=[sb_])
            p.dma("sp", projT[tag, :, t0:t0 + tn], st[:, 0:tn], rd=[sb_])

        groups = []
        for c0 in list(range(0, 1536, 512)) + list(range(3072, 5632, 512)):
            groups.append(([(w_in[:, c0:c0 + 512], 0)], [c0 // 128 + i for i in range(4)]))
        groups.append(([(W["rwkv_w1"][L, 0], 0), (W["rwkv_w1"][L, 1], 64),
                        (W["rwkv_a1"][L, 0], 128), (W["rwkv_a1"][L, 1], 192)], ["l0", "l1"]))
        linear_fm(cx, G, hT, hTB, KT, groups, evac_fm, TB512, wbufs)

        for gi, c0 in enumerate((C_V, C_V + 512, C_V + 1024, C_AK)):
            wt, wb = wbufs.next()
            p.dma("pool", wt[:], w_in[:, c0:c0 + 512].rearrange("(kt p) n -> p kt n", p=128), wr=[wb])
            for tt in range(T // 128):
                ps, pb = G.ps.next()
                mm(p, ps[:], [(hT[:, kt, tt * 128:(tt + 1) * 128], wt[:, kt, :]) for kt in range(KT)],
                   rd=[wb, hTB], wr=[pb])
                st, sb_ = stg.next()
                if tt % 2 == 0:
                    p.ins("dve", lambda e: e.tensor_copy(st[:], ps[:]), rd=[pb], wr=[sb_])
                else:
                    p.ins("act", lambda e: e.activation(st[:], ps[:], AF.Copy), rd=[pb], wr=[sb_])
                p.dma("sp", projTok[tt * 128:(tt + 1) * 128, gi * 512:(gi + 1) * 512], st[:], rd=[sb_])
                if gi == 3 and tt < 8:
                    s, r0 = tt // 2, (tt % 2) * 128
                    p.dma("sp", new_k[s, L, r0:r0 + 128, :], st[:, 0:256], rd=[sb_])
                    p.dma("sp", new_v[s, L, r0:r0 + 128, :], st[:, 256:512], rd=[sb_])
    p.barrier()


def resid_evac(cx, G, stg, xT, yT, j_gate):
    p = cx.p

    def evac(ot, t0, tn, ps, pb):
        c = 0 if t0 < TP else 1
        (xt, xb), (tm, tb) = stg.next()
        p.dma("sp", xt[:, 0:tn], xT[ot, :, t0:t0 + tn], wr=[xb])
        p.ins("act", lambda e: e.activation(tm[:, 0:tn], ps[:, 0:tn], AF.Copy, scale=G.modT[:, j_gate + ot, c:c + 1]),
              rd=[pb, G.modTB], wr=[tb])
        p.ins("dve", lambda e: e.scalar_tensor_tensor(xt[:, 0:tn], xt[:, 0:tn], ALPHA, tm[:, 0:tn], ALU.mult, ALU.add),
              rd=[xb, tb], wr=[xb])
        p.dma("sp", yT[ot, :, t0:t0 + tn], xt[:, 0:tn], rd=[xb])
    return evac


def phase_out(cx, G, L, W, mixT, xT, yT):
    p = cx.p
    with ExitStack() as es:
        mT, mB = cx.sb(es, [128, KT, T], BF16, "mixres")
        for ft in range(KT):
            p.dma("sp", mT[:, ft, :], mixT[ft], wr=[mB])
        wbufs = Ring([cx.sb(es, [128, KT, 512], BF16, "wout") for _ in range(2)])
        stg = Ring([(cx.sb(es, [128, 512], F32, "ox"), cx.sb(es, [128, 512], F32, "ot")) for _ in range(3)])
        groups = [([(W["w_out"][L][:, c0:c0 + 512], 0)], [c0 // 128 + i for i in range(4)]) for c0 in range(0, D, 512)]
        linear_fm(cx, G, mT, mB, KT, groups, resid_evac(cx, G, stg, xT, yT, 32), TB512, wbufs)
    p.barrier()


def ln_pass(cx, G, L, yT, g_vec, b_vec, xT, h2T, h2TB, final=None):
    p = cx.p
    with ExitStack() as es:
        gT, gB = cx.sb(es, [128, KT], F32, "lng")
        bT, bB = cx.sb(es, [128, KT], F32, "lnb")
        vecT(cx, G, es, rows128(g_vec), KT, gT[:], gB)
        vecT(cx, G, es, rows128(b_vec), KT, bT[:], bB)
        ybuf = Ring([cx.sb(es, [128, KT, 512], F32, "lny") for _ in range(2)])
        sqr = Ring([cx.sb(es, [128, 512], F32, "lnsq") for _ in range(3)])
        mean, meanB = cx.sb(es, [128, 512], F32, "mean")
        rstd, rstdB = cx.sb(es, [128, 512], F32, "rstd")
        msq, msqB = cx.sb(es, [128, 512], F32, "msq")
        zr = Ring([cx.sb(es, [128, 512], F32, "lnz") for _ in range(3)])
        if final is not None:
            osb = Ring([cx.sb(es, [128, D], F32, "lno") for _ in range(2)])
        for (t0, tn) in TB512:
            yt, yb = ybuf.next()
            p.dma("sp", yt[:], yT[:, :, t0:t0 + tn].rearrange("f p t -> p f t"), wr=[yb])
            ps1, pb1 = G.ps.next()
            mm(p, ps1[:], [(G.ones[:], yt[:, ft, :]) for ft in range(KT)], rd=[yb, G.onesB], wr=[pb1])
            ps2, pb2 = G.ps.next()
            toks = []
            sq_list = []
            n = KT
            for ft in range(KT):
                sq, sqb = sqr.next()
                p.ins("act", lambda e, sq=sq, ft=ft: e.activation(sq[:], yt[:, ft, :], AF.Square), rd=[yb], wr=[sqb])
                p.ins("pe", lambda e, sq=sq, ft=ft: e.matmul(ps2[:], G.ones[:], sq[:], start=(ft == 0), stop=(ft == n - 1)),
                      rd=[sqb, G.onesB], wr=[pb2])
            p.ins("act", lambda e: e.activation(mean[:], ps1[:], AF.Copy, scale=1.0 / D), rd=[pb1], wr=[meanB])
            p.ins("dve", lambda e: e.tensor_tensor(msq[:], mean[:], mean[:], ALU.mult), rd=[meanB], wr=[msqB])
            p.ins("dve", lambda e: e.scalar_tensor_tensor(msq[:], ps2[:], 1.0 / D, msq[:], ALU.mult, ALU.subtract),
                  rd=[pb2, msqB], wr=[msqB])
            p.ins("dve", lambda e: e.tensor_scalar(msq[:], msq[:], LN_EPS, None, ALU.add), rd=[msqB], wr=[msqB])
            p.ins("act", lambda e: e.activation(msq[:], msq[:], AF.Sqrt), rd=[msqB], wr=[msqB])
            p.ins("dve", lambda e: e.reciprocal(rstd[:], msq[:]), rd=[msqB], wr=[rstdB])
            for ft in range(KT):
                z, zb = zr.next()
                p.ins("dve", lambda e, z=z, ft=ft: e.tensor_tensor(z[:], yt[:, ft, :], mean[:], ALU.subtract),
                      rd=[yb, meanB], wr=[zb])
                p.ins("dve", lambda e, z=z: e.tensor_tensor(z[:], z[:], rstd[:], ALU.mult), rd=[zb, rstdB], wr=[zb])
                p.ins("act", lambda e, z=z, ft=ft: e.activation(z[:], z[:], AF.Identity, bias=bT[:, ft:ft + 1], scale=gT[:, ft:ft + 1]),
                      rd=[zb, gB, bB], wr=[zb])
                if final is None:
                    p.dma("sp", xT[ft, :, t0:t0 + tn], z[:], rd=[zb])
                    if h2T is not None:
                        modulate(cx, G, z[:], zb, h2T[:, ft, t0:t0 + tn], h2TB, ft, 64, 48, t0=t0, tn=tn)
                else:
                    ps, pb = G.ps.next()
                    for j in range(4):
                        mm(p, ps[:, j * 128:(j + 1) * 128], [(z[:, j * 128:(j + 1) * 128], G.ident[:])], rd=[zb, G.identB], wr=[pb])
                    for j in range(4):
                        pass
                    final_store(cx, G, ps, pb, ft, t0, final, osb)
    p.barrier()


def final_store(cx, G, ps, pb, ft, t0, final, osb):
    p = cx.p
    st = final["stage"]
    p.ins("dve" if ft % 2 == 0 else "act",
          (lambda e: e.tensor_copy(st[0][:, :, ft, :], ps[:].rearrange("p (j f) -> p j f", j=4))) if ft % 2 == 0 else
          (lambda e: e.activation(st[0][:, :, ft, :], ps[:].rearrange("p (j f) -> p j f", j=4), AF.Copy)),
          rd=[pb], wr=[st[1]])
    if ft == KT - 1:
        for j in range(4):
            tt = t0 // 128 + j
            dst = final["y_p"][tt * 128:(tt + 1) * 128, :] if tt < 8 else final["y_s"][(tt - 8) * 128:(tt - 7) * 128, :]
            p.dma("sp", dst, st[0][:, j, :, :].rearrange("p a b -> p (a b)"), rd=[st[1]])


def phase_ffn(cx, G, L, W, h2T, h2TB, gT_d, xT, yT):
    p = cx.p
    w_up = W["ffn_w_up"][L]
    GL = T + 6

    def gcol(t):
        for s, (a, n) in enumerate(SEGS):
            if a <= t < a + n:
                return t + s + 1
        raise ValueError

    with ExitStack() as es:
        wbufs = Ring([cx.sb(es, [128, KT, 512], BF16, "wup") for _ in range(2)])
        cw, cwB = cx.sb(es, [128, 3, 86], F32, "convw")
        for k in range(3):
            vecT(cx, G, es, rows128(W["ffn_conv_w"][L, k]), 86, cw[:, k, :], cwB)
        ubuf = {}
        for nm in ("a", "b"):
            ubuf[nm] = Ring([cx.sb(es, [128, GL], F32, "u" + nm) for _ in range(2 if nm == "a" else 1)])
            for (t_, b_) in ubuf[nm].items:
                p.ins("pool", lambda e, t_=t_: e.memset(t_[:], 0.0), wr=[b_])
        cbuf = {nm: cx.sb(es, [128, GL], F32, "c" + nm) for nm in ("a", "b")}
        gout = Ring([cx.sb(es, [128, T], BF16, "gout") for _ in range(1)])
        cur = {}

        def evac(tag, t0, tn, ps, pb):
            nm, j = tag
            if t0 == 0:
                cur[nm] = ubuf[nm].next()
            u, ub = cur[nm]
            for s, (a, n) in enumerate(SEGS):
                lo, hi = max(a, t0), min(a + n, t0 + tn)
                if lo >= hi:
                    continue
                eng = "act" if nm == "a" else "dve"
                src = ps[:, lo - t0:hi - t0]
                dst = u[:, lo + s + 1:hi + s + 1]
                if eng == "act":
                    p.ins("act", lambda e, src=src, dst=dst: e.activation(dst, src, AF.Copy), rd=[pb], wr=[ub])
                else:
                    p.ins("dve", lambda e, src=src, dst=dst: e.tensor_copy(dst, src), rd=[pb], wr=[ub])
            if t0 + tn == T:
                fi = j if nm == "a" else FT + j
                c, cb = cbuf[nm]
                eng = "dve"
                p.ins(eng, lambda e: e.tensor_scalar(c[:, 1:GL - 1], u[:, 1:GL - 1], cw[:, 1, fi:fi + 1], None, ALU.mult),
                      rd=[ub, cwB], wr=[cb])
                p.ins(eng, lambda e: e.scalar_tensor_tensor(c[:, 1:GL - 1], u[:, 0:GL - 2], cw[:, 0, fi:fi + 1], c[:, 1:GL - 1], ALU.mult, ALU.add),
                      rd=[ub, cwB, cb], wr=[cb])
                p.ins(eng, lambda e: e.scalar_tensor_tensor(c[:, 1:GL - 1], u[:, 2:GL], cw[:, 2, fi:fi + 1], c[:, 1:GL - 1], ALU.mult, ALU.add),
                      rd=[ub, cwB, cb], wr=[cb])
                if nm == "b":
                    ca, cab = cbuf["a"]
                    g, gb = gout.next()
                    p.ins("act", lambda e: e.activation(ca[:, 1:GL - 1], ca[:, 1:GL - 1], AF.Silu), rd=[cab], wr=[cab])
                    for s, (a, n) in enumerate(SEGS):
                        p.ins("dve", lambda e, s=s, a=a, n=n: e.tensor_tensor(g[:, a:a + n], ca[:, a + s + 1:a + n + s + 1],
                                                                             c[:, a + s + 1:a + n + s + 1], ALU.mult),
                              rd=[cab, cb], wr=[gb])
                    p.dma("sp", gT_d[j], g[:], rd=[gb])

        groups = []
        for j0 in range(0, FT, 2):
            nj = min(2, FT - j0)
            pieces = [(w_up[:, j0 * 128:(j0 + nj) * 128], 0), (w_up[:, DFF + j0 * 128:DFF + (j0 + nj) * 128], 256)]
            tags = []
            for i in range(nj):
                tags.append((("a", j0 + i), i * 128))
                tags.append((("b", j0 + i), 256 + i * 128))
            groups.append((pieces, tags))
        for pieces, tags in groups:
            wt, wb = wbufs.next()
            for ap, c0 in pieces:
                n = ap.shape[1]
                p.dma("pool", wt[:, :, c0:c0 + n], ap.rearrange("(kt p) n -> p kt n", p=128), wr=[wb])
            for tag, co in tags:
                for (t0, tn) in TB512:
                    ps, pb = G.ps.next()
                    mm(p, ps[:, 0:tn], [(wt[:, kt, co:co + 128], h2T[:, kt, t0:t0 + tn]) for kt in range(KT)],
                       rd=[wb, h2TB], wr=[pb])
                    evac(tag, t0, tn, ps, pb)
    p.barrier()


def phase_down(cx, G, L, W, gT_d, xT, yT):
    p = cx.p
    w_dn = W["ffn_w_down"][L]
    TC = 1536
    with ExitStack() as es:
        gres, gB = cx.sb(es, [128, FT, TC], BF16, "gres")
        wbufs = Ring([cx.sb(es, [128, FT, 256], BF16, "wdn") for _ in range(3)])
        stg = Ring([(cx.sb(es, [128, 512], F32, "dx"), cx.sb(es, [128, 512], F32, "dt")) for _ in range(2)])
        ev = resid_evac(cx, G, stg, xT, yT, 80)
        for tc in range(T // TC):
            for ft in range(FT):
                p.dma("sp", gres[:, ft, :], gT_d[ft, :, tc * TC:(tc + 1) * TC], wr=[gB])
            for c0 in range(0, D, 256):
                wt, wb = wbufs.next()
                p.dma("pool", wt[:], w_dn[:, c0:c0 + 256].rearrange("(kt p) n -> p kt n", p=128), wr=[wb])
                for oc in range(2):
                    ot = c0 // 128 + oc
                    for tb in range(TC // 512):
                        ps, pb = G.ps.next()
                        mm(p, ps[:], [(wt[:, kt, oc * 128:(oc + 1) * 128], gres[:, kt, tb * 512:(tb + 1) * 512]) for kt in range(FT)],
                           rd=[wb, gB], wr=[pb])
                        ev(ot, tc * TC + tb * 512, 512, ps, pb)
    p.barrier()


W_NAMES = ['w_mod', 'b_mod', 'w_in', 'rwkv_w0', 'rwkv_w1', 'rwkv_w2', 'rwkv_a0', 'rwkv_a1', 'rwkv_a2',
           'rwkv_k_k', 'rwkv_k_a', 'rwkv_r_k', 'rwkv_lnx_g', 'rwkv_lnx_b', 'hy_short_w', 'hy_f_w1', 'hy_f_b1',
           'hy_f_freq1', 'hy_f_w2', 'hy_f_b2', 'hy_f_freq2', 'hy_f_w3', 'hy_bias', 'attn_sink', 'w_out',
           'ln1_g', 'ln1_b', 'ffn_w_up', 'ffn_conv_w', 'ffn_w_down', 'ln2_g', 'ln2_b']
W_SHAPES = {
    'w_mod': (NL, D, 6 * D), 'b_mod': (NL, 6 * D), 'w_in': (NL, D, INW), 'rwkv_w0': (NL, 2, RW),
    'rwkv_w1': (NL, 2, D, 64), 'rwkv_w2': (NL, 2, 64, RW), 'rwkv_a0': (NL, 2, RW), 'rwkv_a1': (NL, 2, D, 64),
    'rwkv_a2': (NL, 2, 64, RW), 'rwkv_k_k': (NL, RW), 'rwkv_k_a': (NL, RW), 'rwkv_r_k': (NL, RW),
    'rwkv_lnx_g': (NL, RW), 'rwkv_lnx_b': (NL, RW), 'hy_short_w': (NL, 3, 1536), 'hy_f_w1': (NL, 33, 64),
    'hy_f_b1': (NL, 64), 'hy_f_freq1': (NL, 64), 'hy_f_w2': (NL, 64, 64), 'hy_f_b2': (NL, 64), 'hy_f_freq2': (NL, 64),
    'hy_f_w3': (NL, 64, 2048), 'hy_bias': (NL, 2 * HW), 'attn_sink': (NL, 12), 'w_out': (NL, D, D),
    'ln1_g': (NL, D), 'ln1_b': (NL, D), 'ffn_w_up': (NL, D, 2 * DFF), 'ffn_conv_w': (NL, 3, 2 * DFF),
    'ffn_w_down': (NL, DFF, D), 'ln2_g': (NL, D), 'ln2_b': (NL, D)}


def make_consts():
    c = {}
    c["ident"] = np.eye(128, dtype=np.float32)
    attn_consts(c)
    hyena_consts(c)
    rwkv_consts(c)
    return c


def build(nlayers=NL, debug=None):
    nc = bass.Bass("TRN2", target_bir_lowering=False)
    ext = lambda n, s: nc.dram_tensor(n, list(s), F32, kind="ExternalInput").ap()
    x_p = ext("x_p", (TP, D))
    x_s = ext("x_s", (TS, D))
    c_s = ext("c_s", (1, D))
    c_ctx = ext("c_ctx", (1, D))
    st_in = ext("st_in", (NL, 2, 12, 64, 64))
    ck_in = ext("ck_in", (NL, 512, KVW))
    cv_in = ext("cv_in", (NL, 512, KVW))
    W = {n: ext(n, W_SHAPES[n]) for n in W_NAMES}
    consts = make_consts()
    cst = {n: ext("cst_" + n, v.shape) for n, v in consts.items()}
    out = lambda n, s: nc.dram_tensor(n, list(s), F32, kind="ExternalOutput").ap()
    y_p = out("y_p", (TP, D))
    y_s = out("y_s", (TS, D))
    new_st = out("new_st", (4, NL, 2, 12, 64, 64))
    new_k = out("new_k", (4, NL, 256, KVW))
    new_v = out("new_v", (4, NL, 256, KVW))
    dk = "ExternalOutput" if debug else "Internal"
    xT = nc.dram_tensor("xT", [KT, 128, T], F32, kind=dk).ap()
    yT = nc.dram_tensor("yT", [KT, 128, T], F32, kind=dk).ap()
    projT = nc.dram_tensor("projT", [46, 128, T], F32, kind=dk).ap()
    projTok = nc.dram_tensor("projTok", [T, 2048], F32, kind=dk).ap()
    mixT = nc.dram_tensor("mixT", [KT, 128, T], BF16, kind="Internal").ap()
    gT_d = nc.dram_tensor("gT_d", [FT, 128, T], BF16, kind="Internal").ap()
    hyT_d = nc.dram_tensor("hyT", [12, 128, T], F32, kind="Internal").ap()
    Kd_d = {Lq: nc.dram_tensor("Kd%d" % Lq, [2, 2, Lq // 128 + 1, 128, HW], F32, kind="Internal").ap() for Lq in (256, 2048)}
    if debug and debug.get("mix_override"):
        mix_ov = ext("mix_ov", (KT, 128, T))
    if debug and debug.get("stop") == "mix":
        mix_dbg = out("mix_dbg", (KT, 128, T))
    with ExitStack() as es:
        p = Prog(nc, es)
        cx = Ctx(nc, p, es)
        G = setup_globals(cx, es, cst)
        G.vstage = Ring([cx.sb(es, [128, 128], F32, "vst") for _ in range(2)])
        G.modT, G.modTB = cx.sb(es, [128, 96, 2], F32, "modT")
        G.hyT = hyT_d
        G.Kd = Kd_d
        phase_cond(cx, G, es, c_ctx, c_s)
        phase_x0(cx, G, x_p, x_s, xT)
        for L in range(nlayers):
            phase_mod(cx, G, L, W["w_mod"], W["b_mod"])
            with ExitStack() as les:
                loraT, loraTB = cx.sb(les, [128, 2, T], BF16, "loraT")
                with ExitStack() as hes:
                    hT, hTB = cx.sb(hes, [128, KT, T], BF16, "hT")
                    phase_h(cx, G, xT, hT, hTB)
                    phase_proj(cx, G, L, hT, hTB, W, projT, projTok, loraT, loraTB, new_k, new_v)
                if debug and debug.get("stop") == "proj":
                    break
                if debug and debug.get("mix_override"):
                    with ExitStack() as mes:
                        mt, mb = cx.sb(mes, [128, T], BF16, "mov")
                        for ft in range(KT):
                            p.dma("pool", mt[:], mix_ov[ft], wr=[mb])
                            p.dma("sp", mixT[ft], mt[:], rd=[mb])
                    p.barrier()
                else:
                    phase_mixers(cx, G, L, W, projT, projTok, loraT, loraTB, mixT, st_in, ck_in, cv_in, new_st, cst,
                                 which=(debug or {}).get("which", ("rwkv", "hyena", "attn")))
                    if debug and debug.get("stop") == "mix":
                        with ExitStack() as mes:
                            mt, mb = cx.sb(mes, [128, T], BF16, "mdb")
                            for ft in range(KT):
                                p.dma("sp", mt[:], mixT[ft], wr=[mb])
                                p.dma("pool", mix_dbg[ft], mt[:], rd=[mb])
                        break
            phase_out(cx, G, L, W, mixT, xT, yT)
            with ExitStack() as fes:
                h2T, h2TB = cx.sb(fes, [128, KT, T], BF16, "h2T")
                ln_pass(cx, G, L, yT, W["ln1_g"][L], W["ln1_b"][L], xT, h2T, h2TB)
                if debug and debug.get("stop") == "ln1":
                    break
                phase_ffn(cx, G, L, W, h2T, h2TB, gT_d, xT, yT)
            phase_down(cx, G, L, W, gT_d, xT, yT)
            if L == nlayers - 1:
                with ExitStack() as oes:
                    st = cx.sb(oes, [128, 4, KT, 128], F32, "fstage")
                    ln_pass(cx, G, L, yT, W["ln2_g"][L], W["ln2_b"][L], xT, None, None,
                            final={"stage": st, "y_p": y_p, "y_s": y_s})
            else:
                ln_pass(cx, G, L, yT, W["ln2_g"][L], W["ln2_b"][L], xT, None, None)
        p.barrier()
        print("instructions:", p.nins)
    return nc, consts


def phase_mixers(cx, G, L, W, projT, projTok, loraT, loraTB, mixT, st_in, ck_in, cv_in, new_st, cst, which=("rwkv", "hyena", "attn")):
    if "attn" in which:
        phase_attn(cx, G, L, W, projT, projTok, mixT, ck_in, cv_in, cst)
    if "hyena" in which:
        phase_hyena(cx, G, L, W, projT, mixT, cst)
    if "rwkv" in which:
        phase_rwkv(cx, G, L, W, projT, projTok, loraT, loraTB, mixT, st_in, new_st, cst)


def make_in_maps(inputs, consts):
    maps = []
    for c in range(8):
        m = {}
        m["x_p"] = np.ascontiguousarray(inputs["x_prompt"][4 * c:4 * c + 4]).reshape(TP, D)
        m["x_s"] = np.ascontiguousarray(inputs["x_sample"][c])
        m["c_s"] = np.ascontiguousarray(inputs["c"][c:c + 1])
        m["c_ctx"] = np.ascontiguousarray(inputs["c_ctx"]).reshape(1, D)
        m["st_in"] = np.ascontiguousarray(inputs["state_rwkv"][c])
        m["ck_in"] = np.ascontiguousarray(inputs["cache_k"][c]).reshape(NL, 512, KVW)
        m["cv_in"] = np.ascontiguousarray(inputs["cache_v"][c]).reshape(NL, 512, KVW)
        for n in W_NAMES:
            m[n] = np.ascontiguousarray(inputs[n]).reshape(W_SHAPES[n])
        for n, v in consts.items():
            m["cst_" + n] = v
        maps.append(m)
    return maps


def kernel(**inputs):
    nc, consts = build()
    maps = make_in_maps(inputs, consts)
    res = run_bass_kernel_spmd(nc, maps, core_ids=list(range(8)))
    r = res.results
    y_prompt = np.concatenate([r[c]["y_p"].reshape(4, 256, D) for c in range(8)], 0)
    y_sample = np.stack([r[c]["y_s"] for c in range(8)], 0)
    new_state = np.concatenate([r[c]["new_st"] for c in range(8)], 0)
    new_k = np.concatenate([r[c]["new_k"].reshape(4, NL, 256, 4, 64) for c in range(8)], 0)
    new_v = np.concatenate([r[c]["new_v"].reshape(4, NL, 256, 4, 64) for c in range(8)], 0)
    return (y_prompt.astype(np.float32), y_sample.astype(np.float32), new_state.astype(np.float32),
            new_k.astype(np.float32), new_v.astype(np.float32))


def attn_consts(c):
    t = np.arange(TS)
    row = (t // 64).astype(np.float32)
    col = (t % 64).astype(np.float32)
    nf = 16
    inv = (10000.0 ** (-np.arange(nf, dtype=np.float32) / nf)).astype(np.float32)
    ang = np.concatenate([row[:, None] * inv, col[:, None] * inv], -1).astype(np.float32)
    cs, sn = np.cos(ang).T, np.sin(ang).T
    c["ropeC"] = np.concatenate([cs, cs], 0).astype(np.float32)
    c["ropeS"] = np.concatenate([-sn, sn], 0).astype(np.float32)
    kk = np.arange(128)[:, None]
    qq = np.arange(128)[None, :]
    mprev = (qq <= kk).astype(np.float32)
    mnext = (kk <= qq).astype(np.float32)
    c["mprev"] = np.tile(mprev[:, None, :], (1, 3, 1)).reshape(128, 384)
    c["mnext"] = np.tile(mnext[:, None, :], (1, 3, 1)).reshape(128, 384)


def phase_attn(cx, G, L, W, projT, projTok, mixT, ck_in, cv_in, cst):
    p = cx.p
    pflat = projT.rearrange("a p t -> (a p) t")
    mflat = mixT.rearrange("a p t -> (a p) t")
    with ExitStack() as es:
        ropeC, rcB = cx.sb(es, [64, TS], F32, "ropeC")
        ropeS, rsB = cx.sb(es, [64, TS], F32, "ropeS")
        p.dma("sp", ropeC[:], cst["ropeC"], wr=[rcB])
        p.dma("sp", ropeS[:], cst["ropeS"], wr=[rsB])
        mprev, mpB = cx.sb(es, [128, 384], BF16, "mprev")
        mnext, mnB = cx.sb(es, [128, 384], BF16, "mnext")
        p.dma("pool", mprev[:], cst["mprev"], wr=[mpB])
        p.dma("pool", mnext[:], cst["mnext"], wr=[mnB])
        esk, eskB = cx.sb(es, [128, 12], F32, "esk")
        p.dma("sp", esk[:], W["attn_sink"][L].partition_broadcast(128), wr=[eskB])
        p.ins("act", lambda e: e.activation(esk[:], esk[:], AF.Exp), rd=[eskB], wr=[eskB])
        qraw, qrB = cx.sb(es, [64, 3, TS], F32, "qraw")
        qsw, qsB = cx.sb(es, [64, 3, TS], F32, "qsw")
        qb, qbB = cx.sb(es, [64, 16, 3, 128], BF16, "qb")
        kraw, krB = cx.sb(es, [64, TS], F32, "kraw")
        ksw, ksB = cx.sb(es, [64, TS], F32, "ksw")
        kb, kbB = cx.sb(es, [64, TS], BF16, "kb")
        kc, kcB = cx.sb(es, [64, 512], BF16, "kc")
        ckraw, ckB = cx.sb(es, [128, 4, 64], F32, "ckraw")
        v, vB = cx.sb(es, [128, 16, 65], BF16, "v")
        vc, vcB = cx.sb(es, [128, 4, 65], BF16, "vc")
        oacc, oB = cx.sb(es, [64, 3, TS], BF16, "oacc")
        pTr = Ring([cx.sb(es, [128, 384], BF16, "pT") for _ in range(3)])
        dbr = Ring([(cx.sb(es, [128, 384], F32, "den"), cx.sb(es, [64, 384], F32, "bc")) for _ in range(2)])
        pending = [None]
        p.ins("pool", lambda e: e.memset(v[:, :, 64:65], 1.0), wr=[vB])
        p.ins("pool", lambda e: e.memset(vc[:, :, 64:65], 1.0), wr=[vcB])

        for (tok0, Ls) in SEGS:
            sample = Ls == TS
            nb = Ls // 128
            for g in range(4):
                for h in range(3):
                    r0 = C_Q + (3 * g + h) * 64
                    p.dma("sp", qraw[:, h, 0:Ls], pflat[r0:r0 + 64, tok0:tok0 + Ls], wr=[qrB])
                    if sample:
                        p.dma("sp", qsw[0:32, h, 0:Ls], pflat[r0 + 32:r0 + 64, tok0:tok0 + Ls], wr=[qsB])
                        p.dma("sp", qsw[32:64, h, 0:Ls], pflat[r0:r0 + 32, tok0:tok0 + Ls], wr=[qsB])
                r0 = C_AK + g * 64
                p.dma("sp", kraw[:, 0:Ls], pflat[r0:r0 + 64, tok0:tok0 + Ls], wr=[krB])
                if sample:
                    p.dma("sp", ksw[0:32, 0:Ls], pflat[r0 + 32:r0 + 64, tok0:tok0 + Ls], wr=[ksB])
                    p.dma("sp", ksw[32:64, 0:Ls], pflat[r0:r0 + 32, tok0:tok0 + Ls], wr=[ksB])
                p.dma("pool", v[:, 0:nb, 0:64],
                      projTok[tok0:tok0 + Ls, 1792 + g * 64:1792 + g * 64 + 64].rearrange("(tt p) d -> p tt d", p=128), wr=[vB])
                for h in range(3):
                    dst = qb[:, 0:nb, h, :]
                    if sample:
                        p.ins("dve", lambda e, h=h: e.tensor_tensor(qraw[:, h, :], qraw[:, h, :], ropeC[:], ALU.mult), rd=[qrB, rcB], wr=[qrB])
                        p.ins("pool", lambda e, h=h: e.tensor_tensor(qsw[:, h, :], qsw[:, h, :], ropeS[:], ALU.mult), rd=[qsB, rsB], wr=[qsB])
                        p.ins("dve", lambda e, h=h, dst=dst: e.tensor_tensor(dst, qraw[:, h, :].rearrange("p (b t) -> p b t", t=128),
                                                                           qsw[:, h, :].rearrange("p (b t) -> p b t", t=128), ALU.add),
                              rd=[qrB, qsB], wr=[qbB])
                    else:
                        p.ins("dve", lambda e, h=h, dst=dst: e.tensor_copy(dst, qraw[:, h, 0:Ls].rearrange("p (b t) -> p b t", t=128)),
                              rd=[qrB], wr=[qbB])
                if sample:
                    p.ins("dve", lambda e: e.tensor_tensor(kraw[:], kraw[:], ropeC[:], ALU.mult), rd=[krB, rcB], wr=[krB])
                    p.ins("pool", lambda e: e.tensor_tensor(ksw[:], ksw[:], ropeS[:], ALU.mult), rd=[ksB, rsB], wr=[ksB])
                    p.ins("dve", lambda e: e.tensor_tensor(kb[:], kraw[:], ksw[:], ALU.add), rd=[krB, ksB], wr=[kbB])
                    p.dma("sp", ckraw[:], ck_in[L][:, g * 64:(g + 1) * 64].rearrange("(tt p) d -> p tt d", p=128), wr=[ckB])
                    ps, pb = G.ps.next()
                    for j in range(4):
                        mm(p, ps[0:64, j * 128:(j + 1) * 128], [(ckraw[:, j, :], G.ident[:])], rd=[ckB, G.identB], wr=[pb])
                    p.ins("act", lambda e: e.activation(kc[:], ps[0:64, :], AF.Copy), rd=[pb], wr=[kcB])
                    p.dma("pool", vc[:, :, 0:64], cv_in[L][:, g * 64:(g + 1) * 64].rearrange("(tt p) d -> p tt d", p=128), wr=[vcB])
                else:
                    p.ins("act", lambda e: e.activation(kb[:, 0:Ls], kraw[:, 0:Ls], AF.Copy), rd=[krB], wr=[kbB])
                for i in range(nb):
                    keys = []
                    if sample:
                        for j in (i - 1, i, i + 1):
                            if 0 <= j < nb:
                                keys.append((kb[:, j * 128:(j + 1) * 128], kbB, v[:, j, :], vB,
                                             (mprev, mpB) if j == i - 1 else ((mnext, mnB) if j == i + 1 else None)))
                        for j in range(4):
                            keys.append((kc[:, j * 128:(j + 1) * 128], kcB, vc[:, j, :], vcB, None))
                    else:
                        for j in range(nb):
                            keys.append((kb[:, j * 128:(j + 1) * 128], kbB, v[:, j, :], vB, None))
                    psO, pOB = G.psx.next()
                    nk = len(keys)

                    def score(ki):
                        kap, kB_ = keys[ki][0], keys[ki][1]
                        psS, pSB = G.ps.next()
                        mm(p, psS[:, 0:384], [(kap, qb[:, i, :, :].rearrange("p h t -> p (h t)"))], rd=[kB_, qbB], wr=[pSB])
                        return psS, pSB
                    nxt = score(0)
                    for ki, (kap, kB_, vap, vB_, msk) in enumerate(keys):
                        psS, pSB = nxt
                        if ki + 1 < nk:
                            nxt = score(ki + 1)
                        if ki == min(3, nk - 1) and pending[0] is not None:
                            pending[0]()
                            pending[0] = None
                        pT, pTB = pTr.next()
                        p.ins("act", lambda e, pT=pT, psS=psS: e.activation(pT[:], psS[:, 0:384], AF.Exp, scale=0.125), rd=[pSB], wr=[pTB])
                        if msk is not None:
                            p.ins("dve", lambda e, pT=pT, msk=msk: e.tensor_tensor(pT[:], pT[:], msk[0][:], ALU.mult), rd=[pTB, msk[1]], wr=[pTB])
                        p.ins("pe", lambda e, vap=vap, pT=pT, ki=ki, psO=psO: e.matmul(psO[0:65, 0:384], vap, pT[:], start=(ki == 0), stop=(ki == nk - 1)),
                              rd=[vB_, pTB], wr=[pOB])

                    def tail(i=i, g=g, psO=psO, pOB=pOB):
                        (den, denB), (bc, bcB) = dbr.next()
                        for h in range(3):
                            hh = 3 * g + h
                            p.ins("dve", lambda e, h=h, hh=hh: e.tensor_scalar(den[64:65, h * 128:(h + 1) * 128], psO[64:65, h * 128:(h + 1) * 128],
                                                                             esk[64:65, hh:hh + 1], None, ALU.add), rd=[pOB, eskB], wr=[denB])
                        p.ins("dve", lambda e: e.reciprocal(den[64:65, :], den[64:65, :]), rd=[denB], wr=[denB])
                        psB, pBB = G.ps.next()
                        mm(p, psB[0:64, 0:384], [(G.ones[64:65, 0:64], den[64:65, :])], rd=[G.onesB, denB], wr=[pBB])
                        p.ins("act", lambda e: e.activation(bc[:], psB[0:64, 0:384], AF.Copy), rd=[pBB], wr=[bcB])
                        p.ins("dve", lambda e: e.tensor_tensor(oacc[:, :, i * 128:(i + 1) * 128], psO[0:64, 0:384].rearrange("p (h t) -> p h t", h=3),
                                                              bc[:].rearrange("p (h t) -> p h t", h=3), ALU.mult), rd=[pOB, bcB], wr=[oB])
                    pending[0] = tail
                if pending[0] is not None:
                    pending[0]()
                    pending[0] = None
                for h in range(3):
                    r0 = 1280 + (3 * g + h) * 64
                    p.dma("sp", mflat[r0:r0 + 64, tok0:tok0 + Ls], oacc[:, h, 0:Ls], rd=[oB])
    p.barrier()


HY_LS = (256, 2048)


def hyena_consts(c):
    for Lq in HY_LS:
        LT = Lq // 128
        FTn = LT + 1
        TBL = min(512, Lq)
        NTB = Lq // TBL
        f = np.arange(FTn * 128, dtype=np.float64)
        t = np.arange(Lq, dtype=np.float64)
        valid = (f <= Lq).astype(np.float64)
        ang = np.pi * np.outer(t, f) / Lq
        Cf = np.cos(ang) * valid[None, :]
        Sf = np.sin(ang) * valid[None, :]
        lay = lambda X: np.ascontiguousarray(X.reshape(LT, 128, FTn, 128).transpose(2, 1, 0, 3)).astype(np.float32)
        c["hyCf%d" % Lq] = lay(Cf)
        c["hySf%d" % Lq] = lay(Sf)
        wf = np.where((f == 0) | (f == Lq), 1.0, 2.0) * valid / (2.0 * Lq)
        Ci = (np.cos(ang) * wf[None, :]).T
        Si = (np.sin(ang) * wf[None, :]).T
        layi = lambda X: np.ascontiguousarray(X.reshape(FTn, 128, NTB, TBL).transpose(2, 1, 0, 3)).astype(np.float32)
        c["hyCi%d" % Lq] = layi(Ci)
        c["hySi%d" % Lq] = layi(Si)
        tt = np.linspace(0.0, 1.0, Lq, dtype=np.float32)[:, None]
        bands = 16
        t_res = np.arange(Lq, dtype=np.float32)[:, None]
        fr = np.linspace(1e-4, bands - 1, bands, dtype=np.float32)[None, :]
        a = (2.0 * math.pi * t_res * fr / Lq).astype(np.float32)
        z = np.concatenate([tt, np.cos(a), np.sin(a)], -1).astype(np.float32)
        c["hyz%d" % Lq] = np.ascontiguousarray(z.T)
        mn, mx = math.log(1e-2) / 1.5, math.log(1e-2) / 0.3
        deltas = np.abs(np.linspace(mn, mx, HW, dtype=np.float32))
        env = np.exp(-tt * deltas[None, :]).astype(np.float32)
        c["hyenv%d" % Lq] = np.ascontiguousarray(env.reshape(LT, 128, HW).transpose(1, 0, 2))


def wrap_pi(p, x, xB, tmp, tmpB, n=2):
    for _ in range(n):
        p.ins("dve", lambda e: e.tensor_scalar(tmp, x, -PI, 2 * PI, ALU.is_lt, ALU.mult), rd=[xB], wr=[tmpB])
        p.ins("dve", lambda e: e.tensor_tensor(x, x, tmp, ALU.add), rd=[xB, tmpB], wr=[xB])
        p.ins("dve", lambda e: e.tensor_scalar(tmp, x, PI, 2 * PI, ALU.is_gt, ALU.mult), rd=[xB], wr=[tmpB])
        p.ins("dve", lambda e: e.tensor_tensor(x, x, tmp, ALU.subtract), rd=[xB, tmpB], wr=[xB])


def spectrum(cx, G, X, XB, Lq, cst, which, tabr, sink):
    p = cx.p
    LT = Lq // 128
    for ft in range(LT + 1):
        for cs in which:
            tb, tbB = tabr.next()
            p.dma("pool", tb[:, 0:LT, :], cst["hy%sf%d" % (cs, Lq)][ft], wr=[tbB], max_dma_last_dim=4096)
            ps, pb = G.ps.next()
            mm(p, ps[:], [(tb[:, tt, :], X[:, tt, :]) for tt in range(LT)], rd=[tbB, XB], wr=[pb])
            sink(ft, cs, ps, pb)


def hyena_filters(cx, G, L, W, Lq, cst, Kd):
    p = cx.p
    LT = Lq // 128
    CH = min(512, Lq)
    with ExitStack() as es:
        zT, zB = cx.sb(es, [33, Lq], F32, "hzT")
        p.dma("sp", zT[:], cst["hyz%d" % Lq], wr=[zB])
        w1, w1B = cx.sb(es, [33, 64], F32, "hw1")
        p.dma("sp", w1[:], W["hy_f_w1"][L], wr=[w1B])
        w2, w2B = cx.sb(es, [64, 64], F32, "hw2")
        p.dma("sp", w2[:], W["hy_f_w2"][L], wr=[w2B])
        w3, w3B = cx.sb(es, [64, 2048], F32, "hw3")
        p.dma("sp", w3[:], W["hy_f_w3"][L], wr=[w3B])
        sc, scB = cx.sb(es, [64, 4], F32, "hsc")
        for i, nm in enumerate(("hy_f_b1", "hy_f_freq1", "hy_f_b2", "hy_f_freq2")):
            p.dma("sp", sc[:, i:i + 1], W[nm][L].rearrange("(p o) -> p o", o=1), wr=[scB])
        hm1, h1B = cx.sb(es, [64, Lq], F32, "hm1")
        hm2, h2B = cx.sb(es, [64, Lq], F32, "hm2")
        tmp, tmpB = cx.sb(es, [64, CH], F32, "hwtmp")
        for (src, sB, wgt, wB, dst, dB, bi) in ((zT, zB, w1, w1B, hm1, h1B, 0), (hm1, h1B, w2, w2B, hm2, h2B, 2)):
            for c0 in range(0, Lq, CH):
                ps, pb = G.ps.next()
                mm(p, ps[0:64, 0:CH], [(wgt[:], src[:, c0:c0 + CH])], rd=[wB, sB], wr=[pb])
                d = dst[:, c0:c0 + CH]
                p.ins("dve", lambda e, d=d, ps=ps, bi=bi: e.tensor_scalar(d, ps[0:64, 0:CH], sc[:, bi:bi + 1], sc[:, bi + 1:bi + 2], ALU.add, ALU.mult),
                      rd=[pb, scB], wr=[dB])
                wrap_pi(p, d, dB, tmp[:], tmpB)
                p.ins("act", lambda e, d=d: e.activation(d, d, AF.Sin), rd=[dB], wr=[dB])
        env, envB = cx.sb(es, [128, LT, HW], F32, "henv")
        p.dma("sp", env[:], cst["hyenv%d" % Lq], wr=[envB])
        PM = [cx.sb(es, [128, LT, HW], BF16, "hPM%d" % i) for i in range(4)]
        hfb = [cx.sb(es, [128, HW], F32, "hfb%d" % i) for i in range(4)]
        for lt in range(LT):
            for cb in range(4):
                ps, pb = G.ps.next()
                mm(p, ps[:], [(hm2[:, lt * 128:(lt + 1) * 128], w3[:, cb * 512:(cb + 1) * 512])], rd=[h2B, w3B], wr=[pb])
                hb, hbB = hfb[cb]
                p.ins("dve", lambda e, hb=hb, ps=ps, lt=lt: e.tensor_tensor(hb[:], ps[:], env[:, lt, :], ALU.mult), rd=[pb, envB], wr=[hbB])
                if lt == 0 and cb % 2 == 1:
                    p.ins("dve", lambda e, hb=hb: e.memset(hb[0:1, :], 0.0), rd=[], wr=[hbB])
            for n in range(2):
                (hc, hcB), (ha, haB) = hfb[2 * n], hfb[2 * n + 1]
                p.ins("pool", lambda e, hc=hc, ha=ha, n=n, lt=lt: e.tensor_tensor(PM[2 * n][0][:, lt, :], hc[:], ha[:], ALU.add),
                      rd=[hcB, haB], wr=[PM[2 * n][1]])
                p.ins("pool", lambda e, hc=hc, ha=ha, n=n, lt=lt: e.tensor_tensor(PM[2 * n + 1][0][:, lt, :], hc[:], ha[:], ALU.subtract),
                      rd=[hcB, haB], wr=[PM[2 * n + 1][1]])
        tabr = Ring([cx.sb(es, [128, LT, 128], BF16, "hftab") for _ in range(3)])
        stg = Ring([cx.sb(es, [128, HW], F32, "hkst") for _ in range(3)])
        for n in range(2):
            for ci, cs in enumerate(("C", "S")):
                X, XB = PM[2 * n + ci]

                def sink(ft, cs_, ps, pb, n=n, ci=ci):
                    st, sB_ = stg.next()
                    p.ins("act", lambda e: e.activation(st[:], ps[:], AF.Copy), rd=[pb], wr=[sB_])
                    p.dma("sp", Kd[n, ci, ft], st[:], rd=[sB_])
                spectrum(cx, G, X, XB, Lq, cst, [cs], tabr, sink)
    p.barrier()


def hyena_short(cx, G, L, W, projT, hyT):
    p = cx.p
    GL = T + 6
    with ExitStack() as es:
        sw, swB = cx.sb(es, [128, 3, 12], F32, "hsw")
        for k in range(3):
            vecT(cx, G, es, rows128(W["hy_short_w"][L, k]), 12, sw[:, k, :], swB)
        ur = Ring([cx.sb(es, [128, GL], F32, "hu") for _ in range(2)])
        cr = Ring([cx.sb(es, [128, GL], F32, "hc") for _ in range(2)])
        for (t_, b_) in ur.items:
            p.ins("pool", lambda e, t_=t_: e.memset(t_[:], 0.0), wr=[b_])
        for i in range(12):
            u, uB = ur.next()
            c, cB = cr.next()
            for s, (a, n) in enumerate(SEGS):
                p.dma("sp", u[:, a + s + 1:a + n + s + 1], projT[24 + i, :, a:a + n], wr=[uB])
            p.ins("dve", lambda e: e.tensor_scalar(c[:, 1:GL - 1], u[:, 1:GL - 1], sw[:, 1, i:i + 1], None, ALU.mult), rd=[uB, swB], wr=[cB])
            p.ins("dve", lambda e: e.scalar_tensor_tensor(c[:, 1:GL - 1], u[:, 0:GL - 2], sw[:, 0, i:i + 1], c[:, 1:GL - 1], ALU.mult, ALU.add),
                  rd=[uB, swB, cB], wr=[cB])
            p.ins("dve", lambda e: e.scalar_tensor_tensor(c[:, 1:GL - 1], u[:, 2:GL], sw[:, 2, i:i + 1], c[:, 1:GL - 1], ALU.mult, ALU.add),
                  rd=[uB, swB, cB], wr=[cB])
            for s, (a, n) in enumerate(SEGS):
                p.dma("sp", hyT[i, :, a:a + n], c[:, a + s + 1:a + n + s + 1], rd=[cB])
    p.barrier()


def hyena_seq(cx, G, L, W, Lq, tok0, hyT, Kd, mixT, cst, biasT, biasB):
    p = cx.p
    LT = Lq // 128
    FTn = LT + 1
    TBL = min(512, Lq)
    NTB = Lq // TBL
    with ExitStack() as es:
        zf = [cx.sb(es, [128, 4, Lq], F32, "hzf%d" % i) for i in range(2)]
        Z, ZB = cx.sb(es, [128, LT, HW], BF16, "hZ")
        Yc, YcB = cx.sb(es, [128, FTn, HW], BF16, "hYc")
        Ys, YsB = cx.sb(es, [128, FTn, HW], BF16, "hYs")
        tabr = Ring([cx.sb(es, [128, LT, 128], BF16, "hftab") for _ in range(4)])
        itab = [cx.sb(es, [128, FTn, TBL], BF16, "hitab%d" % i) for i in range(2)]
        kt = Ring([(cx.sb(es, [128, HW], F32, "hKc"), cx.sb(es, [128, HW], F32, "hKs")) for _ in range(2)])
        tmps = Ring([cx.sb(es, [128, HW], F32, "htmp") for _ in range(4)])
        xr = Ring([cx.sb(es, [128, TBL], F32, "hx") for _ in range(2)])
        ob, obB = cx.sb(es, [128, TBL], BF16, "hob")
        cur, curB = zf[0]
        for ct in range(4):
            p.dma("sp", cur[:, ct, :], hyT[ct, :, tok0:tok0 + Lq], wr=[curB])
        for n in range(2):
            cur, curB = zf[n % 2]
            nxt, nxtB = zf[(n + 1) % 2]
            for tt in range(LT):
                ps, pb = G.ps.next()
                for ct in range(4):
                    mm(p, ps[:, ct * 128:(ct + 1) * 128], [(cur[:, ct, tt * 128:(tt + 1) * 128], G.ident[:])], rd=[curB, G.identB], wr=[pb])
                p.ins("act", lambda e, tt=tt, ps=ps: e.activation(Z[:, tt, :], ps[:], AF.Copy), rd=[pb], wr=[ZB])
            state = {}

            def sink(ft, cs, ps, pb, n=n):
                if cs == "C":
                    state["c"] = (ps, pb)
                    return
                (psc, pbc), (pss, pbs) = state["c"], (ps, pb)
                (Kc, KcB), (Ks, KsB) = kt.next()
                p.dma("sp", Kc[:], Kd[n, 0, ft], wr=[KcB])
                p.dma("sp", Ks[:], Kd[n, 1, ft], wr=[KsB])
                (t1, t1B), (t2, t2B), (t3, t3B), (t4, t4B) = tmps.next(), tmps.next(), tmps.next(), tmps.next()
                p.ins("dve", lambda e: e.tensor_tensor(t1[:], psc[:], Kc[:], ALU.mult), rd=[pbc, KcB], wr=[t1B])
                p.ins("dve", lambda e: e.tensor_tensor(t2[:], pss[:], Ks[:], ALU.mult), rd=[pbs, KsB], wr=[t2B])
                p.ins("dve", lambda e: e.tensor_tensor(t3[:], psc[:], Ks[:], ALU.mult), rd=[pbc, KsB], wr=[t3B])
                p.ins("dve", lambda e: e.tensor_tensor(t4[:], pss[:], Kc[:], ALU.mult), rd=[pbs, KcB], wr=[t4B])
                p.ins("pool", lambda e: e.tensor_tensor(Yc[:, ft, :], t1[:], t2[:], ALU.subtract), rd=[t1B, t2B], wr=[YcB])
                p.ins("pool", lambda e: e.tensor_tensor(Ys[:, ft, :], t3[:], t4[:], ALU.add), rd=[t3B, t4B], wr=[YsB])
            spectrum(cx, G, Z, ZB, Lq, cst, ["C", "S"], tabr, sink)
            for tb in range(NTB):
                (Ci, CiB), (Si, SiB) = itab
                p.dma("pool", Ci[:], cst["hyCi%d" % Lq][tb], wr=[CiB], max_dma_last_dim=4096)
                p.dma("pool", Si[:], cst["hySi%d" % Lq][tb], wr=[SiB], max_dma_last_dim=4096)
                for ct in range(4):
                    ps, pb = G.ps.next()
                    pairs = [(Yc[:, ft, ct * 128:(ct + 1) * 128], Ci[:, ft, :]) for ft in range(FTn)] + \
                            [(Ys[:, ft, ct * 128:(ct + 1) * 128], Si[:, ft, :]) for ft in range(FTn)]
                    mm(p, ps[:, 0:TBL], pairs, rd=[YcB, YsB, CiB, SiB], wr=[pb])
                    x, xB = xr.next()
                    p.dma("sp", x[:], hyT[4 * (n + 1) + ct, :, tok0 + tb * TBL:tok0 + (tb + 1) * TBL], wr=[xB])
                    (t1, t1B) = tmps.next()
                    zsl = cur[:, ct, tb * TBL:(tb + 1) * TBL]
                    p.ins("dve", lambda e, t1=t1, zsl=zsl, ps=ps, ct=ct, n=n: e.scalar_tensor_tensor(t1[:, 0:TBL], zsl, biasT[:, n * 4 + ct:n * 4 + ct + 1],
                                                                                                 ps[:, 0:TBL], ALU.mult, ALU.add),
                          rd=[curB, pb, biasB], wr=[t1B])
                    if n == 0:
                        p.ins("pool", lambda e, t1=t1, x=x, ct=ct, tb=tb: e.tensor_tensor(nxt[:, ct, tb * TBL:(tb + 1) * TBL], t1[:, 0:TBL], x[:], ALU.mult),
                              rd=[t1B, xB], wr=[nxtB])
                    else:
                        p.ins("pool", lambda e, t1=t1, x=x: e.tensor_tensor(ob[:], t1[:, 0:TBL], x[:], ALU.mult), rd=[t1B, xB], wr=[obB])
                        p.dma("sp", mixT[6 + ct, :, tok0 + tb * TBL:tok0 + (tb + 1) * TBL], ob[:], rd=[obB])


def phase_hyena(cx, G, L, W, projT, mixT, cst):
    p = cx.p
    hyT = G.hyT
    hyena_short(cx, G, L, W, projT, hyT)
    with ExitStack() as es:
        biasT, biasB = cx.sb(es, [128, 8], F32, "hbias")
        vecT(cx, G, es, rows128(W["hy_bias"][L]), 8, biasT[:], biasB)
        for Lq in HY_LS:
            hyena_filters(cx, G, L, W, Lq, cst, G.Kd[Lq])
        for (tok0, Ls) in SEGS:
            hyena_seq(cx, G, L, W, Ls, tok0, hyT, G.Kd[Ls], mixT, cst, biasT, biasB)
            p.barrier()


def rwkv_consts(c):
    i = np.arange(128)[:, None]
    t = np.arange(128)[None, :]
    sf, inf_ = (i < t).astype(np.float32), (i <= t).astype(np.float32)
    sb_, inb = (i > t).astype(np.float32), (i >= t).astype(np.float32)
    c["rmaskF"] = np.concatenate([sf, inf_, sf, inf_], 1)
    c["rmaskB"] = np.concatenate([sb_, inb, sb_, inb], 1)
    bo = np.zeros((128, 128), np.float32)
    bo[:64, :64] = 1
    bo[64:, 64:] = 1
    c["blockones"] = bo
    hi = np.zeros((128, 2), np.float32)
    hi[:64, 0] = 1
    hi[64:, 1] = 1
    c["hind"] = hi


def phase_rwkv(cx, G, L, W, projT, projTok, loraT, loraTB, mixT, st_in, new_st, cst):
    p = cx.p
    nc = cx.nc
    EM = math.exp(-0.5)
    banks = G.banks
    ps4 = Ring(banks[0:4])
    quarters = []
    for j in range(4):
        for (bt, bb) in banks[4:8]:
            quarters.append((bt[:, j * 128:(j + 1) * 128], bb))
    psq = Ring(quarters)
    with ExitStack() as es:
        def arr(name, dt=F32, n=TS):
            return cx.sb(es, [128, n], dt, name)
        maskF, mFB = cx.sb(es, [128, 512], F32, "rmF")
        maskB, mBB = cx.sb(es, [128, 512], F32, "rmB")
        p.dma("sp", maskF[:], cst["rmaskF"], wr=[mFB])
        p.dma("sp", maskB[:], cst["rmaskB"], wr=[mBB])
        bones, boB = cx.sb(es, [128, 128], F32, "bones")
        p.dma("sp", bones[:], cst["blockones"], wr=[boB])
        hind, hiB = cx.sb(es, [128, 2], F32, "hind")
        p.dma("sp", hind[:], cst["hind"], wr=[hiB])
        w2a, w2B = cx.sb(es, [128, RW], BF16, "w2a")
        a2a, a2B = cx.sb(es, [128, RW], BF16, "a2a")
        p.dma("pool", w2a[:], W["rwkv_w2"][L].rearrange("d k n -> (d k) n"), wr=[w2B])
        p.dma("pool", a2a[:], W["rwkv_a2"][L].rearrange("d k n -> (d k) n"), wr=[a2B])
        pv, pvB = cx.sb(es, [128, 48], F32, "rpv")
        for d in range(2):
            vecT(cx, G, es, rows128(W["rwkv_w0"][L, d]), 6, pv[:, d * 6:d * 6 + 6], pvB)
            vecT(cx, G, es, rows128(W["rwkv_a0"][L, d]), 6, pv[:, 12 + d * 6:12 + d * 6 + 6], pvB)
        vecT(cx, G, es, rows128(W["rwkv_k_k"][L]), 6, pv[:, 24:30], pvB)
        vecT(cx, G, es, rows128(W["rwkv_k_a"][L]), 6, pv[:, 30:36], pvB)
        vecT(cx, G, es, rows128(W["rwkv_r_k"][L]), 6, pv[:, 42:48], pvB)
        p.ins("dve", lambda e: e.tensor_scalar(pv[:, 36:42], pv[:, 30:36], -1.0, 1.0, ALU.mult, ALU.add), rd=[pvB], wr=[pvB])
        lg, lgB = cx.sb(es, [128, 128], F32, "lnxg")
        lb, lbB = cx.sb(es, [128, 128], F32, "lnxb")
        R, RB = arr("R")
        Kf, KfB = arr("Kf")
        KAP, KAPB = arr("KAP")
        KDS, KDSB = arr("KDS")
        A1, A1B = arr("A1")
        A2, A2B = arr("A2")
        A3, A3B = arr("A3")
        A4, A4B = arr("A4")
        A5, A5B = arr("A5")
        A6, A6B = arr("A6")
        A7, A7B = arr("A7")
        A8, A8B = arr("A8")
        comb, combB = cx.sb(es, [128, 16, 2, 128], BF16, "comb")
        BH, BHB = arr("BH", BF16)
        KH, KHB = arr("KH", BF16)
        BTOK, BTB = cx.sb(es, [128, 16, 128], BF16, "BTOK")
        KTOK, KTB = cx.sb(es, [128, 16, 128], BF16, "KTOK")
        VTOK, VTB = cx.sb(es, [128, 16, 128], BF16, "VTOK")
        OSUM, OSB = cx.sb(es, [128, 16, 128], F32, "OSUM")
        ATs = [[cx.sb(es, [128, 384], BF16, "ATs") for h in range(2)] for c in range(16)]
        TTb = [[cx.sb(es, [128, 128], BF16, "TTb") for h in range(2)] for c in range(16)]
        NB = 5
        chain = [{k: [cx.sb(es, [128, 128], (F32 if k == "TT" else BF16), "ch" + k) for _ in range(2)] for k in ("N", "NT", "TT", "TS")} for _ in range(NB)]
        cols, colsB = cx.sb(es, [128, 4, 16], F32, "cols")
        sin1 = cx.sb(es, [64, 128], F32, "sin")
        sb2 = [cx.sb(es, [128, 64], BF16, "Sb2") for _ in range(4)]
        seqbufs = [(cx.sb(es, [128, 64], F32, "S"), cx.sb(es, [128, 64], BF16, "Sb"), sin1,
                    cx.sb(es, [128, 128], BF16, "Xs"), cx.sb(es, [128, 128], BF16, "SAs")) for _ in range(4)]
        st4 = cx.sb(es, [128, 64], F32, "gst")
        bs_s, bsB = cx.sb(es, [128, 32], F32, "bs")

        def v3(a, n):
            return a[:, 0:n * 128].rearrange("p (c t) -> p c t", t=128)

        RSTOP = 9
        RSUB = 99
        RUNIT_DEFS = [(0, 1024, [(0, 256, 0), (256, 256, 1), (512, 256, 2), (768, 256, 3)], False),
                      (1024, 2048, [(0, 2048, None)], True)]
        for (tok0, Ls, seqs, sample) in RUNIT_DEFS:
            nch = Ls // 128
            CH = min(512, Ls)
            for hp in range(6):
                p.dma("sp", R[:, 0:Ls], projT[hp, :, tok0:tok0 + Ls], wr=[RB])
                p.dma("sp", Kf[:, 0:Ls], projT[6 + hp, :, tok0:tok0 + Ls], wr=[KfB])
                p.dma("sp", v3(A3, nch), projTok[tok0:tok0 + Ls, hp * 128:(hp + 1) * 128].rearrange("(c p) f -> p c f", p=128), wr=[A3B])
                p.ins("act", lambda e: e.activation(VTOK[:, 0:nch, :], v3(A3, nch), AF.Copy), rd=[A3B], wr=[VTB])
                p.ins("dve", lambda e: e.tensor_scalar(A7[:, 0:Ls], Kf[:, 0:Ls], pv[:, 24 + hp:25 + hp], None, ALU.mult), rd=[KfB, pvB], wr=[A7B])
                p.ins("pool", lambda e: e.tensor_tensor(A8[:, 0:Ls], A7[:, 0:Ls], A7[:, 0:Ls], ALU.mult), rd=[A7B], wr=[A8B])
                for c0 in range(0, Ls, CH):
                    ps, pb = ps4.next()
                    mm(p, ps[:, 0:CH], [(bones[:], A8[:, c0:c0 + CH])], rd=[boB, A8B], wr=[pb])
                    p.ins("dve", lambda e, ps=ps, c0=c0: e.tensor_scalar(KAP[:, c0:c0 + CH], ps[:, 0:CH], 1e-30, None, ALU.add), rd=[pb], wr=[KAPB])
                p.ins("act", lambda e: e.activation(KAP[:, 0:Ls], KAP[:, 0:Ls], AF.Sqrt), rd=[KAPB], wr=[KAPB])
                p.ins("dve", lambda e: e.reciprocal(KAP[:, 0:Ls], KAP[:, 0:Ls]), rd=[KAPB], wr=[KAPB])
                p.ins("dve", lambda e: e.tensor_tensor(KAP[:, 0:Ls], KAP[:, 0:Ls], A7[:, 0:Ls], ALU.mult), rd=[KAPB, A7B], wr=[KAPB])
                for d in range(2):
                    if RSTOP < 1:
                        break
                    dr = slice(64 * d, 64 * d + 64)
                    mask, mB = (maskF, mFB) if d == 0 else (maskB, mBB)
                    maskN, mNB = (maskB, mBB) if d == 0 else (maskF, mFB)
                    for c0 in range(0, Ls, CH):
                        ps, pb = ps4.next()
                        mm(p, ps[:, 0:CH], [(w2a[dr, hp * 128:(hp + 1) * 128], loraT[dr, 0, tok0 + c0:tok0 + c0 + CH])], rd=[w2B, loraTB], wr=[pb])
                        p.ins("act", lambda e, ps=ps, c0=c0: e.activation(A1[:, c0:c0 + CH], ps[:, 0:CH], AF.Sigmoid, bias=pv[:, d * 6 + hp:d * 6 + hp + 1]),
                              rd=[pb, pvB], wr=[A1B])
                        ps, pb = ps4.next()
                        mm(p, ps[:, 0:CH], [(a2a[dr, hp * 128:(hp + 1) * 128], loraT[dr, 1, tok0 + c0:tok0 + c0 + CH])], rd=[a2B, loraTB], wr=[pb])
                        p.ins("act", lambda e, ps=ps, c0=c0: e.activation(A4[:, c0:c0 + CH], ps[:, 0:CH], AF.Sigmoid, bias=pv[:, 12 + d * 6 + hp:12 + d * 6 + hp + 1]),
                              rd=[pb, pvB], wr=[A4B])
                    if RSUB <= 1:
                        continue
                    p.ins("dve", lambda e: e.tensor_scalar(A1[:, 0:Ls], A1[:, 0:Ls], -EM, None, ALU.mult), rd=[A1B], wr=[A1B])
                    for (sa_, sn_, _) in seqs:
                        p.ins("dve", lambda e, sa_=sa_, sn_=sn_: e.tensor_tensor_scan(A2[:, sa_:sa_ + sn_], A1[:, sa_:sa_ + sn_], A1[:, sa_:sa_ + sn_], 0.0, ALU.add, ALU.min),
                              rd=[A1B], wr=[A2B])
                    p.ins("pool", lambda e: e.tensor_tensor(A3[:, 0:Ls], A2[:, 0:Ls], A1[:, 0:Ls], ALU.subtract), rd=[A2B, A1B], wr=[A3B])
                    p.ins("dve", lambda e: e.tensor_copy(cols[:, 0, 0:nch], v3(A3, nch)[:, :, 0]), rd=[A3B], wr=[colsB])
                    p.ins("dve", lambda e: e.tensor_copy(cols[:, 1, 0:nch], v3(A2, nch)[:, :, 127]), rd=[A2B], wr=[colsB])
                    p.ins("dve", lambda e: e.tensor_scalar(cols[:, 2:4, 0:nch], cols[:, 0:2, 0:nch], -1.0, None, ALU.mult), rd=[colsB], wr=[colsB])
                    if RSUB <= 2:
                        continue
                    p.ins("dve", lambda e: e.tensor_scalar(A5[:, 0:Ls], A4[:, 0:Ls], pv[:, 30 + hp:31 + hp], pv[:, 36 + hp:37 + hp], ALU.mult, ALU.add),
                          rd=[A4B, pvB], wr=[A5B])
                    p.ins("pool", lambda e: e.tensor_tensor(A5[:, 0:Ls], A5[:, 0:Ls], Kf[:, 0:Ls], ALU.mult), rd=[A5B, KfB], wr=[A5B])
                    if d == 0:
                        p.ins("pool", lambda e: e.tensor_copy(KDS[:, 0:Ls], A5[:, 0:Ls]), rd=[A5B], wr=[KDSB])
                    else:
                        p.ins("pool", lambda e: e.tensor_tensor(KDS[:, 0:Ls], KDS[:, 0:Ls], A5[:, 0:Ls], ALU.add), rd=[A5B, KDSB], wr=[KDSB])
                    p.ins("dve", lambda e: e.tensor_tensor(A6[:, 0:Ls], KAP[:, 0:Ls], A4[:, 0:Ls], ALU.mult), rd=[KAPB, A4B], wr=[A6B])
                    if RSUB <= 3:
                        continue
                    CS, CE, NCS, NCE = 0, 1, 2, 3
                    if d == 0:
                        spec = ((A1, A1B, A2, A2B, 1.0, NCS), (A4, A4B, A3, A3B, 1.0, NCS), (A7, A7B, A2, A2B, -1.0, CE), (A8, A8B, A2, A2B, -1.0, CS))
                    else:
                        spec = ((A1, A1B, A3, A3B, -1.0, CE), (A4, A4B, A2, A2B, -1.0, CE), (A7, A7B, A3, A3B, 1.0, NCS), (A8, A8B, A3, A3B, 1.0, NCE))
                    for (dst, dB, src, sB_, scl, ci) in spec:
                        cbc = cols[:, ci, 0:nch].unsqueeze(2).to_broadcast([128, nch, 128])
                        p.ins("dve", lambda e, dst=dst, src=src, scl=scl, cbc=cbc: e.scalar_tensor_tensor(v3(dst, nch), v3(src, nch), scl, cbc, ALU.mult, ALU.add),
                              rd=[sB_, colsB], wr=[dB])
                        p.ins("act", lambda e, dst=dst: e.activation(dst[:, 0:Ls], dst[:, 0:Ls], AF.Exp), rd=[dB], wr=[dB])
                    if RSUB <= 4:
                        continue
                    p.ins("dve", lambda e: e.scalar_tensor_tensor(comb[:, 0:nch, 0, :], v3(KAP, nch), -1.0, v3(A4, nch), ALU.mult, ALU.mult), rd=[KAPB, A4B], wr=[combB])
                    p.ins("pool", lambda e: e.tensor_tensor(comb[:, 0:nch, 1, :], v3(R, nch), v3(A1, nch), ALU.mult), rd=[RB, A1B], wr=[combB])
                    p.ins("dve", lambda e: e.tensor_tensor(BH[:, 0:Ls], A6[:, 0:Ls], A8[:, 0:Ls], ALU.mult), rd=[A6B, A8B], wr=[BHB])
                    p.ins("pool", lambda e: e.tensor_tensor(KH[:, 0:Ls], A5[:, 0:Ls], A8[:, 0:Ls], ALU.mult), rd=[A5B, A8B], wr=[KHB])
                    p.ins("dve", lambda e: e.tensor_tensor(A6[:, 0:Ls], A6[:, 0:Ls], A7[:, 0:Ls], ALU.mult), rd=[A6B, A7B], wr=[A6B])
                    p.ins("pool", lambda e: e.tensor_tensor(A5[:, 0:Ls], A5[:, 0:Ls], A7[:, 0:Ls], ALU.mult), rd=[A5B, A7B], wr=[A5B])
                    if RSUB <= 5:
                        continue
                    for c in range(nch):
                        cs_ = slice(c * 128, (c + 1) * 128)
                        for (src, sB_, dst, dB) in ((A6, A6B, BTOK, BTB), (A5, A5B, KTOK, KTB)):
                            q, qB = psq.next()
                            mm(p, q, [(src[:, cs_], G.ident[:])], rd=[sB_, G.identB], wr=[qB])
                            p.ins("act", lambda e, dst=dst, q=q, c=c: e.activation(dst[:, c, :], q, AF.Copy), rd=[qB], wr=[dB])
                    if RSTOP < 2:
                        continue
                    units = [(c, h) for c in range(nch) for h in range(2)]
                    for b0 in range(0, len(units), NB):
                        batch = units[b0:b0 + NB]
                        for ui, (c, h) in enumerate(batch):
                            hr = slice(64 * h, 64 * h + 64)
                            cs_ = slice(c * 128, (c + 1) * 128)
                            ch_ = chain[ui]
                            psA, pAB = ps4.next()
                            rhs = comb[hr, c, :, :].rearrange("p a t -> p (a t)")
                            mm(p, psA[:, 0:256], [(BH[hr, cs_], rhs)], rd=[BHB, combB], wr=[pAB])
                            mm(p, psA[:, 256:512], [(KH[hr, cs_], rhs)], rd=[KHB, combB], wr=[pAB])
                            (NT0, NT0B) = ch_["NT"][0]
                            p.ins("dve", lambda e, NT0=NT0, psA=psA: e.tensor_tensor(NT0[:], psA[:, 0:128], mask[:, 0:128], ALU.mult), rd=[pAB, mB], wr=[NT0B])
                            at, atB = ATs[c][h]
                            p.ins("dve", lambda e, at=at, psA=psA: e.tensor_tensor(at[:], psA[:, 128:512], mask[:, 128:512], ALU.mult), rd=[pAB, mB], wr=[atB])
                            q, qB = psq.next()
                            mm(p, q, [(comb[hr, c, 0, :], BH[hr, cs_])], rd=[BHB, combB], wr=[qB])
                            (N0, N0B) = ch_["N"][0]
                            p.ins("dve", lambda e, N0=N0, q=q: e.tensor_tensor(N0[:], q, maskN[:, 0:128], ALU.mult), rd=[qB, mNB], wr=[N0B])
                            (TT0, TT0B) = ch_["TT"][0]
                            (TS0, TS0B) = ch_["TS"][0]
                            p.ins("pool", lambda e, TT0=TT0, NT0=NT0: e.tensor_tensor(TT0[:], NT0[:], G.ident[:], ALU.add), rd=[NT0B, G.identB], wr=[TT0B])
                            p.ins("pool", lambda e, TS0=TS0, NT0=NT0: e.tensor_tensor(TS0[:], NT0[:], G.ident[:], ALU.add), rd=[NT0B, G.identB], wr=[TS0B])
                        for j in range(1, 7):
                            a_, b_ = (j - 1) % 2, j % 2
                            for ui, (c, h) in enumerate(batch):
                                ch_ = chain[ui]
                                (Np, NpB), (Nn, NnB) = ch_["N"][a_], ch_["N"][b_]
                                (NTp, NTpB), (NTn, NTnB) = ch_["NT"][a_], ch_["NT"][b_]
                                q, qB = psq.next()
                                mm(p, q, [(NTp[:], Np[:])], rd=[NTpB, NpB], wr=[qB])
                                p.ins("act", lambda e, Nn=Nn, q=q: e.activation(Nn[:], q, AF.Copy), rd=[qB], wr=[NnB])
                                if j < 6:
                                    q2, q2B = psq.next()
                                    mm(p, q2, [(Np[:], NTp[:])], rd=[NTpB, NpB], wr=[q2B])
                                    p.ins("act", lambda e, NTn=NTn, q2=q2: e.activation(NTn[:], q2, AF.Copy), rd=[q2B], wr=[NTnB])
                            for ui, (c, h) in enumerate(batch):
                                ch_ = chain[ui]
                                (Nn, NnB) = ch_["N"][b_]
                                (TTp, TTpB), (TTn, TTnB) = ch_["TT"][a_], ch_["TT"][b_]
                                (TSp, TSpB), (TSn, TSnB) = ch_["TS"][a_], ch_["TS"][b_]
                                q, qB = psq.next()
                                mm(p, q, [(Nn[:], TSp[:])], rd=[NnB, TSpB], wr=[qB])
                                if j < 6:
                                    p.ins("dve", lambda e, TTn=TTn, q=q, TTp=TTp: e.tensor_tensor(TTn[:], q, TTp[:], ALU.add), rd=[qB, TTpB], wr=[TTnB])
                                    p.ins("pool", lambda e, TSn=TSn, TTn=TTn: e.tensor_copy(TSn[:], TTn[:]), rd=[TTnB], wr=[TSnB])
                                else:
                                    tb_, tbB = TTb[c][h]
                                    p.ins("dve", lambda e, tb_=tb_, q=q, TTp=TTp: e.tensor_tensor(tb_[:], q, TTp[:], ALU.add), rd=[qB, TTpB], wr=[tbB])
                    if RSTOP < 3:
                        continue
                    chains = []
                    for qi, (sa_, sn_, sidx) in enumerate(seqs):
                        (S, SB), (Sb, SbB), (sin_, sinB), (Xs, XsB), (SAs, SAsB) = seqbufs[qi]
                        if sample:
                            p.dma("sp", sin_[:].rearrange("v (h k) -> v h k", h=2), st_in[L, d, 2 * hp:2 * hp + 2].rearrange("h v k -> v h k"), wr=[sinB])
                            q, qB = psq.next()
                            mm(p, q[:, 0:64], [(sin_[:], G.ident[0:64, 0:64])], rd=[sinB, G.identB], wr=[qB])
                            p.ins("dve", lambda e, q=q, S=S: e.tensor_copy(S[:], q[:, 0:64]), rd=[qB], wr=[SB])
                        else:
                            p.ins("dve", lambda e, S=S: e.memset(S[:], 0.0), wr=[SB])
                        p.ins("act", lambda e, S=S, Sb=Sb: e.activation(Sb[:], S[:], AF.Copy), rd=[SB], wr=[SbB])
                        c0_, cn_ = sa_ // 128, sn_ // 128
                        order = list(range(c0_, c0_ + cn_)) if d == 0 else list(range(c0_ + cn_ - 1, c0_ - 1, -1))
                        chains.append((qi, sidx, order))
                    for step in range(max(len(o_) for (_, _, o_) in chains)):
                        for (qi, sidx, order) in chains:
                            if step >= len(order):
                                continue
                            c = order[step]
                            (S, SB), (Sb0, Sb0B), (sin_, sinB), (Xs, XsB), (SAs, SAsB) = seqbufs[qi]
                            (Sb1, Sb1B) = sb2[qi]
                            (Sb, SbB), (Sbn, SbnB) = ((Sb0, Sb0B), (Sb1, Sb1B)) if step % 2 == 0 else ((Sb1, Sb1B), (Sb0, Sb0B))
                            Xq, XqB = psq.next()
                            for h in range(2):
                                hr = slice(64 * h, 64 * h + 64)
                                hc = slice(64 * h, 64 * h + 64)
                                at, atB = ATs[c][h]
                                mm(p, Xq[:, hc], [(comb[hr, c, 0, :], Sb[hr, :]), (at[:, 128:256], VTOK[:, c, hc])], rd=[combB, SbB, atB, VTB], wr=[XqB])
                            p.ins("act", lambda e, Xq=Xq, Xs=Xs: e.activation(Xs[:], Xq, AF.Copy), rd=[XqB], wr=[XsB])
                            Sq, SqB = psq.next()
                            for h in range(2):
                                hc = slice(64 * h, 64 * h + 64)
                                tb_, tbB = TTb[c][h]
                                mm(p, Sq[:, hc], [(tb_[:], Xs[:, hc])], rd=[tbB, XsB], wr=[SqB])
                            p.ins("dve", lambda e, Sq=Sq, SAs=SAs: e.tensor_copy(SAs[:], Sq), rd=[SqB], wr=[SAsB])
                            Uq, UqB = psq.next()
                            for h in range(2):
                                hc = slice(64 * h, 64 * h + 64)
                                mm(p, Uq[:, hc], [(BTOK[:, c, :], SAs[:, hc]), (KTOK[:, c, :], VTOK[:, c, hc])], rd=[BTB, KTB, SAsB, VTB], wr=[UqB])
                            tcol = c * 128 + (127 if d == 0 else 0)
                            for h in range(2):
                                hr = slice(64 * h, 64 * h + 64)
                                hc = slice(64 * h, 64 * h + 64)
                                p.ins("dve", lambda e, hr=hr, hc=hc, Uq=Uq, tcol=tcol, S=S, Sbn=Sbn: e.scalar_tensor_tensor(Sbn[hr, :], S[hr, :], A1[hr, tcol:tcol + 1], Uq[hr, hc], ALU.mult, ALU.add),
                                      rd=[SB, A1B, UqB], wr=[SbnB])
                            Oq, OqB = psq.next()
                            for h in range(2):
                                hr = slice(64 * h, 64 * h + 64)
                                hc = slice(64 * h, 64 * h + 64)
                                at, atB = ATs[c][h]
                                mm(p, Oq[:, hc], [(comb[hr, c, 1, :], Sb[hr, :]), (at[:, 0:128], SAs[:, hc]), (at[:, 256:384], VTOK[:, c, hc])],
                                   rd=[combB, SbB, atB, SAsB, VTB], wr=[OqB])
                            if d == 0:
                                p.ins("act", lambda e, Oq=Oq, c=c: e.activation(OSUM[:, c, :], Oq, AF.Copy), rd=[OqB], wr=[OSB])
                            else:
                                p.ins("dve", lambda e, Oq=Oq, c=c: e.tensor_tensor(OSUM[:, c, :], OSUM[:, c, :], Oq, ALU.add), rd=[OqB, OSB], wr=[OSB])
                            for h in range(2):
                                hr = slice(64 * h, 64 * h + 64)
                                hc = slice(64 * h, 64 * h + 64)
                                p.ins("dve", lambda e, hr=hr, hc=hc, Uq=Uq, tcol=tcol, S=S: e.scalar_tensor_tensor(S[hr, :], S[hr, :], A1[hr, tcol:tcol + 1], Uq[hr, hc], ALU.mult, ALU.add),
                                      rd=[SB, A1B, UqB], wr=[SB])
                    if not sample:
                        for (qi, sidx, order) in chains:
                            (S, SB), (Sb, SbB), (sin_, sinB), (Xs, XsB), (SAs, SAsB) = seqbufs[qi]
                            q, qB = psq.next()
                            mm(p, q[0:64, :], [(S[:], G.ident[:])], rd=[SB, G.identB], wr=[qB])
                            p.ins("dve", lambda e, q=q, sin_=sin_: e.tensor_copy(sin_[:], q[0:64, :]), rd=[qB], wr=[sinB])
                            p.dma("sp", new_st[sidx, L, d, 2 * hp:2 * hp + 2].rearrange("h v k -> v h k"), sin_[:].rearrange("v (h k) -> v h k", h=2), rd=[sinB])
                if RSTOP < 4:
                    continue
                p.dma("sp", lg[:], W["rwkv_lnx_g"][L][hp * 128:(hp + 1) * 128].partition_broadcast(128), wr=[lgB])
                p.dma("sp", lb[:], W["rwkv_lnx_b"][L][hp * 128:(hp + 1) * 128].partition_broadcast(128), wr=[lbB])
                VF = v3(A5, nch)
                VFB = A5B
                p.dma("sp", VF, projTok[tok0:tok0 + Ls, hp * 128:(hp + 1) * 128].rearrange("(c p) f -> p c f", p=128), wr=[VFB])
                n2 = nch * 2
                o3 = OSUM[:, 0:nch, :].rearrange("p c (h v) -> p (c h) v", h=2)
                p.ins("dve", lambda e: e.tensor_reduce(st4[0][:, 0:n2], o3, AX.X, ALU.add), rd=[OSB], wr=[st4[1]])
                p.ins("pool", lambda e: e.tensor_tensor(v3(A7, nch), OSUM[:, 0:nch, :], OSUM[:, 0:nch, :], ALU.mult), rd=[OSB], wr=[A7B])
                p.ins("dve", lambda e: e.tensor_reduce(st4[0][:, 32:32 + n2], A7[:, 0:Ls].rearrange("p (a v) -> p a v", v=64), AX.X, ALU.add), rd=[A7B], wr=[st4[1]])
                sm, sq = st4[0][:, 0:n2], st4[0][:, 32:32 + n2]
                p.ins("dve", lambda e: e.tensor_scalar(sm, sm, 1.0 / 64, None, ALU.mult), rd=[st4[1]], wr=[st4[1]])
                p.ins("dve", lambda e: e.tensor_scalar(sq, sq, 1.0 / 64, GN_EPS, ALU.mult, ALU.add), rd=[st4[1]], wr=[st4[1]])
                p.ins("dve", lambda e: e.tensor_tensor(bs_s[:, 0:n2], sm, sm, ALU.mult), rd=[st4[1]], wr=[bsB])
                p.ins("dve", lambda e: e.tensor_tensor(sq, sq, bs_s[:, 0:n2], ALU.subtract), rd=[st4[1], bsB], wr=[st4[1]])
                p.ins("act", lambda e: e.activation(sq, sq, AF.Sqrt), rd=[st4[1]], wr=[st4[1]])
                p.ins("dve", lambda e: e.reciprocal(sq, sq), rd=[st4[1]], wr=[st4[1]])
                p.ins("dve", lambda e: e.scalar_tensor_tensor(A8[:, 0:Ls], R[:, 0:Ls], pv[:, 42 + hp:43 + hp], KDS[:, 0:Ls], ALU.mult, ALU.mult), rd=[RB, pvB, KDSB], wr=[A8B])
                for c in range(nch):
                    q, qB = psq.next()
                    mm(p, q[:, 0:2], [(A8[:, c * 128:(c + 1) * 128], hind[:])], rd=[A8B, hiB], wr=[qB])
                    p.ins("act", lambda e, q=q, c=c: e.activation(bs_s[:, 2 * c:2 * c + 2], q[:, 0:2], AF.Copy), rd=[qB], wr=[bsB])
                G6 = v3(A6, nch)
                p.dma("sp", G6, projTok[tok0:tok0 + Ls, 768 + hp * 128:768 + (hp + 1) * 128].rearrange("(c p) f -> p c f", p=128), wr=[A6B])
                p.ins("act", lambda e: e.activation(A6[:, 0:Ls], A6[:, 0:Ls], AF.Sigmoid), rd=[A6B], wr=[A6B])
                mean_bc = st4[0][:, 0:n2].unsqueeze(2).to_broadcast([128, n2, 64])
                rstd_bc = st4[0][:, 32:32 + n2].unsqueeze(2).to_broadcast([128, n2, 64])
                bs_bc = bs_s[:, 0:n2].unsqueeze(2).to_broadcast([128, n2, 64])
                lg_bc = lg[:].unsqueeze(1).to_broadcast([128, nch, 128])
                lb_bc = lb[:].unsqueeze(1).to_broadcast([128, nch, 128])
                OS3 = OSUM[:, 0:nch, :]
                VF3h = VF.rearrange("p c (h v) -> p (c h) v", h=2)
                p.ins("dve", lambda e: e.tensor_tensor(o3, o3, mean_bc, ALU.subtract), rd=[OSB, st4[1]], wr=[OSB])
                p.ins("dve", lambda e: e.tensor_tensor(o3, o3, rstd_bc, ALU.mult), rd=[OSB, st4[1]], wr=[OSB])
                p.ins("pool", lambda e: e.tensor_tensor(OS3, OS3, lg_bc, ALU.mult), rd=[OSB, lgB], wr=[OSB])
                p.ins("pool", lambda e: e.tensor_tensor(OS3, OS3, lb_bc, ALU.add), rd=[OSB, lbB], wr=[OSB])
                p.ins("dve", lambda e: e.tensor_tensor(VF3h, VF3h, bs_bc, ALU.mult), rd=[VFB, bsB], wr=[VFB])
                p.ins("dve", lambda e: e.tensor_tensor(OS3, OS3, VF, ALU.add), rd=[OSB, VFB], wr=[OSB])
                p.ins("pool", lambda e: e.tensor_tensor(OS3, OS3, G6, ALU.mult), rd=[OSB, A6B], wr=[OSB])
                for c in range(nch):
                    q, qB = psq.next()
                    mm(p, q, [(OSUM[:, c, :], G.ident[:])], rd=[OSB, G.identB], wr=[qB])
                    p.ins("act", lambda e, q=q, c=c: e.activation(BH[:, c * 128:(c + 1) * 128], q, AF.Copy), rd=[qB], wr=[BHB])
                p.dma("sp", mixT[hp, :, tok0:tok0 + Ls], BH[:, 0:Ls], rd=[BHB])
    p.barrier()
```
